# Optimizing a Trainium2 kernel written in Bass

```python
import jax, jax.numpy as jnp
from jax import lax
import numpy as np

D_MODEL = 1024
BATCH = 4
SEQ = 4096
DEPTH = 2
DEC_BATCH = 128
DEC_SEQ = 8
PAST_LEN = 2048
PAGE_SIZE = 128

HEAD_DIM = 64
RW_DIM = D_MODEL // 4
RW_HEADS = RW_DIM // HEAD_DIM
W_LORA = 64
A_LORA = 64
G_LORA = 128
RW_PROJ = 3 * RW_DIM + W_LORA + A_LORA + G_LORA
CONV_DIM = D_MODEL // 4
CONV_GROUPS = CONV_DIM // HEAD_DIM
CONV_K = 3
SB_DIM = D_MODEL // 2
SB_HEADS = SB_DIM // HEAD_DIM
SB_BIAS_INIT = -8.0
MIX_DIM = RW_DIM + CONV_DIM + SB_DIM
IN_COLS = RW_PROJ + 3 * CONV_DIM + 3 * SB_DIM
D_FF = 4 * D_MODEL
PLE_DIM = 256
Q_BLOCK = 128
RMS_EPS = 1e-6
GN_EPS = 64e-5

kernel_name = 'hybrid_rwkv7_shortconv_stickbreak_step'


def rmsnorm(x, g):
    xf = x.astype(jnp.float32)
    y = xf * lax.rsqrt(jnp.mean(xf * xf, axis=-1, keepdims=True) + RMS_EPS)
    return (y * g).astype(x.dtype)


def wkv7_scan(r, decay, k, v, a_vec, b_vec, s0):
    def step(s, inp):
        r_t, w_t, k_t, v_t, a_t, b_t = inp
        sa = jnp.einsum('bhvk,bhk->bhv', s, a_t)
        s = (s * w_t[:, :, None, :] + sa[..., None] * b_t[:, :, None, :]
             + v_t[..., None] * k_t[:, :, None, :])
        y = jnp.einsum('bhvk,bhk->bhv', s, r_t)
        return s, y
    xs = tuple(jnp.moveaxis(t, 1, 0) for t in (r, decay, k, v, a_vec, b_vec))
    s_final, ys = lax.scan(step, s0, xs)
    return jnp.moveaxis(ys, 0, 1), s_final


def rwkv7_mixer(proj, prev_row, s0, lw):
    B, T, _ = proj.shape
    f32 = jnp.float32
    proj = proj.astype(f32)
    prev = jnp.concatenate([prev_row.astype(f32)[:, None], proj[:, :-1]], axis=1)
    xs = proj + (prev - proj) * lw['mu_shift']
    cuts = [RW_DIM, 2 * RW_DIM, 3 * RW_DIM, 3 * RW_DIM + W_LORA, 3 * RW_DIM + W_LORA + A_LORA]
    r, k, v, dw, da, dg = jnp.split(xs, cuts, axis=-1)
    w_log = -jax.nn.softplus(-(lw['w0'] + jnp.tanh(dw) @ lw['w2'])) - 0.5
    decay = jnp.exp(-jnp.exp(w_log))
    a = jax.nn.sigmoid(lw['a0'] + da @ lw['a2'])
    g = jax.nn.sigmoid(dg) @ lw['g2']
    hs = lambda t: t.reshape(B, T, RW_HEADS, HEAD_DIM)
    kk = hs(k * lw['k_k'])
    kk = kk * lax.rsqrt(jnp.sum(kk * kk, axis=-1, keepdims=True) + 1e-12)
    k = k * (1.0 + (a - 1.0) * lw['k_a'])
    rh, kh, vh, ah = hs(r), hs(k), hs(v), hs(a)
    y, s_fin = wkv7_scan(rh, hs(decay), kh, vh, -kk, kk * ah, s0.astype(f32))
    mu = jnp.mean(y, axis=-1, keepdims=True)
    var = jnp.mean(jnp.square(y - mu), axis=-1, keepdims=True)
    yn = ((y - mu) * lax.rsqrt(var + GN_EPS)).reshape(B, T, RW_DIM) * lw['gn_w'] + lw['gn_b']
    bonus = jnp.sum(rh * kh * lw['r_k'], axis=-1, keepdims=True) * vh
    out = (yn + bonus.reshape(B, T, RW_DIM)) * g
    return out, s_fin, proj[:, -1]


def short_conv_mixer(pb, pc, px, conv_prev, conv_w):
    T = px.shape[1]
    u = pc * px
    ext = jnp.concatenate([conv_prev.astype(u.dtype), u], axis=1)
    y = conv_w[0] * ext[:, 0:T]
    for j in range(1, CONV_K):
        y = y + conv_w[j] * ext[:, j:j + T]
    return pb * y, ext[:, -(CONV_K - 1):]


def stick_breaking(q, k, v, past_len, bias):
    B, T, H, d = q.shape
    S = k.shape[1]
    qb = Q_BLOCK if T % Q_BLOCK == 0 else T
    nb = T // qb
    scale = d ** -0.5
    kf = k.astype(jnp.float32)
    vf = v.astype(jnp.float32)
    bias_f = bias.astype(jnp.float32)[None, :, None, None]
    k_pos = jnp.arange(S)
    q_blocks = jnp.moveaxis(q.reshape(B, nb, qb, H, d), 1, 0)
    q_pos_blocks = (past_len + jnp.arange(T)).reshape(nb, qb)

    def block(args):
        q_blk, q_pos = args
        z = jnp.einsum('bqhd,bkhd->bhqk', q_blk.astype(jnp.float32), kf) * scale + bias_f
        mask = k_pos[None, :] < q_pos[:, None]
        log_beta = jnp.where(mask, jax.nn.log_sigmoid(z), -jnp.inf)
        log_keep = jnp.where(mask, jax.nn.log_sigmoid(-z), 0.0)
        later = lax.cumsum(log_keep, axis=3, reverse=True) - log_keep
        wts = jnp.exp(log_beta + later)
        return jnp.einsum('bhqk,bkhd->bqhd', wts, vf)

    out = lax.map(block, (q_blocks, q_pos_blocks))
    return jnp.moveaxis(out, 0, 1).reshape(B, T, H, d).astype(q.dtype)


def decoder_layer(x, p, past_len, shift_prev, wkv_prev, conv_prev, k_past, v_past, lw):
    B, T, _ = x.shape
    h = rmsnorm(x, lw['g_mix'])
    proj = h @ lw['w_in']
    p_rw, p_cv, p_sb = jnp.split(proj, [RW_PROJ, RW_PROJ + 3 * CONV_DIM], axis=-1)
    o_rw, wkv_new, shift_new = rwkv7_mixer(p_rw, shift_prev, wkv_prev, lw)
    cb, cc, cx = jnp.split(p_cv, 3, axis=-1)
    o_cv, conv_new = short_conv_mixer(cb, cc, cx, conv_prev, lw['conv_w'])
    q, k, v = [t.reshape(B, T, SB_HEADS, HEAD_DIM) for t in jnp.split(p_sb, 3, axis=-1)]
    q = rmsnorm(q, lw['q_gain'])
    k = rmsnorm(k, lw['k_gain'])
    if k_past is None:
        k_all, v_all = k, v
    else:
        k_all = jnp.concatenate([k_past.astype(k.dtype), k], axis=1)
        v_all = jnp.concatenate([v_past.astype(v.dtype), v], axis=1)
    o_sb = stick_breaking(q, k_all, v_all, past_len, lw['sb_bias']).reshape(B, T, SB_DIM)
    mixed = jnp.concatenate([o_rw.astype(x.dtype), o_cv.astype(x.dtype), o_sb.astype(x.dtype)], axis=-1)
    x = x + mixed @ lw['w_out']
    h2 = rmsnorm(x, lw['g_mlp'])
    x = x + jnp.square(jax.nn.relu(h2 @ lw['w_up'])) @ lw['w_down']
    gate = jax.nn.sigmoid(rmsnorm(x, lw['g_ple']) @ lw['w_ple_gate'])
    x = x + gate * (p @ lw['w_ple_proj'])
    return (x, k, v, wkv_new.astype(wkv_prev.dtype), shift_new.astype(shift_prev.dtype),
            conv_new.astype(conv_prev.dtype))


def gather_past(cache_l, page_table):
    pages = cache_l[page_table]
    b, n, ps, hh, d = pages.shape
    return pages.reshape(b, n * ps, hh, d)


def run_trunk(x, p, past_len, shift, wkv, conv, cache_k, cache_v, page_table, params):
    ks, vs, wkvs, shifts, convs = [], [], [], [], []
    for l in range(DEPTH):
        lw = {name: arr[l] for name, arr in params.items()}
        if cache_k is None:
            k_past, v_past = None, None
        else:
            k_past = gather_past(cache_k[l], page_table)
            v_past = gather_past(cache_v[l], page_table)
        x, k_new, v_new, wkv_new, shift_new, conv_new = decoder_layer(
            x, p[l], past_len, shift[l], wkv[l], conv[l], k_past, v_past, lw)
        ks.append(k_new)
        vs.append(v_new)
        wkvs.append(wkv_new)
        shifts.append(shift_new)
        convs.append(conv_new)
    return (x, jnp.stack(ks), jnp.stack(vs), jnp.stack(wkvs), jnp.stack(shifts), jnp.stack(convs))


def setup_inputs(seed: int = 0) -> dict:
    key = jax.random.key(seed)
    keys = iter(jax.random.split(key, 48))
    f32 = jnp.float32

    def nrm(shape, scale=1.0):
        return jax.random.normal(next(keys), shape, f32) * scale

    n_pages = PAST_LEN // PAGE_SIZE
    n_phys = (DEC_BATCH * n_pages * 5) // 4
    page_table = jax.random.permutation(next(keys), n_phys)[:DEC_BATCH * n_pages]
    page_table = page_table.reshape(DEC_BATCH, n_pages).astype(jnp.int32)
    return {
        'x_prompt': nrm((BATCH, SEQ, D_MODEL)),
        'x_sample': nrm((DEC_BATCH, DEC_SEQ, D_MODEL)),
        'cache_k': nrm((DEPTH, n_phys, PAGE_SIZE, SB_HEADS, HEAD_DIM)),
        'cache_v': nrm((DEPTH, n_phys, PAGE_SIZE, SB_HEADS, HEAD_DIM)),
        'state_wkv': nrm((DEPTH, DEC_BATCH, RW_HEADS, HEAD_DIM, HEAD_DIM), 0.5),
        'state_shift': nrm((DEPTH, DEC_BATCH, RW_PROJ)),
        'state_conv': nrm((DEPTH, DEC_BATCH, CONV_K - 1, CONV_DIM)),
        'page_table': page_table,
        'p_prompt': nrm((DEPTH, BATCH, SEQ, PLE_DIM)),
        'p_sample': nrm((DEPTH, DEC_BATCH, DEC_SEQ, PLE_DIM)),
        'g_mix': 1.0 + nrm((DEPTH, D_MODEL), 0.1),
        'w_in': nrm((DEPTH, D_MODEL, IN_COLS), D_MODEL ** -0.5),
        'mu_shift': jax.random.uniform(next(keys), (DEPTH, RW_PROJ), f32, 0.0, 1.0),
        'w0': nrm((DEPTH, RW_DIM), 0.5) - 1.0,
        'w2': nrm((DEPTH, W_LORA, RW_DIM), 0.5 * W_LORA ** -0.5),
        'a0': nrm((DEPTH, RW_DIM), 0.5),
        'a2': nrm((DEPTH, A_LORA, RW_DIM), A_LORA ** -0.5),
        'g2': nrm((DEPTH, G_LORA, RW_DIM), G_LORA ** -0.5),
        'k_k': 1.0 + nrm((DEPTH, RW_DIM), 0.1),
        'k_a': 1.0 + nrm((DEPTH, RW_DIM), 0.1),
        'r_k': nrm((DEPTH, RW_HEADS, HEAD_DIM), 0.1),
        'gn_w': 1.0 + nrm((DEPTH, RW_DIM), 0.1),
        'gn_b': nrm((DEPTH, RW_DIM), 0.01),
        'conv_w': nrm((DEPTH, CONV_K, CONV_DIM), CONV_K ** -0.5),
        'q_gain': 1.0 + nrm((DEPTH, HEAD_DIM), 0.1),
        'k_gain': 1.0 + nrm((DEPTH, HEAD_DIM), 0.1),
        'sb_bias': SB_BIAS_INIT + nrm((DEPTH, SB_HEADS), 0.1),
        'w_out': nrm((DEPTH, MIX_DIM, D_MODEL), MIX_DIM ** -0.5),
        'g_mlp': 1.0 + nrm((DEPTH, D_MODEL), 0.1),
        'w_up': nrm((DEPTH, D_MODEL, D_FF), D_MODEL ** -0.5),
        'w_down': nrm((DEPTH, D_FF, D_MODEL), D_FF ** -0.5),
        'g_ple': 1.0 + nrm((DEPTH, D_MODEL), 0.1),
        'w_ple_gate': nrm((DEPTH, D_MODEL, D_MODEL), D_MODEL ** -0.5),
        'w_ple_proj': nrm((DEPTH, PLE_DIM, D_MODEL), PLE_DIM ** -0.5),
    }


def reference(x_prompt, x_sample, cache_k, cache_v, state_wkv, state_shift, state_conv, page_table,
              p_prompt, p_sample, g_mix, w_in, mu_shift, w0, w2, a0, a2, g2, k_k, k_a, r_k,
              gn_w, gn_b, conv_w, q_gain, k_gain, sb_bias, w_out, g_mlp, w_up, w_down, g_ple,
              w_ple_gate, w_ple_proj):
    params = dict(g_mix=g_mix, w_in=w_in, mu_shift=mu_shift, w0=w0, w2=w2, a0=a0, a2=a2, g2=g2,
                  k_k=k_k, k_a=k_a, r_k=r_k, gn_w=gn_w, gn_b=gn_b, conv_w=conv_w,
                  q_gain=q_gain, k_gain=k_gain, sb_bias=sb_bias, w_out=w_out, g_mlp=g_mlp,
                  w_up=w_up, w_down=w_down, g_ple=g_ple, w_ple_gate=w_ple_gate,
                  w_ple_proj=w_ple_proj)
    dt = x_prompt.dtype
    shift0 = jnp.zeros((DEPTH, BATCH, RW_PROJ), dt)
    wkv0 = jnp.zeros((DEPTH, BATCH, RW_HEADS, HEAD_DIM, HEAD_DIM), dt)
    conv0 = jnp.zeros((DEPTH, BATCH, CONV_K - 1, CONV_DIM), dt)
    (y_prompt, k_prompt, v_prompt, wkv_prompt, shift_prompt, conv_prompt) = run_trunk(
        x_prompt, p_prompt, 0, shift0, wkv0, conv0, None, None, None, params)
    (y_sample, k_sample, v_sample, wkv_sample, shift_sample, conv_sample) = run_trunk(
        x_sample, p_sample, PAST_LEN, state_shift, state_wkv, state_conv,
        cache_k, cache_v, page_table, params)
    return (y_prompt, y_sample, k_prompt, v_prompt, wkv_prompt, shift_prompt, conv_prompt,
            k_sample, v_sample, wkv_sample, shift_sample, conv_sample)
```

```python
import numpy as np
import concourse.bass as bass
import concourse.mybir as mybir
from concourse.bass_utils import run_bass_kernel_spmd

F32 = mybir.dt.float32
BF16 = mybir.dt.bfloat16
I32 = mybir.dt.int32
AF = mybir.ActivationFunctionType
ALU = mybir.AluOpType
AX = mybir.AxisListType

SAME_ENGINE_SYNC = True
USE_CACHE = True
RWS = 0
RW4 = 0
NCORES = 8
D = 1024
SEQ = 4096
NSS = 16
DT = 8
NPAGES = 16
NPHYS = 2560
INC = 3328
RMS_EPS = 1e-6
GN_EPS = 64e-5


class Buf:
    __slots__ = ("name", "w", "r")

    def __init__(self, name):
        self.name = name
        self.w = None
        self.r = {}


class _Rec:
    def __getattr__(self, name):
        def f(*a, **k):
            self.__dict__["call"] = (name, a, k)
            return self
        return f


def _bind(fn):
    r = _Rec()
    fn(r)
    name, a, k = r.__dict__["call"]
    return lambda e: getattr(e, name)(*a, **k)


class KB:
    ENGS = ("pe", "act", "dve", "pool", "sp")

    def __init__(self, nc, n_dma_sems=40):
        self.nc = nc
        self.q = {e: [] for e in self.ENGS}
        self.cnt = {e: 0 for e in self.ENGS}
        self.known = {e: {} for e in self.ENGS}
        self.pend = {}
        self.sems = {}
        self._ctx = []
        for e in ("pe", "act", "dve", "pool"):
            self._sem("s_" + e)
        self.dpool = {}
        for qn, n in (("sp", n_dma_sems), ("pool", n_dma_sems), ("act", 8)):
            self.dpool[qn] = {"sems": [self._sem(f"d_{qn}{i}") for i in range(n)],
                              "tot": [0] * n, "next": 0}

    def _sem(self, name):
        cm = self.nc.semaphore(name)
        s = cm.__enter__()
        self._ctx.append(cm)
        self.sems[name] = s
        return name

    def _deps(self, eng, reads, writes):
        deps = {}

        def add(ev, ev_eng):
            k, v = ev[0], ev[1]
            if ev_eng == eng and not k.startswith("d_"):
                if eng == "pe" or not SAME_ENGINE_SYNC:
                    return
            if deps.get(k, 0) < v:
                deps[k] = v

        for b in reads:
            if b.w is not None:
                add(b.w, b.w[2])
        for b in writes:
            if b.w is not None:
                add(b.w, b.w[2])
            for re_, ev in b.r.items():
                add(ev, re_.split("#")[0])
        out = []
        kn = self.known[eng]
        for k, v in deps.items():
            if kn.get(k, 0) < v:
                kn[k] = v
                out.append((k, v))
        return out

    def op(self, eng, fn, reads=(), writes=(), inc=True):
        inc = True
        waits = self._deps(eng, reads, writes)
        fn = _bind(fn)
        self.q[eng].append((waits, fn, inc))
        pr, pw = self.pend.setdefault(eng, ([], []))
        if inc:
            self.cnt[eng] += 1
            ev = ("s_" + eng, self.cnt[eng], eng)
            for b in list(reads) + pr:
                b.r[eng] = (ev[0], ev[1])
            for b in list(writes) + pw:
                b.w = ev
                b.r = {}
            del pr[:]
            del pw[:]
        else:
            pr.extend(reads)
            pw.extend(writes)

    def dma(self, qn, fn, reads=(), writes=()):
        pool = self.dpool[qn]
        i = pool["next"]
        pool["next"] = (i + 1) % len(pool["sems"])
        key = pool["sems"][i]
        waits = self._deps(qn, reads, writes)
        prev = pool["tot"][i]
        kn = self.known[qn]
        if prev > 0 and kn.get(key, 0) < prev:
            kn[key] = prev
            waits.append((key, prev))
        val = prev + 16
        pool["tot"][i] = val
        fn = _bind(fn)
        self.q[qn].append((waits, fn, ("dma", key)))
        ev = (key, val, "dma")
        for b in reads:
            b.r["dma#%s%d" % (qn, i)] = (key, val)
        for b in writes:
            b.w = ev
            b.r = {}

    def barrier(self):
        for e in self.ENGS:
            waits = []
            kn = self.known[e]
            for e2 in ("pe", "act", "dve", "pool"):
                if e2 == e:
                    continue
                v = self.cnt[e2]
                k = "s_" + e2
                if v > 0 and kn.get(k, 0) < v:
                    kn[k] = v
                    waits.append((k, v))
            for qn, pool in self.dpool.items():
                for i, k in enumerate(pool["sems"]):
                    v = pool["tot"][i]
                    if v > 0 and kn.get(k, 0) < v:
                        kn[k] = v
                        waits.append((k, v))
            if waits:
                self.q[e].append((waits, None, False))

    def finish(self):
        self.barrier()
        nc = self.nc
        sems = self.sems
        q = self.q

        def emit(e, eobj):
            for waits, fn, inc in q[e]:
                for k, v in waits:
                    eobj.wait_ge(sems[k], v)
                if fn is None:
                    continue
                ins = fn(eobj)
                if inc is True:
                    ins.then_inc(sems["s_" + e], 1)
                elif inc:
                    ins.then_inc(sems[inc[1]], 16)

        with nc.Block() as block:
            @block.tensor
            def _(t):
                emit("pe", t)

            @block.scalar
            def _(t):
                emit("act", t)

            @block.vector
            def _(t):
                emit("dve", t)

            @block.gpsimd
            def _(t):
                emit("pool", t)

            @block.sync
            def _(t):
                emit("sp", t)
        for cm in reversed(self._ctx):
            cm.__exit__(None, None, None)
        self._ctx = []


class Scope:
    def __init__(self, nc):
        self.nc = nc
        self.stack = []

    UID = [0]

    def sb(self, name, shape, dt):
        Scope.UID[0] += 1
        name = "%s_%d" % (name, Scope.UID[0])
        cm = self.nc.sbuf_tensor(name, list(shape), dt)
        t = cm.__enter__()
        self.stack.append(cm)
        return t, Buf(name)

    def ps(self, name, shape, dt=F32):
        Scope.UID[0] += 1
        name = "%s_%d" % (name, Scope.UID[0])
        cm = self.nc.psum_tensor(name, list(shape), dt)
        t = cm.__enter__()
        self.stack.append(cm)
        return t, Buf(name)

    def close(self):
        for cm in reversed(self.stack):
            cm.__exit__(None, None, None)
        self.stack = []


def make_consts():
    p = np.arange(128)[:, None]
    f = np.arange(128)[None, :]
    c = {}
    c["ident"] = (p == f)
    c["ntri"] = -(p >= f).astype(np.float32)
    c["nones"] = -np.ones((128, 128), np.float32)
    c["mlt"] = (p < f)
    c["mle"] = (p <= f)
    c["mgt"] = (f < p)
    c["blk64"] = ((p // 64) == (f // 64))
    c["identrep"] = ((p % 64) == np.arange(64)[None, :])
    sp_, tp_ = p // 8, p % 8
    cols = np.arange(NSS * 64)[None, :]
    cs, ct = cols // 64, cols % 8
    c["smask"] = ((sp_ == cs) & (tp_ < ct))
    c["iota"] = p.astype(np.float32)
    c["zrow"] = np.zeros((128, 512), np.float32)
    c["sm2"] = ((p // 8) == (f // 8)) & ((p % 8) < (f % 8))
    c["m12"] = np.concatenate([(p < f)[:, 0:64], (p <= f)[:, 0:64]], axis=1)
    offs = {}
    o = 0
    arrs = []
    for k, v in c.items():
        v = np.asarray(v, np.float32)
        offs[k] = (o, v.shape[1])
        o += v.shape[1]
        arrs.append(v)
    return np.ascontiguousarray(np.concatenate(arrs, axis=1)), offs


CONSTS, COFF = make_consts()
NCONST = CONSTS.shape[1]


def build_program(nlayers=2, do_phase2=True, ngroups=17, nphys=NPHYS):
    nc = bass.Bass("TRN2", target_bir_lowering=False)

    def din(name, shape, dt=F32):
        return nc.dram_tensor(name, list(shape), dt, kind="ExternalInput").ap()

    def dout(name, shape):
        return nc.dram_tensor(name, list(shape), F32, kind="ExternalOutput").ap()

    xp = din("xp", [SEQ, D]); xs = din("xs", [128, D])
    if USE_CACHE:
        ck = din("cache_k", [2 * nphys * 128, 512]); cv = din("cache_v", [2 * nphys * 128, 512])
    s_wkv = din("s_wkv", [2, NSS * 4 * 64, 64]); s_shift = din("s_shift", [2, NSS, 1024])
    s_conv = din("s_conv", [2, NSS * 2, 256]); ptab = din("ptab", [1, NSS * NPAGES], I32)
    pp = din("pp", [2, SEQ, 256]); psm = din("ps", [2, 128, 256])
    consts = din("consts", [128, NCONST])
    W = {}
    for name, shape in (("g_mix", [2, D]), ("w_in", [2, D, INC]), ("mu_shift", [2, D]), ("w0", [2, 256]),
                        ("w2", [2, 64, 256]), ("a0", [2, 256]), ("a2", [2, 64, 256]), ("g2", [2, 128, 256]),
                        ("k_k", [2, 256]), ("k_a", [2, 256]), ("r_k", [2, 256]), ("gn_w", [2, 256]),
                        ("gn_b", [2, 256]), ("conv_w", [2, 3, 256]), ("q_gain", [2, 64]), ("k_gain", [2, 64]),
                        ("sb_bias", [2, 8]), ("w_out", [2, D, D]), ("g_mlp", [2, D]), ("w_up", [2, D, 4 * D]),
                        ("w_down", [2, 4 * D, D]), ("g_ple", [2, D]), ("w_ple_gate", [2, D, D]),
                        ("w_ple_proj", [2, 256, D])):
        W[name] = din(name, shape)
    y_p = dout("y_p", [SEQ, D]); y_s = dout("y_s", [128, D])
    k_p = dout("k_p", [2, SEQ, 512]); v_p = dout("v_p", [2, SEQ, 512])
    wkv_p = dout("wkv_p", [2, 4 * 64, 64]); shift_p = dout("shift_p", [2, D]); conv_p = dout("conv_p", [2, 2, 256])
    k_s = dout("k_s", [2, 128, 512]); v_s = dout("v_s", [2, 128, 512])
    wkv_s = dout("wkv_s", [2, NSS * 4 * 64, 64]); shift_s = dout("shift_s", [2, NSS, D])
    conv_s = dout("conv_s", [2, NSS * 2, 256])
    NTOK = SEQ + 128
    x1s = nc.dram_tensor("x1s", [8, 128, NTOK], F32, kind="Internal").ap()
    xls = nc.dram_tensor("xls", [8, 128, NTOK], F32, kind="Internal").ap()

    kbd = KB(nc)
    op = kbd.op
    dma = kbd.dma
    glob = Scope(nc)

    cf, bcf = glob.sb("cf", [128, 129], F32)
    cb, bcb = glob.sb("cb", [128, NCONST], BF16)
    dma("sp", lambda e: e.dma_start(out=cf[:, 0:128], in_=consts[:, COFF["ident"][0]:COFF["ident"][0] + 128]), writes=[bcf])
    dma("sp", lambda e: e.dma_start(out=cf[:, 128:129], in_=consts[:, COFF["iota"][0]:COFF["iota"][0] + 1], allow_slow_non_contiguous=True), writes=[bcf])
    dma("pool", lambda e: e.dma_start(out=cb[:], in_=consts[:, :]), writes=[bcb])

    def C(name, bf=True, rows=slice(0, 128), cols=None):
        o, n = COFF[name]
        if not bf:
            assert name == "ident"
            return cf[rows, 0:128]
        if cols is None:
            return cb[rows, o:o + n]
        return cb[rows, o + cols.start:o + cols.stop]

    KC = [bcf, bcb]

    PS = [glob.ps("ps%d" % i, [128, 512], F32) for i in range(8)]
    psi = [0]

    def nps(banks=(0, 1, 2, 3, 4)):
        i = banks[psi[0] % len(banks)]
        psi[0] += 1
        return PS[i]

    def mm(out, lhsT, rhs, start, stop, reads, wbuf, inc):
        op("pe", lambda e: e.matmul(out, lhsT=lhsT, rhs=rhs, start=start, stop=stop, skip_group_check=True),
           reads=reads, writes=[wbuf], inc=inc)

    def rsqrt_inplace(t_ap, bt, eng_r="dve"):
        op("act", lambda e: e.activation(out=t_ap, in_=t_ap, func=AF.Sqrt), reads=[bt], writes=[bt])
        op(eng_r, lambda e: e.reciprocal(out=t_ap, in_=t_ap), reads=[bt], writes=[bt])

    NG = 16
    groups = [(g * 256, 256, 1, 256) for g in range(NG)] + [(SEQ, 128, NSS, DT)]
    if ngroups < 0:
        groups = groups[-1:]
    elif ngroups < 17:
        groups = groups[:ngroups]
    ktscr = nc.dram_tensor("ktscr", [32, 128, 512], BF16, kind="Internal").ap()
    vscr = nc.dram_tensor("vscr", [32, 128, 512], BF16, kind="Internal").ap()
    bkts = [Buf("kts%d" % i) for i in range(32)]
    bvss = [Buf("vss%d" % i) for i in range(32)]
    bx1s = [Buf("x1s%d" % i) for i in range(17)]
    bxls = [Buf("xls%d" % i) for i in range(17)]
    identf = C("ident", bf=False)
    identb = C("ident")

    def rmsnorm(NT, src_t, bsrc, gcol, bg, dst_t, bdst, sq, bsq, rstd, brstd):
        pr, bpr = nps()
        for c in range(8):
            op("act", lambda e, c=c: e.activation(out=sq[c % 2][:, 0:NT], in_=src_t[:, c, 0:NT], func=AF.Square),
               reads=[bsrc], writes=[bsq[c % 2]])
            mm(pr[:, 0:NT], C("nones"), sq[c % 2][:, 0:NT], c == 0, c == 7, [bsq[c % 2]] + KC, bpr, c == 7)
        op("dve", lambda e: e.tensor_scalar(out=rstd[:, 0:NT], in0=pr[:, 0:NT], scalar1=-1.0 / D,
                                            scalar2=RMS_EPS, op0=ALU.mult, op1=ALU.add),
           reads=[bpr], writes=[brstd])
        rsqrt_inplace(rstd[:, 0:NT], brstd)
        for c in range(8):
            op("dve", lambda e, c=c: e.scalar_tensor_tensor(
                out=dst_t[:, c, 0:NT], in0=src_t[:, c, 0:NT], scalar=gcol(c), in1=rstd[:, 0:NT],
                op0=ALU.mult, op1=ALU.mult), reads=[bsrc, brstd, bg], writes=[bdst])

    def rwkv_mix(l, gi, t0, NT, nseq, T, sample):
        C_ = 8 if sample else 64
        nch = NT // C_
        EXPM05 = float(np.exp(-0.5))
        rX, kX, vX = xsh[:, 0:2, 0:NT], xsh[:, 2:4, 0:NT], xsh[:, 4:6, 0:NT]

        def F(t):
            return t[:, :, 0:NT]

        op("act", lambda e: e.activation(out=dwa[0:64, 0:NT], in_=xsh[0:64, 6, 0:NT], func=AF.Tanh), reads=[bxsh], writes=[bdwa])
        op("act", lambda e: e.copy(out=dwa[64:128, 0:NT], in_=xsh[64:128, 6, 0:NT]), reads=[bxsh], writes=[bdwa])
        op("act", lambda e: e.activation(out=sdg[:, 0:NT], in_=xsh[:, 7, 0:NT], func=AF.Sigmoid), reads=[bxsh], writes=[bsdg])
        for j in range(2):
            pw_, bpw_ = nps()
            mm(pw_[:, 0:NT], w2t[0:64, j * 128:(j + 1) * 128], dwa[0:64, 0:NT], True, True, [bw2, bdwa], bpw_, True)
            op("act", lambda e: e.activation(out=F1[:, j, 0:NT], in_=pw_[:, 0:NT], func=AF.Sigmoid, bias=pc("w0", j), scale=1.0),
               reads=[bpw_, bpcol], writes=[bF1])
            pa_, bpa_ = nps()
            mm(pa_[:, 0:NT], w2t[64:128, j * 128:(j + 1) * 128], dwa[64:128, 0:NT], True, True, [bw2, bdwa], bpa_, True)
            op("act", lambda e: e.activation(out=F2[:, j, 0:NT], in_=pa_[:, 0:NT], func=AF.Sigmoid, bias=pc("a0", j), scale=1.0),
               reads=[bpa_, bpcol], writes=[bF2])
        op("dve", lambda e: e.tensor_scalar(out=F(F1), in0=F(F1), scalar1=-EXPM05, scalar2=None, op0=ALU.mult),
           reads=[bF1], writes=[bF1])
        for j in range(2):
            op("dve", lambda e: e.tensor_scalar(out=F4[:, j, 0:NT], in0=kX[:, j, :], scalar1=pc("k_k", j), scalar2=None,
                                                op0=ALU.mult), reads=[bxsh, bpcol], writes=[bF4])
            op("pool", lambda e: e.tensor_tensor(out=sq[0][:, 0:NT], in0=F4[:, j, 0:NT], in1=F4[:, j, 0:NT], op=ALU.mult),
               reads=[bF4], writes=[bsq[0]])
            pk_, bpk_ = nps()
            mm(pk_[:, 0:NT], C("blk64"), sq[0][:, 0:NT], True, True, [bsq[0]] + KC, bpk_, True)
            op("dve", lambda e: e.tensor_scalar(out=rstd[:, 0:NT], in0=pk_[:, 0:NT], scalar1=1e-12, scalar2=None, op0=ALU.add),
               reads=[bpk_], writes=[brstd])
            rsqrt_inplace(rstd[:, 0:NT], brstd)
            op("dve", lambda e: e.tensor_tensor(out=F4[:, j, 0:NT], in0=F4[:, j, 0:NT], in1=rstd[:, 0:NT], op=ALU.mult),
               reads=[bF4, brstd], writes=[bF4])
            op("dve", lambda e: e.tensor_scalar(out=F5[:, j, 0:NT], in0=F2[:, j, 0:NT], scalar1=pc("k_a", j), scalar2=pc("omka", j),
                                                op0=ALU.mult, op1=ALU.add), reads=[bF2, bpcol], writes=[bF5])
            op("dve", lambda e: e.tensor_tensor(out=F5[:, j, 0:NT], in0=F5[:, j, 0:NT], in1=kX[:, j, :], op=ALU.mult),
               reads=[bF5, bxsh], writes=[bF5])
        src_t, bsrc_ = F1, bF1
        pp_ = [(F6, bF6), (F7, bF7)]
        k_ = 0
        s_ = 1
        while s_ < C_:
            dst_t, bdst_ = pp_[k_ % 2]
            sv = src_t[:, :, 0:NT].rearrange("p c (n t) -> p c n t", t=C_)
            dv = dst_t[:, :, 0:NT].rearrange("p c (n t) -> p c n t", t=C_)
            op("pool", lambda e: e.tensor_copy(out=dst_t[:, :, 0:NT], in_=src_t[:, :, 0:NT]), reads=[bsrc_], writes=[bdst_])
            for j in range(2):
                op("pool", lambda e: e.tensor_tensor(out=dv[:, j, :, s_:C_], in0=sv[:, j, :, s_:C_], in1=sv[:, j, :, 0:C_ - s_],
                                                     op=ALU.add), reads=[bsrc_, bdst_], writes=[bdst_])
            src_t, bsrc_ = dst_t, bdst_
            k_ += 1
            s_ *= 2
        cl, bcl = src_t, bsrc_
        oth, both = pp_[k_ % 2]
        clv = cl[:, :, 0:NT].rearrange("p c (n t) -> p c n t", t=C_)
        ARv = AR[:, :, 0:2 * NT].rearrange("p c (n w t) -> p c n w t", w=2, t=C_)

        def v4(t):
            return t[:, :, 0:NT].rearrange("p c (n t) -> p c n t", t=C_)
        op("act", lambda e: e.activation(out=F(F8), in_=cl[:, :, 0:NT], func=AF.Exp), reads=[bcl], writes=[bF8])
        for j in range(2):
            op("dve", lambda e: e.tensor_tensor(out=ARv[:, j, :, 1, :], in0=v4(xsh[:, 0:2])[:, j], in1=v4(F8)[:, j], op=ALU.mult),
               reads=[bxsh, bF8], writes=[bAR])
        for j in range(2):
            op("dve", lambda e: e.tensor_tensor(
                out=dP[:, j, 0:nch, :], in0=C("identrep").unsqueeze(1).to_broadcast([128, nch, 64]),
                in1=v4(F8)[:, j, :, C_ - 1:C_].to_broadcast([128, nch, 64]), op=ALU.mult), reads=[bF8] + KC, writes=[bdP])
        op("dve", lambda e: e.tensor_tensor(out=F(oth), in0=F(F4), in1=F(F2), op=ALU.mult), reads=[bF4, bF2], writes=[both])
        op("act", lambda e: e.activation(out=F(F8), in_=cl[:, :, 0:NT], func=AF.Exp, scale=-1.0), reads=[bcl], writes=[bF8])
        op("dve", lambda e: e.tensor_tensor(out=F(btT), in0=F(oth), in1=F(F8), op=ALU.mult), reads=[both, bF8], writes=[bbtT])
        op("dve", lambda e: e.tensor_tensor(out=F(ktT), in0=F(F5), in1=F(F8), op=ALU.mult), reads=[bF5, bF8], writes=[bktT])
        for j in range(2):
            op("pool", lambda e: e.tensor_tensor(out=v4(F8)[:, j], in0=clv[:, j, :, C_ - 1:C_].to_broadcast([128, nch, C_]),
                                                 in1=clv[:, j], op=ALU.subtract), reads=[bcl], writes=[bF8])
        op("act", lambda e: e.activation(out=F(F8), in_=F(F8), func=AF.Exp), reads=[bF8], writes=[bF8])
        op("dve", lambda e: e.tensor_tensor(out=F(bhX), in0=F(oth), in1=F(F8), op=ALU.mult), reads=[both, bF8], writes=[bbhX])
        op("dve", lambda e: e.tensor_tensor(out=F(khT), in0=F(F5), in1=F(F8), op=ALU.mult), reads=[bF5, bF8], writes=[bkhT])
        op("pool", lambda e: e.tensor_tensor(out=F(F8), in0=cl[:, :, 0:NT], in1=F(F1), op=ALU.subtract), reads=[bcl, bF1], writes=[bF8])
        op("act", lambda e: e.activation(out=F(F8), in_=F(F8), func=AF.Exp), reads=[bF8], writes=[bF8])
        for j in range(2):
            op("dve", lambda e: e.scalar_tensor_tensor(out=ARv[:, j, :, 0, :], in0=v4(F4)[:, j], scalar=-1.0, in1=v4(F8)[:, j],
                                                       op0=ALU.mult, op1=ALU.mult), reads=[bF4, bF8], writes=[bAR])
        op("pool", lambda e: e.tensor_copy(out=F(vbT), in_=vX), reads=[bxsh], writes=[bvbT])
        if RWS == 1:
            op("pool", lambda e: e.memset(mixT[:, 0:2, 0:NT], 0.0), writes=[bmixT])
            return
        if not sample and gi == 0:
            op("dve", lambda e: e.memset(STa[0][:], 0.0), writes=[bSTa[0]])
        sti = rw_state["i"]

        m12 = C("m12")
        for c in range(nch):
            cs_ = slice(c * C_, (c + 1) * C_)
            if sample:
                dma("sp", lambda e: e.dma_start(out=s0t[:, 0:64], in_=s_wkv[l, c * 256:c * 256 + 128, :]), writes=[bs0t])
                dma("sp", lambda e: e.dma_start(out=s0t[:, 64:128], in_=s_wkv[l, c * 256 + 128:c * 256 + 256, :]), writes=[bs0t])
                op("pool", lambda e: e.tensor_copy(out=s0b[:, :], in_=s0t[:, :]), reads=[bs0t], writes=[bs0b])
                for hh in range(2):
                    ps0, bps0 = nps()
                    ps0v = ps0[:, 0:64].bitcast(BF16)
                    hb = hh * 64
                    for hp in range(2):
                        op("pe", lambda e: e.transpose(out=ps0v[hb:hb + 64, hp * 64:(hp + 1) * 64],
                                                       in_=s0b[hb:hb + 64, hp * 64:(hp + 1) * 64],
                                                       identity=identb[hb:hb + 64, hb:hb + 64]),
                           reads=[bs0b] + KC, writes=[bps0])
                    op("act", lambda e: e.copy(out=STa[sti % 2][hb:hb + 64, :, :],
                                               in_=ps0v[hb:hb + 64, :].rearrange("p (a v) -> p a v", a=2)),
                       reads=[bps0], writes=[bSTa[sti % 2]])
            ST_, bST_ = STa[sti % 2], bSTa[sti % 2]
            STn, bSTn = STa[(sti + 1) % 2], bSTa[(sti + 1) % 2]
            ptk, bptk = nps()
            ptkv = ptk[:, 0:512].bitcast(BF16)
            srcs = [(lambda hp: ARv[:, hp, c, 0, :], bAR), (lambda hp: bhX[:, hp, cs_], bbhX),
                    (lambda hp: khT[:, hp, cs_], bkhT), (lambda hp: vbT[:, hp, cs_], bvbT)]
            for qi, (sf, bsf) in enumerate(srcs):
                for hp in range(2):
                    o_ = (qi * 2 + hp) * 128
                    op("pe", lambda e: e.transpose(out=ptkv[0:C_, o_:o_ + 128], in_=sf(hp), identity=identb),
                       reads=[bsf] + KC, writes=[bptk], inc=(qi == 3 and hp == 1))
            op("act", lambda e: e.copy(out=tok[0:C_, :], in_=ptkv[0:C_, :]), reads=[bptk], writes=[btok])

            def TK(qi, h):
                o_ = (qi * 2 + h // 2) * 128 + (h % 2) * 64
                return tok[0:C_, o_:o_ + 64]
            if RWS == 2:
                continue
            mk12 = m12[0:C_, :].rearrange("p (w j) -> p w j", w=2)[:, :, 0:C_]
            for which in range(3):
                pE, bpE = nps(); pO, bpO = nps()
                for h in range(4):
                    hp, hb = h // 2, (h % 2) * 64
                    pp2, bpp2 = (pE, bpE) if h % 2 == 0 else (pO, bpO)
                    arr = ARv[hb:hb + 64, hp, c, :, :]
                    if which == 0:
                        mm(pp2[0:C_, hp * 2 * C_:(hp + 1) * 2 * C_], btT[hb:hb + 64, hp, cs_], arr, True, True, [bbtT, bAR], bpp2, True)
                    elif which == 1:
                        mm(pp2[0:C_, hp * 2 * C_:(hp + 1) * 2 * C_], ktT[hb:hb + 64, hp, cs_], arr, True, True, [bktT, bAR], bpp2, True)
                    else:
                        mm(pp2[0:C_, hp * C_:(hp + 1) * C_], ARv[hb:hb + 64, hp, c, 0, :], btT[hb:hb + 64, hp, cs_], True, True,
                           [bbtT, bAR], bpp2, True)
                for h in range(4):
                    hp = h // 2
                    pp2, bpp2 = (pE, bpE) if h % 2 == 0 else (pO, bpO)
                    if which < 2:
                        At, bAt = (A1, bA1) if which == 0 else (A2, bA2)
                        op("dve", lambda e: e.tensor_tensor(
                            out=At[0:C_, h, :, 0:C_], in0=pp2[0:C_, hp * 2 * C_:(hp + 1) * 2 * C_].rearrange("p (w j) -> p w j", w=2),
                            in1=mk12, op=ALU.mult), reads=[bpp2] + KC, writes=[bAt])
                    else:
                        op("dve", lambda e: e.tensor_tensor(
                            out=Nm[0][0:C_, h, 0:C_], in0=pp2[0:C_, hp * C_:(hp + 1) * C_], in1=C("mgt")[0:C_, 0:C_], op=ALU.mult),
                           reads=[bpp2] + KC, writes=[bNm[0]])
            if RWS == 3:
                continue
            identC = identb[0:C_, 0:C_]
            for h in range(4):
                op("pool", lambda e: e.tensor_tensor(out=TT[0][0:C_, h, 0:C_], in0=A1[0:C_, h, 0, 0:C_], in1=identC, op=ALU.add),
                   reads=[bA1] + KC, writes=[bTT[0]])
                op("pool", lambda e: e.tensor_copy(out=NmT[0][0:C_, h, 0:C_], in_=A1[0:C_, h, 0, 0:C_]), reads=[bA1], writes=[bNmT[0]])
            ti = 0; ni = 0
            m_ = 2
            while m_ < C_:
                if RW4 == 1:
                    break
                last = (2 * m_ >= C_)
                pn, bpn = nps(); pnt, bpnt = nps()
                for h in range(4):
                    mm(pn[0:C_, h * C_:(h + 1) * C_], NmT[ni][0:C_, h, 0:C_], Nm[ni][0:C_, h, 0:C_], True, True,
                       [bNm[ni], bNmT[ni]], bpn, h == 3)
                if not last:
                    for h in range(4):
                        mm(pnt[0:C_, h * C_:(h + 1) * C_], Nm[ni][0:C_, h, 0:C_], NmT[ni][0:C_, h, 0:C_], True, True,
                           [bNm[ni], bNmT[ni]], bpnt, h == 3)
                if RW4 == 2:
                    break
                n2 = 1 - ni
                pnv = pn[0:C_, 0:4 * C_].rearrange("p (h j) -> p h j", h=4)
                op("act", lambda e: e.copy(out=Nm[n2][0:C_, :, 0:C_], in_=pnv), reads=[bpn], writes=[bNm[n2]])
                op("dve", lambda e: e.tensor_tensor(out=Qm[0:C_, :, 0:C_], in0=Nm[n2][0:C_, :, 0:C_],
                                                    in1=identC.unsqueeze(1).to_broadcast([C_, 4, C_]), op=ALU.add),
                   reads=[bNm[n2]] + KC, writes=[bQm])
                if not last:
                    op("act", lambda e: e.copy(out=NmT[n2][0:C_, :, 0:C_],
                                               in_=pnt[0:C_, 0:4 * C_].rearrange("p (h j) -> p h j", h=4)),
                       reads=[bpnt], writes=[bNmT[n2]])
                if RW4 == 3:
                    break
                ptt, bptt = nps()
                for h in range(4):
                    mm(ptt[0:C_, h * C_:(h + 1) * C_], Qm[0:C_, h, 0:C_], TT[ti][0:C_, h, 0:C_], True, True, [bQm, bTT[ti]], bptt, h == 3)
                op("act", lambda e: e.copy(out=TT[1 - ti][0:C_, :, 0:C_],
                                           in_=ptt[0:C_, 0:4 * C_].rearrange("p (h j) -> p h j", h=4)),
                   reads=[bptt], writes=[bTT[1 - ti]])
                ti = 1 - ti
                ni = n2
                m_ *= 2
                if RW4 == 4:
                    break
            TTf, bTTf = TT[ti], bTT[ti]
            if RWS == 4:
                continue
            pw2, bpw2 = nps()
            for h in range(4):
                mm(pw2[0:C_, h * 64:(h + 1) * 64], A2[0:C_, h, 0, 0:C_], TK(3, h), True, True, [bA2, btok], bpw2, h == 3)
            op("act", lambda e: e.copy(out=XW[0:C_, :, 1, :], in_=pw2[0:C_, 0:256].rearrange("p (h v) -> p h v", h=4)),
               reads=[bpw2], writes=[bXW])
            op("pool", lambda e: e.tensor_copy(out=XW[0:C_, :, 0, :], in_=tok[0:C_, 0:256].rearrange("p (h k) -> p h k", h=4)),
               reads=[btok], writes=[bXW])
            pav, bpav = nps()
            for h in range(4):
                mm(pav[0:C_, h * 128:(h + 1) * 128], TTf[0:C_, h, 0:C_], XW[0:C_, h, :, :], True, True, [bTTf, bXW], bpav, h == 3)
            op("act", lambda e: e.copy(out=AV[0:C_, :, :], in_=pav[0:C_, 0:512].rearrange("p (h x) -> p h x", h=4)),
               reads=[bpav], writes=[bAV])
            if RWS == 5:
                continue
            pat, bpat = nps()
            for h in range(4):
                hp, hb = h // 2, (h % 2) * 64
                mm(pat[hb:hb + 64, hp * C_:(hp + 1) * C_], TK(0, h), TTf[0:C_, h, 0:C_], True, True, [btok, bTTf], bpat, h == 3)
            op("act", lambda e: e.copy(out=AhT[:, :, 0:C_], in_=pat[:, 0:2 * C_].rearrange("p (a t) -> p a t", a=2)),
               reads=[bpat], writes=[bAhT])
            pm_, bpm_ = nps()
            for h in range(4):
                hp, hb = h // 2, (h % 2) * 64
                mm(pm_[hb:hb + 64, hp * 64:(hp + 1) * 64], AV[0:C_, h, 0:64], TK(1, h), True, True, [bAV, btok], bpm_, h == 3)
            op("dve", lambda e: e.tensor_tensor(out=MT[:, :, :], in0=pm_[:, 0:128].rearrange("p (a k) -> p a k", a=2),
                                                in1=dP[:, :, c, :], op=ALU.add), reads=[bpm_, bdP], writes=[bMT])
            pnn, bpnn = nps()
            for h in range(4):
                hp, hb = h // 2, (h % 2) * 64
                mm(pnn[hb:hb + 64, hp * 64:(hp + 1) * 64], TK(1, h), AV[0:C_, h, 64:128], True, False, [bAV, btok], bpnn, False)
                mm(pnn[hb:hb + 64, hp * 64:(hp + 1) * 64], TK(2, h), TK(3, h), False, True, [btok], bpnn, h == 3)
            op("act", lambda e: e.copy(out=NNs[:, :, :], in_=pnn[:, 0:128].rearrange("p (a v) -> p a v", a=2)),
               reads=[bpnn], writes=[bNNs])
            if RWS == 6:
                continue
            puE, bpuE = nps(); puO, bpuO = nps()
            for h in range(4):
                hp, hb = h // 2, (h % 2) * 64
                pu, bpu = (puE, bpuE) if h % 2 == 0 else (puO, bpuO)
                mm(pu[0:C_, hp * 64:(hp + 1) * 64], AhT[hb:hb + 64, hp, 0:C_], ST_[hb:hb + 64, hp, :], True, True, [bAhT, bST_], bpu, True)
            for h in range(4):
                hp = h // 2
                pu, bpu = (puE, bpuE) if h % 2 == 0 else (puO, bpuO)
                op("dve", lambda e: e.tensor_tensor(out=Ub[0:C_, h, :], in0=pu[0:C_, hp * 64:(hp + 1) * 64],
                                                    in1=AV[0:C_, h, 64:128], op=ALU.add), reads=[bpu, bAV], writes=[bUb])
            for h in range(4):
                hp, hb = h // 2, (h % 2) * 64
                py1_, bpy1_ = (pY1, bpY1) if h % 2 == 0 else (pY1o, bpY1o)
                mm(py1_[hb:hb + 64, hp * 256 + c * C_:hp * 256 + (c + 1) * C_], ST_[hb:hb + 64, hp, :], ARv[hb:hb + 64, hp, c, 1, :],
                   True, True, [bST_, bAR], bpy1_, True)
            for h in range(4):
                hp, hb = h // 2, (h % 2) * 64
                oy = pY2[hb:hb + 64, hp * 256 + c * C_:hp * 256 + (c + 1) * C_]
                mm(oy, Ub[0:C_, h, :], A1[0:C_, h, 1, 0:C_], True, False, [bUb, bA1], bpY2, False)
                mm(oy, TK(3, h), A2[0:C_, h, 1, 0:C_], False, True, [btok, bA2], bpY2, h == 3)
            psE, bpsE = nps(); psO, bpsO = nps()
            for h in range(4):
                hp, hb = h // 2, (h % 2) * 64
                ps_, bps_ = (psE, bpsE) if h % 2 == 0 else (psO, bpsO)
                mm(ps_[hb:hb + 64, hp * 64:(hp + 1) * 64], MT[hb:hb + 64, hp, :], ST_[hb:hb + 64, hp, :], True, True, [bMT, bST_], bps_, True)
            need_out = sample or (gi == NG - 1 and c == nch - 1)
            for hh, (ps_, bps_) in enumerate(((psE, bpsE), (psO, bpsO))):
                prt = slice(hh * 64, hh * 64 + 64)
                psv = ps_[prt, 0:128].rearrange("p (a v) -> p a v", a=2)
                op("dve", lambda e: e.tensor_tensor(out=STn[prt, :, :], in0=psv, in1=NNs[prt, :, :], op=ALU.add),
                   reads=[bps_, bNNs], writes=[bSTn])
                if need_out:
                    op("dve", lambda e: e.tensor_tensor(out=STf[prt, :, :], in0=psv, in1=NNs[prt, :, :], op=ALU.add),
                       reads=[bps_, bNNs], writes=[bSTf])
            if need_out:
                poE, bpoE = nps(); poO, bpoO = nps()
                for h in range(4):
                    hp, hb = h // 2, (h % 2) * 64
                    po_, bpo_ = (poE, bpoE) if h % 2 == 0 else (poO, bpoO)
                    op("pe", lambda e: e.transpose(out=po_[0:64, hp * 64:(hp + 1) * 64], in_=STf[hb:hb + 64, hp, :],
                                                   identity=identf[hb:hb + 64, hb:hb + 64]),
                       reads=[bSTf] + KC, writes=[bpo_])
                for h in range(4):
                    hp = h // 2
                    po_, bpo_ = (poE, bpoE) if h % 2 == 0 else (poO, bpoO)
                    op("act", lambda e: e.copy(out=swo[:, h * 64:(h + 1) * 64], in_=po_[0:64, hp * 64:(hp + 1) * 64]),
                       reads=[bpo_], writes=[bswo])
                dsto = wkv_s[l, c * 256:(c + 1) * 256, :] if sample else wkv_p[l, :, :]
                for h in range(4):
                    dma("sp", lambda e: e.dma_start(out=dsto[h * 64:(h + 1) * 64, :], in_=swo[0:64, h * 64:(h + 1) * 64]),
                        reads=[bswo])
            sti += 1
        rw_state["i"] = sti
        if 2 <= RWS <= 6:
            op("pool", lambda e: e.memset(mixT[:, 0:2, 0:NT], 0.0), writes=[bmixT])
            return
        yT_, byT_ = F8, bF8
        op("act", lambda e: e.copy(out=yT_[0:64, :, 0:NT], in_=pY1[0:64, 0:512].rearrange("p (a t) -> p a t", a=2)[:, :, 0:NT]),
           reads=[bpY1], writes=[byT_])
        op("act", lambda e: e.copy(out=yT_[64:128, :, 0:NT], in_=pY1o[64:128, 0:512].rearrange("p (a t) -> p a t", a=2)[:, :, 0:NT]),
           reads=[bpY1o], writes=[byT_])
        op("dve", lambda e: e.tensor_tensor(out=yT_[:, :, 0:NT], in0=yT_[:, :, 0:NT],
                                            in1=pY2[:, 0:512].rearrange("p (a t) -> p a t", a=2)[:, :, 0:NT], op=ALU.add),
           reads=[bpY2, byT_], writes=[byT_])
        for j in range(2):
            pg_, bpg_ = nps()
            mm(pg_[:, 0:NT], g2t[:, j * 128:(j + 1) * 128], sdg[:, 0:NT], True, True, [bg2, bsdg], bpg_, True)
            op("act", lambda e: e.copy(out=F1[:, j, 0:NT], in_=pg_[:, 0:NT]), reads=[bpg_], writes=[bF1])
        for j in range(2):
            yj = yT_[:, j, 0:NT]
            op("pool", lambda e: e.tensor_copy(out=sq[0][:, 0:NT], in_=yj), reads=[byT_], writes=[bsq[0]])
            op("pool", lambda e: e.tensor_tensor(out=sq[1][:, 0:NT], in0=yj, in1=yj, op=ALU.mult), reads=[byT_], writes=[bsq[1]])
            pm1, bpm1 = nps(); pm2, bpm2 = nps()
            mm(pm1[:, 0:NT], C("blk64"), sq[0][:, 0:NT], True, True, [bsq[0]] + KC, bpm1, True)
            mm(pm2[:, 0:NT], C("blk64"), sq[1][:, 0:NT], True, True, [bsq[1]] + KC, bpm2, True)
            mu_ = F6[:, j, 0:NT]; var_ = F7[:, j, 0:NT]
            op("dve", lambda e: e.tensor_scalar(out=mu_, in0=pm1[:, 0:NT], scalar1=1.0 / 64, scalar2=None, op0=ALU.mult), reads=[bpm1], writes=[bF6])
            op("dve", lambda e: e.tensor_tensor(out=var_, in0=mu_, in1=mu_, op=ALU.mult), reads=[bF6], writes=[bF7])
            op("dve", lambda e: e.scalar_tensor_tensor(out=var_, in0=pm2[:, 0:NT], scalar=1.0 / 64, in1=var_,
                                                       op0=ALU.mult, op1=ALU.subtract), reads=[bpm2, bF7], writes=[bF7])
            op("dve", lambda e: e.tensor_scalar(out=var_, in0=var_, scalar1=GN_EPS, scalar2=None, op0=ALU.add), reads=[bF7], writes=[bF7])
            rsqrt_inplace(var_, bF7)
            op("dve", lambda e: e.tensor_tensor(out=yj, in0=yj, in1=mu_, op=ALU.subtract), reads=[byT_, bF6], writes=[byT_])
            op("dve", lambda e: e.tensor_tensor(out=yj, in0=yj, in1=var_, op=ALU.mult), reads=[byT_, bF7], writes=[byT_])
            op("dve", lambda e: e.tensor_scalar(out=yj, in0=yj, scalar1=pc("gn_w", j), scalar2=pc("gn_b", j), op0=ALU.mult, op1=ALU.add),
               reads=[byT_, bpcol], writes=[byT_])
            op("dve", lambda e: e.scalar_tensor_tensor(out=sq[0][:, 0:NT], in0=rX[:, j, :], scalar=pc("r_k", j), in1=F5[:, j, 0:NT],
                                                       op0=ALU.mult, op1=ALU.mult), reads=[bxsh, bF5, bpcol], writes=[bsq[0]])
            pb_, bpb_ = nps()
            mm(pb_[:, 0:NT], C("blk64"), sq[0][:, 0:NT], True, True, [bsq[0]] + KC, bpb_, True)
            op("dve", lambda e: e.tensor_tensor(out=mu_, in0=pb_[:, 0:NT], in1=vX[:, j, :], op=ALU.mult), reads=[bpb_, bxsh], writes=[bF6])
            op("dve", lambda e: e.tensor_tensor(out=yj, in0=yj, in1=mu_, op=ALU.add), reads=[byT_, bF6], writes=[byT_])
            op("dve", lambda e: e.tensor_tensor(out=mixT[:, j, 0:NT], in0=yj, in1=F1[:, j, 0:NT], op=ALU.mult),
               reads=[byT_, bF1], writes=[bmixT])

    def sample_attn(l):
        NT = 128
        Rflat = Rf[:].rearrange("p a b -> p (a b)")
        eS = Rflat[:, 0:1024]
        wS = Rflat[:, 1024:2048]
        Rbflat = Rb[:].rearrange("p a b -> p (a b)")
        lkN = Rbflat[:, 0:1024]
        wN = Rbflat[:, 1024:2048]
        cmask = xtok[:, 0:1024]
        dma("sp", lambda e: e.dma_start(out=pti[:, :], in_=ptab.partition_broadcast(128)), writes=[bpti])
        op("dve", lambda e: e.tensor_copy(out=x1[:, 0:256], in_=pti[:, :]), reads=[bpti], writes=[bx1])
        op("dve", lambda e: e.tensor_scalar(out=x1[:, 0:256], in0=x1[:, 0:256], scalar1=128.0, scalar2=cf[:, 128:129],
                                            op0=ALU.mult, op1=ALU.add), reads=[bx1] + KC, writes=[bx1])
        op("dve", lambda e: e.tensor_scalar(out=x1[:, 0:256], in0=x1[:, 0:256], scalar1=float(l * nphys * 128), scalar2=None,
                                            op0=ALU.add), reads=[bx1], writes=[bx1])
        op("dve", lambda e: e.tensor_copy(out=idxi[:, :], in_=x1[:, 0:256]), reads=[bx1], writes=[bidxi])
        op("act", lambda e: e.activation(out=cexp[:, :], in_=sbb[:, :], func=AF.Exp), reads=[bsbb], writes=[bcexp])
        op("dve", lambda e: e.tensor_tensor(out=cmask.rearrange("p (h c) -> p h c", h=8),
                                            in0=C("sm2").unsqueeze(1).to_broadcast([128, 8, 128]),
                                            in1=cexp[:, :].unsqueeze(2).to_broadcast([128, 8, 128]), op=ALU.mult),
           reads=[bcexp] + KC, writes=[bxtok])
        dma("sp", lambda e: e.dma_start(out=qT8[:, :, :].rearrange("p (a b) t -> p a b t", b=2)[:, :, 0, :], in_=qT[0:64, :, 0:128]),
            reads=[bqT], writes=[bqT8])
        dma("sp", lambda e: e.dma_start(out=qT8[:, :, :].rearrange("p (a b) t -> p a b t", b=2)[:, :, 1, :], in_=qT[64:128, :, 0:128]),
            reads=[bqT], writes=[bqT8])
        ptk_, bptk_ = nps()
        ptkv_ = ptk_[:, 0:512].bitcast(BF16)
        for h in range(8):
            op("pe", lambda e: e.transpose(out=ptkv_[0:64, h * 128:(h + 1) * 128], in_=kbf[:, h * 64:(h + 1) * 64], identity=identb),
               reads=[bkbf] + KC, writes=[bptk_])
        op("act", lambda e: e.copy(out=KTs8[:, :, :], in_=ptkv_[0:64, :].rearrange("p (h t) -> p h t", h=8)),
           reads=[bptk_], writes=[bKTs8])
        po, bpo = PS[6]
        mm(po[:, :], C("zrow", rows=slice(0, 1), cols=slice(0, 128)), C("zrow", rows=slice(0, 1)), True, True, KC, bpo, True)
        zb = [nps(), nps()]
        for h in range(8):
            pz, bpz = zb[h // 4]
            mm(pz[:, (h % 4) * 128:(h % 4 + 1) * 128], KTs8[:, h, :], qT8[:, h, :], True, True, [bKTs8, bqT8], bpz, True)
        for i, (pz, bpz) in enumerate(zb):
            op("act", lambda e: e.activation(out=eS[:, i * 512:(i + 1) * 512], in_=pz[:, :], func=AF.Exp), reads=[bpz], writes=[bRf])
        op("dve", lambda e: e.tensor_tensor(out=eS, in0=eS, in1=cmask, op=ALU.mult), reads=[bRf, bxtok], writes=[bRf])
        op("act", lambda e: e.activation(out=lkN, in_=eS, func=AF.Ln, bias=1.0, scale=1.0), reads=[bRf], writes=[bRb])
        lb = [nps(), nps()]
        for i, (pl_, bpl_) in enumerate(lb):
            mm(pl_[:, :], C("ntri"), lkN[:, i * 512:(i + 1) * 512], True, False, [bRb] + KC, bpl_, True)
        for h in range(8):
            pl_, bpl_ = lb[h // 4]
            mm(pl_[:, (h % 4) * 128:(h % 4 + 1) * 128], KTs8[:, h, :], qT8[:, h, :], False, True, [bKTs8, bqT8], bpl_, True)
        for i, (pl_, bpl_) in enumerate(lb):
            op("act", lambda e: e.activation(out=wS[:, i * 512:(i + 1) * 512], in_=pl_[:, :], func=AF.Exp), reads=[bpl_], writes=[bRf])
        op("dve", lambda e: e.tensor_tensor(out=wN, in0=wS, in1=cmask, op=ALU.mult), reads=[bRf, bxtok], writes=[bRb])
        for h in range(8):
            hp, hb = h // 2, (h % 2) * 64
            mm(po[hb:hb + 64, hp * 128:(hp + 1) * 128], Vs[:, h * 64:(h + 1) * 64], wN[:, h * 128:(h + 1) * 128], False, False,
               [bVs, bRb], bpo, True)
        lkv = lkN.rearrange("p (h s t) -> p h s t", h=8, s=NSS)
        ce3 = cexp[:, :].unsqueeze(2).to_broadcast([128, 8, 8])
        kpgs = [(ksq, bksq), (ktok, bktok)]
        pgi = 0
        for s_ in range(NSS):
            op("pool", lambda e: e.tensor_copy(out=Rs[:, :].rearrange("p (h t) -> p h t", h=8), in_=lkv[:, :, s_, :]),
               reads=[bRb], writes=[bRs])
            op("pool", lambda e: e.tensor_copy(out=Rsb[:, :], in_=Rs[:, :]), reads=[bRs], writes=[bRsb])
            for j in range(NPAGES - 1, -1, -1):
                col = s_ * NPAGES + j
                kpg, bkpg = kpgs[pgi % 2]
                ktp, bktp = ktp2[pgi % 2]
                vb_, bvb_ = vl4[pgi % 3]
                pgi += 1
                dma("pool", lambda e: e.indirect_dma_start(out=kpg[:, :], out_offset=None, in_=ck[:, :],
                    in_offset=bass.IndirectOffsetOnAxis(ap=idxi[:, col:col + 1], axis=0)), reads=[bidxi], writes=[bkpg])
                dma("pool", lambda e: e.indirect_dma_start(out=vtok[:, :], out_offset=None, in_=cv[:, :],
                    in_offset=bass.IndirectOffsetOnAxis(ap=idxi[:, col:col + 1], axis=0)), reads=[bidxi], writes=[bvtok])
                op("pool", lambda e: e.tensor_copy(out=vb_[:, :], in_=vtok[:, :]), reads=[bvtok], writes=[bvb_])
                op("dve", lambda e: e.tensor_copy(out=kbf[:, :], in_=kpg[:, :]), reads=[bkpg], writes=[bkbf])
                ptp, bptp = nps()
                ptpv = ptp[:, 0:512].bitcast(BF16)
                for h in range(8):
                    op("pe", lambda e: e.transpose(out=ptpv[0:64, h * 128:(h + 1) * 128], in_=kbf[:, h * 64:(h + 1) * 64],
                                                   identity=identb), reads=[bkbf] + KC, writes=[bptp])
                op("act", lambda e: e.copy(out=ktp[:, :], in_=ptpv[0:64, :]), reads=[bptp], writes=[bktp])
                pz, bpz = nps()
                for h in range(8):
                    mm(pz[:, h * 8:(h + 1) * 8], ktp[:, h * 128:(h + 1) * 128], qT8[:, h, s_ * 8:(s_ + 1) * 8], True, True,
                       [bktp, bqT8], bpz, True)
                eb, beb = eb3[pgi % 2]
                lk, blk = lk3[pgi % 3]
                wT, bwT = wT3[pgi % 3]
                op("act", lambda e: e.activation(out=eb[:, 0:64], in_=pz[:, 0:64], func=AF.Exp), reads=[bpz], writes=[beb])
                op("dve", lambda e: e.tensor_tensor(out=eb[:, 0:64].rearrange("p (h t) -> p h t", h=8),
                                                    in0=eb[:, 0:64].rearrange("p (h t) -> p h t", h=8), in1=ce3, op=ALU.mult),
                   reads=[beb, bcexp], writes=[beb])
                op("act", lambda e: e.activation(out=lk[:, 0:64], in_=eb[:, 0:64], func=AF.Ln, bias=1.0, scale=1.0),
                   reads=[beb], writes=[blk])
                pl_, bpl_ = nps()
                mm(pl_[:, 0:64], C("ntri"), lk[:, 0:64], True, False, [blk] + KC, bpl_, True)
                mm(pl_[:, 0:64], C("nones"), Rsb[:, :], False, False, [bRsb] + KC, bpl_, True)
                for h in range(8):
                    mm(pl_[:, h * 8:(h + 1) * 8], ktp[:, h * 128:(h + 1) * 128], qT8[:, h, s_ * 8:(s_ + 1) * 8], False, h == 7,
                       [bktp, bqT8], bpl_, True)
                op("act", lambda e: e.activation(out=eb[:, 64:128], in_=pl_[:, 0:64], func=AF.Exp), reads=[bpl_], writes=[beb])
                op("dve", lambda e: e.tensor_tensor(out=wT[:, 0:64].rearrange("p (h t) -> p h t", h=8),
                                                    in0=eb[:, 64:128].rearrange("p (h t) -> p h t", h=8), in1=ce3, op=ALU.mult),
                   reads=[beb, bcexp], writes=[bwT])
                for h in range(8):
                    hp, hb = h // 2, (h % 2) * 64
                    mm(po[hb:hb + 64, hp * 128 + s_ * 8:hp * 128 + (s_ + 1) * 8], vb_[:, h * 64:(h + 1) * 64], wT[:, h * 8:(h + 1) * 8],
                       False, False, [bvb_, bwT], bpo, True)
                if j > 0:
                    op("pool", lambda e: e.tensor_tensor(out=Rs[:, :], in0=Rs[:, :], in1=lk[:, 0:64], op=ALU.add),
                       reads=[bRs, blk], writes=[bRs])
                    op("pool", lambda e: e.tensor_copy(out=Rsb[:, :], in_=Rs[:, :]), reads=[bRs], writes=[bRsb])
        for hp in range(4):
            op("act", lambda e: e.copy(out=mixT[:, 4 + hp, 0:NT], in_=po[:, hp * 128:(hp + 1) * 128]), reads=[bpo], writes=[bmixT])

    for l in range(nlayers):
        P1 = Scope(nc)
        win, bwin = P1.sb("win", [128, 8, INC], BF16)
        wout, bwout = P1.sb("wout", [128, 8, D], BF16)
        w2t, bw2 = P1.sb("w2t", [128, 256], BF16)
        g2t, bg2 = P1.sb("g2t", [128, 256], BF16)
        pcol, bpcol = P1.sb("pcol", [128, 64], F32)
        kgn, bkgn = P1.sb("kgn", [128, 64], F32)
        sbb, bsbb = P1.sb("sbb", [128, 8], F32)
        sbr, bsbr = P1.sb("sbr", [1, 64], F32)
        wsrc = W["w_in"][l].rearrange("(kc p) n -> p kc n", p=128)
        for kc in range(8):
            dma("pool", lambda e, kc=kc: e.dma_start(out=win[:, kc, :], in_=wsrc[:, kc, :]), writes=[bwin])
        wsrc2 = W["w_out"][l].rearrange("(kc p) n -> p kc n", p=128)
        for kc in range(0, 8, 4):
            dma("pool", lambda e, kc=kc: e.dma_start(out=wout[:, kc:kc + 4, :], in_=wsrc2[:, kc:kc + 4, :]), writes=[bwout])
        dma("pool", lambda e: e.dma_start(out=w2t[0:64, :], in_=W["w2"][l]), writes=[bw2])
        dma("pool", lambda e: e.dma_start(out=w2t[64:128, :], in_=W["a2"][l]), writes=[bw2])
        dma("pool", lambda e: e.dma_start(out=g2t[:, :], in_=W["g2"][l]), writes=[bg2])
        PCO = {"g_mix": 0, "mu": 8, "omu": 16, "w0": 24, "a0": 26, "k_k": 28, "k_a": 30, "r_k": 32, "gn_w": 34,
               "gn_b": 36, "conv": 38, "qg": 44, "omka": 46}

        def ldcol(name, off, n):
            src = W[name][l].rearrange("(c p) -> p c", p=128)
            dma("sp", lambda e: e.dma_start(out=pcol[:, off:off + n], in_=src, allow_slow_non_contiguous=True),
                writes=[bpcol])
        ldcol("g_mix", 0, 8); ldcol("mu_shift", 8, 8)
        for nm in ("w0", "a0", "k_k", "k_a", "r_k", "gn_w", "gn_b"):
            ldcol(nm, PCO[nm], 2)
        for j in range(3):
            srcj = W["conv_w"][l, j].rearrange("(c p) -> p c", p=128)
            dma("sp", lambda e, j=j, srcj=srcj: e.dma_start(out=pcol[:, 38 + 2 * j:40 + 2 * j], in_=srcj,
                                                          allow_slow_non_contiguous=True), writes=[bpcol])
        qgsrc = W["q_gain"][l].rearrange("(p o) -> p o", o=1)
        dma("sp", lambda e: e.dma_start(out=pcol[0:64, 44:45], in_=qgsrc), writes=[bpcol])
        dma("sp", lambda e: e.dma_start(out=pcol[64:128, 44:45], in_=qgsrc), writes=[bpcol])
        op("dve", lambda e: e.tensor_scalar(out=pcol[:, 16:24], in0=pcol[:, 8:16], scalar1=-1.0, scalar2=1.0,
                                            op0=ALU.mult, op1=ALU.add), reads=[bpcol], writes=[bpcol])
        op("dve", lambda e: e.tensor_scalar(out=pcol[:, 44:45], in0=pcol[:, 44:45], scalar1=0.125, scalar2=None,
                                            op0=ALU.mult), reads=[bpcol], writes=[bpcol])
        op("dve", lambda e: e.tensor_scalar(out=pcol[:, 46:48], in0=pcol[:, 30:32], scalar1=-1.0, scalar2=1.0,
                                            op0=ALU.mult, op1=ALU.add), reads=[bpcol], writes=[bpcol])
        dma("sp", lambda e: e.dma_start(out=kgn[:, :], in_=W["k_gain"][l:l + 1, :].partition_broadcast(128)), writes=[bkgn])
        dma("sp", lambda e: e.dma_start(out=sbb[:, :], in_=W["sb_bias"][l:l + 1, :].partition_broadcast(128)), writes=[bsbb])
        op("dve", lambda e: e.tensor_copy(out=sbr[0:1, :].rearrange("p (h t) -> p h t", t=8),
                                          in_=sbb[0:1, :].unsqueeze(2).to_broadcast([1, 8, 8])),
           reads=[bsbb], writes=[bsbr])

        def pc(name, c=0, n=1):
            o = PCO[name] + c
            return pcol[:, o:o + n]

        rwc, brwc = P1.sb("rwc", [128, 8], F32)
        cvc, bcvc = P1.sb("cvc", [128, 2, 2], F32)
        op("dve", lambda e: e.memset(rwc[:], 0.0), writes=[brwc])
        op("dve", lambda e: e.memset(cvc[:], 0.0), writes=[bcvc])

        G = Scope(nc)
        xT, bxT = G.sb("xT", [128, 8, 256], F32)
        hT, bhT = G.sb("hT", [128, 8, 256], BF16)
        sq2 = [G.sb("sq%d" % i, [128, 256], BF16) for i in range(2)]
        sq, bsq = [t for t, _ in sq2], [b for _, b in sq2]
        rstd, brstd = G.sb("rstd", [128, 256], F32)
        xtok, bxtok = G.sb("xtok", [128, D], F32)
        rwt2 = [G.sb("rwt%d" % i, [128, 264], F32) for i in range(2)]
        sttT, bsttT = G.sb("sttT", [128, 8, NSS], F32)
        shs, bshs = G.sb("shs", [128, 8, NSS], F32)
        xsh, bxsh = G.sb("xsh", [128, 8, 256], F32)
        cvb, bcvb = G.sb("cvb", [128, 2, 256], F32)
        cvu, bcvu = G.sb("cvu", [128, 2, 320], F32)
        cvt, bcvt = G.sb("cvt", [128, 2, 256], F32)
        cst, bcst = G.sb("cst", [NSS * 2, 256], F32)
        qT, bqT = G.sb("qT", [128, 4, 256], BF16)
        qf, bqf = G.sb("qf", [128, 256], F32)
        mixT, bmixT = G.sb("mixT", [128, 8, 256], BF16)
        ktok, bktok = G.sb("ktok", [128, 512], F32)
        ksq, bksq = G.sb("ksq", [128, 512], F32)
        kss, bkss = G.sb("kss", [128, 8], F32)
        kbf, bkbf = G.sb("kbf", [128, 512], BF16)
        ktb, bktb = G.sb("ktb", [128, 512], BF16)
        vtok, bvtok = G.sb("vtok", [128, 512], F32)
        vbf, bvbf = G.sb("vbf", [128, 512], BF16)
        KTs, bKTs = G.sb("KTs", [128, 512], BF16)
        Vs, bVs = G.sb("Vs", [128, 512], BF16)
        x1, bx1 = G.sb("x1", [128, 256], F32)
        eb3 = [G.sb("ebuf%d" % i, [128, 256], F32) for i in range(2)]
        lk3 = [G.sb("lk%d" % i, [128, 256], BF16) for i in range(3)]
        wT3 = [G.sb("wT%d" % i, [128, 256], BF16) for i in range(3)]
        Rf, bRf = G.sb("Rf", [128, 8, 256], F32)
        Rb, bRb = G.sb("Rb", [128, 8, 256], BF16)
        ktl4 = [G.sb("ktl%d" % i, [128, 512], BF16) for i in range(3)]
        vl4 = [G.sb("vl%d" % i, [128, 512], BF16) for i in range(3)]
        rot = {"e": 0, "l": 0, "w": 0, "k": 0}
        rw_state = {"i": 0}
        pti, bpti = G.sb("pti", [128, 256], I32)
        idxi, bidxi = G.sb("idxi", [128, 256], I32)
        cexp, bcexp = G.sb("cexp", [128, 8], F32)
        qT8, bqT8 = G.sb("qT8", [64, 8, 128], BF16)
        KTs8, bKTs8 = G.sb("KTs8", [64, 8, 128], BF16)
        ktp2 = [G.sb("ktp%d" % i, [64, 1024], BF16) for i in range(2)]
        Rs, bRs = G.sb("Rs", [128, 64], F32)
        Rsb, bRsb = G.sb("Rsb", [128, 64], BF16)
        pY1, bpY1 = PS[6]
        pY2, bpY2 = PS[7]
        pY1o, bpY1o = PS[5]
        dwa, bdwa = G.sb("dwa", [128, 256], BF16)
        sdg, bsdg = G.sb("sdg", [128, 256], BF16)
        F1, bF1 = G.sb("F1", [128, 2, 256], F32); F2, bF2 = G.sb("F2", [128, 2, 256], F32)
        F4, bF4 = G.sb("F4", [128, 2, 256], F32); F5, bF5 = G.sb("F5", [128, 2, 256], F32)
        F6, bF6 = G.sb("F6", [128, 2, 256], F32); F7, bF7 = G.sb("F7", [128, 2, 256], F32)
        F8, bF8 = G.sb("F8", [128, 2, 256], F32)
        AR, bAR = G.sb("AR", [128, 2, 512], BF16)
        btT, bbtT = G.sb("btT", [128, 2, 256], BF16); ktT, bktT = G.sb("ktT", [128, 2, 256], BF16)
        bhX, bbhX = G.sb("bhX", [128, 2, 256], BF16); khT, bkhT = G.sb("khT", [128, 2, 256], BF16)
        vbT, bvbT = G.sb("vbT", [128, 2, 256], BF16)
        dP, bdP = G.sb("dP", [128, 2, 16, 64], BF16)
        tok, btok = G.sb("tok", [64, 1024], BF16)
        A1, bA1 = G.sb("A1", [64, 4, 2, 64], BF16); A2, bA2 = G.sb("A2", [64, 4, 2, 64], BF16)
        _n = [G.sb("Nm%d" % i, [64, 4, 64], BF16) for i in range(2)]; Nm = [t for t, _ in _n]; bNm = [b for _, b in _n]
        _n = [G.sb("NmT%d" % i, [64, 4, 64], BF16) for i in range(2)]; NmT = [t for t, _ in _n]; bNmT = [b for _, b in _n]
        _n = [G.sb("TT%d" % i, [64, 4, 64], BF16) for i in range(2)]; TT = [t for t, _ in _n]; bTT = [b for _, b in _n]
        Qm, bQm = G.sb("Qm", [64, 4, 64], BF16)
        XW, bXW = G.sb("XW", [64, 4, 2, 64], BF16)
        AV, bAV = G.sb("AV", [64, 4, 128], BF16)
        AhT, bAhT = G.sb("AhT", [128, 2, 64], BF16)
        MT, bMT = G.sb("MT", [128, 2, 64], BF16)
        NNs, bNNs = G.sb("NNs", [128, 2, 64], F32)
        Ub, bUb = G.sb("Ub", [64, 4, 64], BF16)
        _n = [G.sb("STa%d" % i, [128, 2, 64], BF16) for i in range(2)]; STa = [t for t, _ in _n]; bSTa = [b for _, b in _n]
        STf, bSTf = G.sb("STf", [128, 2, 64], F32)
        swo, bswo = G.sb("swo", [64, 256], F32)
        s0t, bs0t = G.sb("s0t", [128, 128], F32)
        s0b, bs0b = G.sb("s0b", [128, 128], BF16)

        for gi, (t0, NT, nseq, T) in enumerate(groups):
            sample = nseq > 1
            ntile = NT // 128
            if l == 0:
                src = xs if sample else xp
                for tt in range(ntile):
                    r0 = tt * 128 if sample else t0 + tt * 128
                    dma("sp", lambda e, r0=r0, src=src: e.dma_start(out=xtok[:], in_=src[r0:r0 + 128, :]), writes=[bxtok])
                    for c4 in range(2):
                        pt, bpt = nps()
                        for c in range(4):
                            cc = c4 * 4 + c
                            op("pe", lambda e, c=c, cc=cc, pt=pt: e.transpose(out=pt[:, c * 128:(c + 1) * 128],
                               in_=xtok[:, cc * 128:(cc + 1) * 128], identity=identf),
                               reads=[bxtok] + KC, writes=[bpt], inc=(c == 3))
                        op("act", lambda e, pt=pt, c4=c4, tt=tt: e.copy(
                            out=xT[:, c4 * 4:c4 * 4 + 4, tt * 128:(tt + 1) * 128],
                            in_=pt[:, :].rearrange("p (c t) -> p c t", c=4)), reads=[bpt], writes=[bxT])
            else:
                for c in range(8):
                    dma("sp", lambda e, c=c: e.dma_start(out=xT[:, c, 0:NT], in_=xls[c, :, t0:t0 + NT]),
                        reads=[bxls[gi]], writes=[bxT])

            rmsnorm(NT, xT, bxT, lambda c: pc("g_mix", c), bpcol, hT, bhT, sq, bsq, rstd, brstd)

            def proj_chunk(cc):
                pt, bpt = nps()
                for kc in range(8):
                    mm(pt[:, 0:NT], win[:, kc, cc * 128:(cc + 1) * 128], hT[:, kc, 0:NT], kc == 0, kc == 7,
                       [bwin, bhT], bpt, kc == 7)
                return pt, bpt

            if sample:
                dma("sp", lambda e: e.dma_start(out=xtok[0:NSS, :], in_=s_shift[l]), writes=[bxtok])
                pt, bpt = nps()
                for c in range(8):
                    op("pe", lambda e, c=c, pt=pt: e.transpose(out=pt[:, c * 16:(c + 1) * 16],
                       in_=xtok[0:NSS, c * 128:(c + 1) * 128], identity=identf[0:NSS, 0:NSS]),
                       reads=[bxtok] + KC, writes=[bpt], inc=(c == 7))
                op("act", lambda e, pt=pt: e.copy(out=sttT[:, :, :], in_=pt[:, 0:128].rearrange("p (c s) -> p c s", c=8)),
                   reads=[bpt], writes=[bsttT])
            xshv = xsh[:, :, 0:NT].rearrange("p c (s t) -> p c s t", s=nseq)
            for cc in range(8):
                rwt, brwt = rwt2[cc % 2]
                rwv = rwt[:, 0:nseq * (T + 1)].rearrange("p (s t) -> p s t", s=nseq)
                pt, bpt = proj_chunk(cc)
                op("act", lambda e, pt=pt, rwv=rwv: e.copy(
                    out=rwv[:, :, 1:T + 1], in_=pt[:, 0:NT].rearrange("p (s t) -> p s t", s=nseq)),
                   reads=[bpt], writes=[brwt])
                if sample:
                    op("act", lambda e, rwv=rwv, cc=cc: e.copy(out=rwv[:, :, 0], in_=sttT[:, cc, :]),
                       reads=[bsttT], writes=[brwt])
                    op("pool", lambda e, rwv=rwv, cc=cc: e.tensor_copy(out=shs[:, cc, :], in_=rwv[:, :, T]),
                       reads=[brwt], writes=[bshs])
                else:
                    op("act", lambda e, rwv=rwv, cc=cc: e.copy(out=rwv[:, 0, 0:1], in_=rwc[:, cc:cc + 1]),
                       reads=[brwc], writes=[brwt])
                    op("pool", lambda e, rwv=rwv, cc=cc: e.tensor_copy(out=rwc[:, cc:cc + 1], in_=rwv[:, 0, T:T + 1]),
                       reads=[brwt], writes=[brwc])
                eng = "dve"
                op("pool", lambda e, cc=cc, rwv=rwv: e.tensor_scalar(out=xshv[:, cc], in0=rwv[:, :, 0:T], scalar1=pc("mu", cc),
                                                                 scalar2=None, op0=ALU.mult),
                   reads=[brwt, bpcol], writes=[bxsh])
                op(eng, lambda e, cc=cc, rwv=rwv: e.scalar_tensor_tensor(
                    out=xshv[:, cc], in0=rwv[:, :, 1:T + 1], scalar=pc("omu", cc), in1=xshv[:, cc],
                    op0=ALU.mult, op1=ALU.add), reads=[brwt, bxsh, bpcol], writes=[bxsh])
            if sample:
                for c in range(8):
                    dma("sp", lambda e, c=c: e.dma_start(
                        out=shift_s[l, :, c * 128:(c + 1) * 128].rearrange("s p -> p s"), in_=shs[:, c, :],
                        allow_slow_non_contiguous=True), reads=[bshs])
            elif gi == NG - 1:
                dma("sp", lambda e: e.dma_start(out=shift_p[l].rearrange("(c p) -> p c", p=128),
                                                in_=rwc[:, :], allow_slow_non_contiguous=True), reads=[brwc])

            cuv = cvu[:, :, 0:nseq * (T + 2)].rearrange("p c (s t) -> p c s t", s=nseq)
            if sample:
                dma("sp", lambda e: e.dma_start(out=cst[:, :], in_=s_conv[l]), writes=[bcst])
                pt, bpt = nps()
                for c in range(2):
                    op("pe", lambda e, c=c, pt=pt: e.transpose(out=pt[:, c * 32:(c + 1) * 32],
                       in_=cst[0:32, c * 128:(c + 1) * 128], identity=identf[0:32, 0:32]),
                       reads=[bcst] + KC, writes=[bpt], inc=(c == 1))
                op("act", lambda e, pt=pt: e.copy(out=cuv[:, :, :, 0:2],
                                                  in_=pt[:, 0:64].rearrange("p (c s j) -> p c s j", c=2, j=2)),
                   reads=[bpt], writes=[bcvu])
            else:
                op("act", lambda e: e.copy(out=cuv[:, :, 0, 0:2], in_=cvc[:, :, :]), reads=[bcvc], writes=[bcvu])
            for c in range(2):
                pt, bpt = proj_chunk(8 + c)
                op("act", lambda e, pt=pt, c=c: e.copy(out=cvb[:, c, 0:NT], in_=pt[:, 0:NT]), reads=[bpt], writes=[bcvb])
            for c in range(2):
                pt, bpt = proj_chunk(10 + c)
                op("act", lambda e, pt=pt, c=c: e.copy(out=cvt[:, c, 0:NT], in_=pt[:, 0:NT]), reads=[bpt], writes=[bcvt])
                pt2, bpt2 = proj_chunk(12 + c)
                op("dve", lambda e, pt2=pt2, c=c: e.tensor_tensor(
                    out=cuv[:, c, :, 2:T + 2], in0=cvt[:, c, 0:NT].rearrange("p (s t) -> p s t", s=nseq),
                    in1=pt2[:, 0:NT].rearrange("p (s t) -> p s t", s=nseq), op=ALU.mult),
                   reads=[bpt2, bcvt], writes=[bcvu])
            cvtv = cvt[:, :, 0:NT].rearrange("p c (s t) -> p c s t", s=nseq)
            for c in range(2):
                eng = "dve"
                op(eng, lambda e, c=c: e.tensor_scalar(out=cvtv[:, c], in0=cuv[:, c, :, 0:T], scalar1=pc("conv", 0 + c),
                                                       scalar2=None, op0=ALU.mult), reads=[bcvu, bpcol], writes=[bcvt])
                for j in (1, 2):
                    op(eng, lambda e, c=c, j=j: e.scalar_tensor_tensor(
                        out=cvtv[:, c], in0=cuv[:, c, :, j:j + T], scalar=pc("conv", 2 * j + c), in1=cvtv[:, c],
                        op0=ALU.mult, op1=ALU.add), reads=[bcvu, bcvt, bpcol], writes=[bcvt])
                op(eng, lambda e, c=c: e.tensor_tensor(out=mixT[:, 2 + c, 0:NT], in0=cvt[:, c, 0:NT],
                                                       in1=cvb[:, c, 0:NT], op=ALU.mult),
                   reads=[bcvt, bcvb], writes=[bmixT])
            if sample:
                for c in range(2):
                    for j in range(2):
                        dma("sp", lambda e, c=c, j=j: e.dma_start(
                            out=conv_s[l].rearrange("(s j) f -> j f s", j=2)[j, c * 128:(c + 1) * 128, :],
                            in_=cuv[:, c, :, T + j], allow_slow_non_contiguous=True), reads=[bcvu])
            else:
                op("pool", lambda e: e.tensor_copy(out=cvc[:, :, :], in_=cuv[:, :, 0, T:T + 2]), reads=[bcvu], writes=[bcvc])
                if gi == NG - 1:
                    for j in range(2):
                        dma("sp", lambda e, j=j: e.dma_start(out=conv_p[l, j].rearrange("(c p) -> p c", p=128),
                                                             in_=cvc[:, :, j], allow_slow_non_contiguous=True),
                            reads=[bcvc])

            for hp in range(4):
                pt, bpt = proj_chunk(14 + hp)
                op("act", lambda e, pt=pt: e.copy(out=qf[:, 0:NT], in_=pt[:, 0:NT]), reads=[bpt], writes=[bqf])
                op("act", lambda e, pt=pt: e.activation(out=sq[0][:, 0:NT], in_=pt[:, 0:NT], func=AF.Square),
                   reads=[bpt], writes=[bsq[0]])
                pr, bpr = nps()
                mm(pr[:, 0:NT], C("blk64"), sq[0][:, 0:NT], True, True, [bsq[0]] + KC, bpr, True)
                op("dve", lambda e, pr=pr: e.tensor_scalar(out=rstd[:, 0:NT], in0=pr[:, 0:NT], scalar1=1.0 / 64,
                                                           scalar2=RMS_EPS, op0=ALU.mult, op1=ALU.add),
                   reads=[bpr], writes=[brstd])
                rsqrt_inplace(rstd[:, 0:NT], brstd)
                op("dve", lambda e, hp=hp: e.scalar_tensor_tensor(out=qT[:, hp, 0:NT], in0=qf[:, 0:NT], scalar=pc("qg"),
                                                                  in1=rstd[:, 0:NT], op0=ALU.mult, op1=ALU.mult),
                   reads=[bqf, brstd, bpcol], writes=[bqT])

            for tt in range(ntile):
                tok0 = t0 + tt * 128
                kbi = tok0 // 128
                for which in range(2):
                    pt, bpt = nps()
                    cbase = 2304 if which == 0 else 2816
                    for kc in range(8):
                        mm(pt[:, :], hT[:, kc, tt * 128:(tt + 1) * 128], win[:, kc, cbase:cbase + 512], kc == 0, kc == 7,
                           [bwin, bhT], bpt, kc == 7)
                    if which == 0:
                        op("act", lambda e, pt=pt: e.copy(out=ktok[:, :], in_=pt[:, :]), reads=[bpt], writes=[bktok])
                        op("pool", lambda e: e.tensor_tensor(out=ksq[:, :], in0=ktok[:, :], in1=ktok[:, :], op=ALU.mult),
                           reads=[bktok], writes=[bksq])
                        op("dve", lambda e: e.tensor_reduce(out=kss[:, :], in_=ksq[:, :].rearrange("p (h d) -> p h d", d=64),
                                                            axis=AX.X, op=ALU.add), reads=[bksq], writes=[bkss])
                        op("dve", lambda e: e.tensor_scalar(out=kss[:, :], in0=kss[:, :], scalar1=1.0 / 64, scalar2=RMS_EPS,
                                                            op0=ALU.mult, op1=ALU.add), reads=[bkss], writes=[bkss])
                        rsqrt_inplace(kss[:, :], bkss)
                        kv3 = ktok[:, :].rearrange("p (h d) -> p h d", d=64)
                        op("dve", lambda e, kv3=kv3: e.tensor_tensor(out=kv3, in0=kv3,
                                                                     in1=kss[:, :].unsqueeze(2).to_broadcast([128, 8, 64]),
                                                                     op=ALU.mult), reads=[bktok, bkss], writes=[bktok])
                        op("dve", lambda e, kv3=kv3: e.tensor_tensor(out=kv3, in0=kv3,
                                                                     in1=kgn[:, :].unsqueeze(1).to_broadcast([128, 8, 64]),
                                                                     op=ALU.mult), reads=[bktok, bkgn], writes=[bktok])
                        dst = k_s[l, :, :] if sample else k_p[l, tok0:tok0 + 128, :]
                        dma("sp", lambda e, dst=dst: e.dma_start(out=dst, in_=ktok[:, :]), reads=[bktok])
                        op("pool", lambda e: e.tensor_copy(out=kbf[:, :], in_=ktok[:, :]), reads=[bktok], writes=[bkbf])
                        ptb, bptb = nps()
                        ptbv = ptb[:, 0:256].bitcast(BF16)
                        for hp in range(4):
                            op("pe", lambda e, hp=hp, ptbv=ptbv: e.transpose(out=ptbv[:, hp * 128:(hp + 1) * 128],
                               in_=kbf[:, hp * 128:(hp + 1) * 128], identity=identb),
                               reads=[bkbf] + KC, writes=[bptb], inc=(hp == 3))
                        if sample:
                            op("act", lambda e, ptbv=ptbv: e.copy(out=KTs[:, :], in_=ptbv), reads=[bptb], writes=[bKTs])
                        else:
                            op("act", lambda e, ptbv=ptbv: e.copy(out=ktb[:, :], in_=ptbv), reads=[bptb], writes=[bktb])
                            dma("sp", lambda e, kbi=kbi: e.dma_start(out=ktscr[kbi], in_=ktb[:, :]),
                                reads=[bktb], writes=[bkts[kbi]])
                    else:
                        op("act", lambda e, pt=pt: e.copy(out=vtok[:, :], in_=pt[:, :]), reads=[bpt], writes=[bvtok])
                        dst = v_s[l, :, :] if sample else v_p[l, tok0:tok0 + 128, :]
                        dma("sp", lambda e, dst=dst: e.dma_start(out=dst, in_=vtok[:, :]), reads=[bvtok])
                        if sample:
                            op("pool", lambda e: e.tensor_copy(out=Vs[:, :], in_=vtok[:, :]), reads=[bvtok], writes=[bVs])
                        else:
                            op("pool", lambda e: e.tensor_copy(out=vbf[:, :], in_=vtok[:, :]), reads=[bvtok], writes=[bvbf])
                            dma("sp", lambda e, kbi=kbi: e.dma_start(out=vscr[kbi], in_=vbf[:, :]),
                                reads=[bvbf], writes=[bvss[kbi]])

            psO = [PS[6], PS[7]]
            if not sample:
                g = gi
                for (po, bpo) in psO:
                    mm(po[:, :], C("zrow", rows=slice(0, 1), cols=slice(0, 128)), C("zrow", rows=slice(0, 1)), True, True,
                       KC, bpo, True)
                op("pool", lambda e: e.memset(Rf[:], 0.0), writes=[bRf])
                op("pool", lambda e: e.memset(Rb[:], 0.0), writes=[bRb])
                kbmax = 2 * g + 1
                for kb in range(kbmax, -1, -1):
                    j = kb - 2 * g
                    cs = max(0, j) * 128
                    ktl, bktl = ktl4[rot["k"] % 3]
                    vl, bvl = vl4[rot["k"] % 3]
                    rot["k"] += 1
                    dma("sp", lambda e, kb=kb, ktl=ktl: e.dma_start(out=ktl[:, :], in_=ktscr[kb]), reads=[bkts[kb]], writes=[bktl])
                    dma("sp", lambda e, kb=kb, vl=vl: e.dma_start(out=vl[:, :], in_=vscr[kb]), reads=[bvss[kb]], writes=[bvl])
                    for h in range(8):
                        hp, hb = h // 2, (h % 2) * 64
                        eb, beb = eb3[rot["e"] % 2]; rot["e"] += 1
                        lk, blk = lk3[rot["l"] % 3]; rot["l"] += 1
                        wT, bwT = wT3[rot["w"] % 3]; rot["w"] += 1
                        kop = ktl[hb:hb + 64, hp * 128:(hp + 1) * 128]
                        qop = qT[hb:hb + 64, hp, cs:NT]
                        pa, bpa = nps()
                        mm(pa[:, cs:NT], kop, qop, True, True, [bktl, bqT], bpa, True)
                        op("act", lambda e, pa=pa, eb=eb, h=h, cs=cs: e.activation(
                            out=eb[:, cs:NT], in_=pa[:, cs:NT], func=AF.Exp, bias=sbb[:, h:h + 1], scale=1.0),
                           reads=[bpa, bsbb], writes=[beb])
                        op("act", lambda e, eb=eb, lk=lk, cs=cs: e.activation(
                            out=lk[:, cs:NT], in_=eb[:, cs:NT], func=AF.Ln, bias=1.0, scale=1.0),
                           reads=[beb], writes=[blk])
                        if j >= 0:
                            op("dve", lambda e, lk=lk, cs=cs: e.tensor_tensor(
                                out=lk[:, cs:cs + 128], in0=lk[:, cs:cs + 128], in1=C("mlt"), op=ALU.mult),
                               reads=[blk] + KC, writes=[blk])
                        pb, bpb = nps()
                        mm(pb[:, cs:NT], C("ntri"), lk[:, cs:NT], True, False, [blk] + KC, bpb, False)
                        if kb < kbmax:
                            mm(pb[:, cs:NT], C("nones"), Rb[:, h, cs:NT], False, False, [bRb] + KC, bpb, False)
                        mm(pb[:, cs:NT], kop, qop, False, True, [bktl, bqT], bpb, True)
                        op("act", lambda e, pb=pb, wT=wT, h=h, cs=cs: e.activation(
                            out=wT[:, cs:NT], in_=pb[:, cs:NT], func=AF.Exp, bias=sbb[:, h:h + 1], scale=1.0),
                           reads=[bpb, bsbb], writes=[bwT])
                        if j >= 0:
                            op("dve", lambda e, wT=wT, cs=cs: e.tensor_tensor(
                                out=wT[:, cs:cs + 128], in0=wT[:, cs:cs + 128], in1=C("mlt"), op=ALU.mult),
                               reads=[bwT] + KC, writes=[bwT])
                        po, bpo = psO[hp // 2]
                        co = (hp % 2) * 256
                        mm(po[hb:hb + 64, co + cs:co + NT], vl[:, h * 64:(h + 1) * 64], wT[:, cs:NT], False, False,
                           [bvl, bwT], bpo, True)
                        if kb > 0:
                            op("pool", lambda e, lk=lk, h=h, cs=cs: e.tensor_tensor(
                                out=Rf[:, h, cs:NT], in0=Rf[:, h, cs:NT], in1=lk[:, cs:NT], op=ALU.add),
                               reads=[bRf, blk], writes=[bRf])
                            op("pool", lambda e, h=h, cs=cs: e.tensor_copy(out=Rb[:, h, cs:NT], in_=Rf[:, h, cs:NT]),
                               reads=[bRf], writes=[bRb])
                for hp in range(4):
                    po, bpo = psO[hp // 2]
                    co = (hp % 2) * 256
                    op("act", lambda e, po=po, co=co, hp=hp: e.copy(out=mixT[:, 4 + hp, 0:NT], in_=po[:, co:co + NT]),
                       reads=[bpo], writes=[bmixT])
            else:
                if USE_CACHE:
                    sample_attn(l)
                else:
                    sample_attention(l, locals())

            rwkv_mix(l, gi, t0, NT, nseq, T, sample)

            for j in range(8):
                pt, bpt = nps()
                for kc in range(8):
                    mm(pt[:, 0:NT], wout[:, kc, j * 128:(j + 1) * 128], mixT[:, kc, 0:NT], kc == 0, kc == 7,
                       [bwout, bmixT], bpt, kc == 7)
                op("dve", lambda e, pt=pt, j=j: e.tensor_tensor(out=x1[:, 0:NT], in0=pt[:, 0:NT], in1=xT[:, j, 0:NT],
                                                                op=ALU.add), reads=[bpt, bxT], writes=[bx1])
                dma("sp", lambda e, j=j: e.dma_start(out=x1s[j, :, t0:t0 + NT], in_=x1[:, 0:NT]), reads=[bx1],
                    writes=[bx1s[gi]])
        kbd.barrier()
        G.close()
        P1.close()
        if not do_phase2:
            continue
        P2 = Scope(nc)
        wup, bwup = P2.sb("wup", [128, 8, 4 * D], BF16)
        wdn, bwdn = P2.sb("wdn", [128, 32, D], BF16)
        wpg, bwpg = P2.sb("wpg", [128, 8, D], BF16)
        wpp, bwpp = P2.sb("wpp", [128, 2, D], BF16)
        pc2, bpc2 = P2.sb("pc2", [128, 16], F32)
        s_up = W["w_up"][l].rearrange("(kc p) n -> p kc n", p=128)
        for kc in range(8):
            dma("pool", lambda e, kc=kc: e.dma_start(out=wup[:, kc, :], in_=s_up[:, kc, :]), writes=[bwup])
        s_dn = W["w_down"][l].rearrange("(kc p) n -> p kc n", p=128)
        for kc in range(0, 32, 4):
            dma("pool", lambda e, kc=kc: e.dma_start(out=wdn[:, kc:kc + 4, :], in_=s_dn[:, kc:kc + 4, :]), writes=[bwdn])
        s_pg = W["w_ple_gate"][l].rearrange("(kc p) n -> p kc n", p=128)
        for kc in range(0, 8, 4):
            dma("pool", lambda e, kc=kc: e.dma_start(out=wpg[:, kc:kc + 4, :], in_=s_pg[:, kc:kc + 4, :]), writes=[bwpg])
        s_pp = W["w_ple_proj"][l].rearrange("(kc p) n -> p kc n", p=128)
        dma("pool", lambda e: e.dma_start(out=wpp[:, :, :], in_=s_pp), writes=[bwpp])
        for nm, off in (("g_mlp", 0), ("g_ple", 8)):
            srcc = W[nm][l].rearrange("(c p) -> p c", p=128)
            dma("sp", lambda e, srcc=srcc, off=off: e.dma_start(out=pc2[:, off:off + 8], in_=srcc,
                                                               allow_slow_non_contiguous=True), writes=[bpc2])
        G2 = Scope(nc)
        xT, bxT = G2.sb("xT2", [128, 8, 256], F32)
        hT, bhT = G2.sb("hT2", [128, 8, 256], BF16)
        sq2 = [G2.sb("sq2%d" % i, [128, 256], BF16) for i in range(2)]
        sq, bsq = [t for t, _ in sq2], [b for _, b in sq2]
        rstd, brstd = G2.sb("rstd2", [128, 256], F32)
        actT, bactT = G2.sb("actT", [128, 32, 256], BF16)
        rl2 = [G2.sb("rl%d" % i, [128, 256], BF16) for i in range(2)]
        ptok, bptok = G2.sb("ptok", [128, 256], F32)
        pT, bpT = G2.sb("pT", [128, 2, 256], BF16)
        gt2 = [G2.sb("gt%d" % i, [128, 256], F32) for i in range(2)]
        ytok, bytok = G2.sb("ytok", [128, D], F32)
        for gi, (t0, NT, nseq, T) in enumerate(groups):
            sample = nseq > 1
            ntile = NT // 128
            for c in range(8):
                dma("sp", lambda e, c=c: e.dma_start(out=xT[:, c, 0:NT], in_=x1s[c, :, t0:t0 + NT]),
                    reads=[bx1s[gi]], writes=[bxT])
            rmsnorm(NT, xT, bxT, lambda c: pc2[:, c:c + 1], bpc2, hT, bhT, sq, bsq, rstd, brstd)
            for fc in range(32):
                pt, bpt = nps((0, 1, 2, 3, 4, 5, 6, 7))
                for kc in range(8):
                    mm(pt[:, 0:NT], wup[:, kc, fc * 128:(fc + 1) * 128], hT[:, kc, 0:NT], kc == 0, kc == 7,
                       [bwup, bhT], bpt, kc == 7)
                rl, brl = rl2[fc % 2]
                op("act", lambda e, pt=pt, rl=rl: e.activation(out=rl[:, 0:NT], in_=pt[:, 0:NT], func=AF.Relu),
                   reads=[bpt], writes=[brl])
                op("pool" if fc % 2 == 0 else "dve", lambda e, rl=rl, fc=fc: e.tensor_tensor(
                    out=actT[:, fc, 0:NT], in0=rl[:, 0:NT], in1=rl[:, 0:NT], op=ALU.mult), reads=[brl], writes=[bactT])
            for j in range(8):
                pt, bpt = nps((0, 1, 2, 3, 4, 5, 6, 7))
                for fc in range(32):
                    mm(pt[:, 0:NT], wdn[:, fc, j * 128:(j + 1) * 128], actT[:, fc, 0:NT], fc == 0, fc == 31,
                       [bwdn, bactT], bpt, fc == 31)
                op("dve", lambda e, pt=pt, j=j: e.tensor_tensor(out=xT[:, j, 0:NT], in0=pt[:, 0:NT], in1=xT[:, j, 0:NT],
                                                                op=ALU.add), reads=[bpt, bxT], writes=[bxT])
            rmsnorm(NT, xT, bxT, lambda c: pc2[:, 8 + c:9 + c], bpc2, hT, bhT, sq, bsq, rstd, brstd)
            psrc = psm[l] if sample else pp[l]
            for tt in range(ntile):
                r0 = tt * 128 if sample else t0 + tt * 128
                dma("sp", lambda e, r0=r0, psrc=psrc: e.dma_start(out=ptok[:, :], in_=psrc[r0:r0 + 128, :]), writes=[bptok])
                pt, bpt = nps((0, 1, 2, 3, 4, 5, 6, 7))
                for c in range(2):
                    op("pe", lambda e, c=c, pt=pt: e.transpose(out=pt[:, c * 128:(c + 1) * 128],
                       in_=ptok[:, c * 128:(c + 1) * 128], identity=identf), reads=[bptok] + KC, writes=[bpt], inc=(c == 1))
                op("act", lambda e, pt=pt, tt=tt: e.copy(out=pT[:, :, tt * 128:(tt + 1) * 128],
                                                         in_=pt[:, 0:256].rearrange("p (c t) -> p c t", c=2)),
                   reads=[bpt], writes=[bpT])
            for j in range(8):
                pg, bpg = nps((0, 1, 2, 3, 4, 5, 6, 7))
                for kc in range(8):
                    mm(pg[:, 0:NT], wpg[:, kc, j * 128:(j + 1) * 128], hT[:, kc, 0:NT], kc == 0, kc == 7,
                       [bwpg, bhT], bpg, kc == 7)
                gt, bgt = gt2[j % 2]
                op("act", lambda e, pg=pg, gt=gt: e.activation(out=gt[:, 0:NT], in_=pg[:, 0:NT], func=AF.Sigmoid),
                   reads=[bpg], writes=[bgt])
                pq, bpq = nps((0, 1, 2, 3, 4, 5, 6, 7))
                for c in range(2):
                    mm(pq[:, 0:NT], wpp[:, c, j * 128:(j + 1) * 128], pT[:, c, 0:NT], c == 0, c == 1,
                       [bwpp, bpT], bpq, c == 1)
                op("dve", lambda e, pq=pq, gt=gt: e.tensor_tensor(out=gt[:, 0:NT], in0=gt[:, 0:NT], in1=pq[:, 0:NT],
                                                                  op=ALU.mult), reads=[bgt, bpq], writes=[bgt])
                op("pool", lambda e, gt=gt, j=j: e.tensor_tensor(out=xT[:, j, 0:NT], in0=xT[:, j, 0:NT], in1=gt[:, 0:NT],
                                                                 op=ALU.add), reads=[bgt, bxT], writes=[bxT])
            if l < nlayers - 1:
                for c in range(8):
                    dma("sp", lambda e, c=c: e.dma_start(out=xls[c, :, t0:t0 + NT], in_=xT[:, c, 0:NT]),
                        reads=[bxT], writes=[bxls[gi]])
            else:
                for tt in range(ntile):
                    for c4 in range(2):
                        pt, bpt = nps((0, 1, 2, 3, 4, 5, 6, 7))
                        for c in range(4):
                            cc = c4 * 4 + c
                            op("pe", lambda e, c=c, cc=cc, pt=pt, tt=tt: e.transpose(
                                out=pt[:, c * 128:(c + 1) * 128], in_=xT[:, cc, tt * 128:(tt + 1) * 128], identity=identf),
                               reads=[bxT] + KC, writes=[bpt], inc=(c == 3))
                        op("act", lambda e, pt=pt, c4=c4: e.copy(out=ytok[:, c4 * 512:(c4 + 1) * 512], in_=pt[:, :]),
                           reads=[bpt], writes=[bytok])
                    dst = y_s[tt * 128:(tt + 1) * 128, :] if sample else y_p[t0 + tt * 128:t0 + (tt + 1) * 128, :]
                    dma("sp", lambda e, dst=dst: e.dma_start(out=dst, in_=ytok[:, :]), reads=[bytok])
        kbd.barrier()
        G2.close()
        P2.close()
    kbd.finish()
    glob.close()
    return nc


def sample_attention(l, L):
    kb = L["kbd"]; mixT = L["mixT"]; bmixT = L["bmixT"]
    kb.op("pool", lambda e: e.memset(mixT[:, 4:8, 0:128], 0.0), writes=[bmixT])


def rwkv(l, gi, t0, NT, nseq, T, sample, L):
    kb = L["kbd"]; mixT = L["mixT"]; bmixT = L["bmixT"]
    kb.op("pool", lambda e: e.memset(mixT[:, 0:2, 0:NT], 0.0), writes=[bmixT])


def kernel(**inp):
    f32 = np.float32
    nphys = int(inp["cache_k"].shape[1])
    nc = build_program(nphys=nphys)
    ck = np.ascontiguousarray(inp["cache_k"], dtype=f32).reshape(2 * nphys * 128, 512)
    cv = np.ascontiguousarray(inp["cache_v"], dtype=f32).reshape(2 * nphys * 128, 512)
    wnames = ["g_mix", "w_in", "mu_shift", "w0", "w2", "a0", "a2", "g2", "k_k", "k_a", "r_k", "gn_w", "gn_b", "conv_w",
              "q_gain", "k_gain", "sb_bias", "w_out", "g_mlp", "w_up", "w_down", "g_ple", "w_ple_gate", "w_ple_proj"]
    wd = {n: np.ascontiguousarray(inp[n], dtype=f32) for n in wnames}
    wd["r_k"] = wd["r_k"].reshape(2, 256)
    in_maps = []
    for c in range(NCORES):
        b = c % 4
        s0 = c * NSS
        m = dict(wd)
        m["xp"] = np.ascontiguousarray(inp["x_prompt"][b])
        m["xs"] = np.ascontiguousarray(inp["x_sample"][s0:s0 + NSS]).reshape(128, D)
        if USE_CACHE:
            m["cache_k"] = ck
            m["cache_v"] = cv
        m["s_wkv"] = np.ascontiguousarray(inp["state_wkv"][:, s0:s0 + NSS]).reshape(2, NSS * 4 * 64, 64)
        m["s_shift"] = np.ascontiguousarray(inp["state_shift"][:, s0:s0 + NSS])
        m["s_conv"] = np.ascontiguousarray(inp["state_conv"][:, s0:s0 + NSS]).reshape(2, NSS * 2, 256)
        m["ptab"] = np.ascontiguousarray(inp["page_table"][s0:s0 + NSS]).reshape(1, NSS * NPAGES).astype(np.int32)
        m["pp"] = np.ascontiguousarray(inp["p_prompt"][:, b])
        m["ps"] = np.ascontiguousarray(inp["p_sample"][:, s0:s0 + NSS]).reshape(2, 128, 256)
        m["consts"] = CONSTS
        in_maps.append(m)
    res = run_bass_kernel_spmd(nc, in_maps, core_ids=list(range(NCORES)))
    R = res.results
    y_prompt = np.stack([R[b]["y_p"] for b in range(4)])
    y_sample = np.concatenate([R[c]["y_s"].reshape(NSS, DT, D) for c in range(NCORES)], 0)
    k_prompt = np.stack([R[b]["k_p"] for b in range(4)], 1).reshape(2, 4, SEQ, 8, 64)
    v_prompt = np.stack([R[b]["v_p"] for b in range(4)], 1).reshape(2, 4, SEQ, 8, 64)
    wkv_prompt = np.stack([R[b]["wkv_p"] for b in range(4)], 1).reshape(2, 4, 4, 64, 64)
    shift_prompt = np.stack([R[b]["shift_p"] for b in range(4)], 1)
    conv_prompt = np.stack([R[b]["conv_p"] for b in range(4)], 1)
    k_sample = np.concatenate([R[c]["k_s"].reshape(2, NSS, DT, 8, 64) for c in range(NCORES)], 1)
    v_sample = np.concatenate([R[c]["v_s"].reshape(2, NSS, DT, 8, 64) for c in range(NCORES)], 1)
    wkv_sample = np.concatenate([R[c]["wkv_s"].reshape(2, NSS, 4, 64, 64) for c in range(NCORES)], 1)
    shift_sample = np.concatenate([R[c]["shift_s"] for c in range(NCORES)], 1)
    conv_sample = np.concatenate([R[c]["conv_s"].reshape(2, NSS, 2, 256) for c in range(NCORES)], 1)
    outs = (y_prompt, y_sample, k_prompt, v_prompt, wkv_prompt, shift_prompt, conv_prompt,
            k_sample, v_sample, wkv_sample, shift_sample, conv_sample)
    return tuple(np.ascontiguousarray(o, dtype=f32) for o in outs)
```

```python
import numpy as np
import concourse.bass as bass
import concourse.mybir as mybir
from concourse.bass_utils import run_bass_kernel_spmd

F32 = mybir.dt.float32
BF16 = mybir.dt.bfloat16
I32 = mybir.dt.int32
AF = mybir.ActivationFunctionType
ALU = mybir.AluOpType
AX = mybir.AxisListType

SAME_ENGINE_SYNC = True
USE_CACHE = True
RWS = 0
RW4 = 0
NCORES = 8
D = 1024
SEQ = 4096
NSS = 16
DT = 8
NPAGES = 16
NPHYS = 2560
INC = 3328
RMS_EPS = 1e-6
GN_EPS = 64e-5


class Buf:
    __slots__ = ("name", "w", "r")

    def __init__(self, name):
        self.name = name
        self.w = None
        self.r = {}


class _Rec:
    def __getattr__(self, name):
        def f(*a, **k):
            self.__dict__["call"] = (name, a, k)
            return self
        return f


def _bind(fn):
    r = _Rec()
    fn(r)
    name, a, k = r.__dict__["call"]
    return lambda e: getattr(e, name)(*a, **k)


class KB:
    ENGS = ("pe", "act", "dve", "pool", "sp")

    def __init__(self, nc, n_dma_sems=40):
        self.nc = nc
        self.q = {e: [] for e in self.ENGS}
        self.cnt = {e: 0 for e in self.ENGS}
        self.known = {e: {} for e in self.ENGS}
        self.pend = {}
        self.sems = {}
        self._ctx = []
        for e in ("pe", "act", "dve", "pool"):
            self._sem("s_" + e)
        self.dpool = {}
        for qn, n in (("sp", n_dma_sems), ("pool", n_dma_sems), ("act", 8)):
            self.dpool[qn] = {"sems": [self._sem(f"d_{qn}{i}") for i in range(n)],
                              "tot": [0] * n, "next": 0}

    def _sem(self, name):
        cm = self.nc.semaphore(name)
        s = cm.__enter__()
        self._ctx.append(cm)
        self.sems[name] = s
        return name

    def _deps(self, eng, reads, writes):
        deps = {}

        def add(ev, ev_eng):
            k, v = ev[0], ev[1]
            if ev_eng == eng and not k.startswith("d_"):
                if eng == "pe" or not SAME_ENGINE_SYNC:
                    return
            if deps.get(k, 0) < v:
                deps[k] = v

        for b in reads:
            if b.w is not None:
                add(b.w, b.w[2])
        for b in writes:
            if b.w is not None:
                add(b.w, b.w[2])
            for re_, ev in b.r.items():
                add(ev, re_.split("#")[0])
        out = []
        kn = self.known[eng]
        for k, v in deps.items():
            if kn.get(k, 0) < v:
                kn[k] = v
                out.append((k, v))
        return out

    def op(self, eng, fn, reads=(), writes=(), inc=True):
        inc = True
        waits = self._deps(eng, reads, writes)
        fn = _bind(fn)
        self.q[eng].append((waits, fn, inc))
        pr, pw = self.pend.setdefault(eng, ([], []))
        if inc:
            self.cnt[eng] += 1
            ev = ("s_" + eng, self.cnt[eng], eng)
            for b in list(reads) + pr:
                b.r[eng] = (ev[0], ev[1])
            for b in list(writes) + pw:
                b.w = ev
                b.r = {}
            del pr[:]
            del pw[:]
        else:
            pr.extend(reads)
            pw.extend(writes)

    def dma(self, qn, fn, reads=(), writes=()):
        pool = self.dpool[qn]
        i = pool["next"]
        pool["next"] = (i + 1) % len(pool["sems"])
        key = pool["sems"][i]
        waits = self._deps(qn, reads, writes)
        prev = pool["tot"][i]
        kn = self.known[qn]
        if prev > 0 and kn.get(key, 0) < prev:
            kn[key] = prev
            waits.append((key, prev))
        val = prev + 16
        pool["tot"][i] = val
        fn = _bind(fn)
        self.q[qn].append((waits, fn, ("dma", key)))
        ev = (key, val, "dma")
        for b in reads:
            b.r["dma#%s%d" % (qn, i)] = (key, val)
        for b in writes:
            b.w = ev
            b.r = {}

    def barrier(self):
        for e in self.ENGS:
            waits = []
            kn = self.known[e]
            for e2 in ("pe", "act", "dve", "pool"):
                if e2 == e:
                    continue
                v = self.cnt[e2]
                k = "s_" + e2
                if v > 0 and kn.get(k, 0) < v:
                    kn[k] = v
                    waits.append((k, v))
            for qn, pool in self.dpool.items():
                for i, k in enumerate(pool["sems"]):
                    v = pool["tot"][i]
                    if v > 0 and kn.get(k, 0) < v:
                        kn[k] = v
                        waits.append((k, v))
            if waits:
                self.q[e].append((waits, None, False))

    def finish(self):
        self.barrier()
        nc = self.nc
        sems = self.sems
        q = self.q

        def emit(e, eobj):
            for waits, fn, inc in q[e]:
                for k, v in waits:
                    eobj.wait_ge(sems[k], v)
                if fn is None:
                    continue
                ins = fn(eobj)
                if inc is True:
                    ins.then_inc(sems["s_" + e], 1)
                elif inc:
                    ins.then_inc(sems[inc[1]], 16)

        with nc.Block() as block:
            @block.tensor
            def _(t):
                emit("pe", t)

            @block.scalar
            def _(t):
                emit("act", t)

            @block.vector
            def _(t):
                emit("dve", t)

            @block.gpsimd
            def _(t):
                emit("pool", t)

            @block.sync
            def _(t):
                emit("sp", t)
        for cm in reversed(self._ctx):
            cm.__exit__(None, None, None)
        self._ctx = []


class Scope:
    def __init__(self, nc):
        self.nc = nc
        self.stack = []

    UID = [0]

    def sb(self, name, shape, dt):
        Scope.UID[0] += 1
        name = "%s_%d" % (name, Scope.UID[0])
        cm = self.nc.sbuf_tensor(name, list(shape), dt)
        t = cm.__enter__()
        self.stack.append(cm)
        return t, Buf(name)

    def ps(self, name, shape, dt=F32):
        Scope.UID[0] += 1
        name = "%s_%d" % (name, Scope.UID[0])
        cm = self.nc.psum_tensor(name, list(shape), dt)
        t = cm.__enter__()
        self.stack.append(cm)
        return t, Buf(name)

    def close(self):
        for cm in reversed(self.stack):
            cm.__exit__(None, None, None)
        self.stack = []


def make_consts():
    p = np.arange(128)[:, None]
    f = np.arange(128)[None, :]
    c = {}
    c["ident"] = (p == f)
    c["ntri"] = -(p >= f).astype(np.float32)
    c["nones"] = -np.ones((128, 128), np.float32)
    c["mlt"] = (p < f)
    c["mle"] = (p <= f)
    c["mgt"] = (f < p)
    c["blk64"] = ((p // 64) == (f // 64))
    c["identrep"] = ((p % 64) == np.arange(64)[None, :])
    sp_, tp_ = p // 8, p % 8
    cols = np.arange(NSS * 64)[None, :]
    cs, ct = cols // 64, cols % 8
    c["smask"] = ((sp_ == cs) & (tp_ < ct))
    c["iota"] = p.astype(np.float32)
    c["zrow"] = np.zeros((128, 512), np.float32)
    c["sm2"] = ((p // 8) == (f // 8)) & ((p % 8) < (f % 8))
    c["m12"] = np.concatenate([(p < f)[:, 0:64], (p <= f)[:, 0:64]], axis=1)
    offs = {}
    o = 0
    arrs = []
    for k, v in c.items():
        v = np.asarray(v, np.float32)
        offs[k] = (o, v.shape[1])
        o += v.shape[1]
        arrs.append(v)
    return np.ascontiguousarray(np.concatenate(arrs, axis=1)), offs


CONSTS, COFF = make_consts()
NCONST = CONSTS.shape[1]


def build_program(nlayers=2, do_phase2=True, ngroups=17, nphys=NPHYS):
    nc = bass.Bass("TRN2", target_bir_lowering=False)

    def din(name, shape, dt=F32):
        return nc.dram_tensor(name, list(shape), dt, kind="ExternalInput").ap()

    def dout(name, shape):
        return nc.dram_tensor(name, list(shape), F32, kind="ExternalOutput").ap()

    xp = din("xp", [SEQ, D]); xs = din("xs", [128, D])
    if USE_CACHE:
        ck = din("cache_k", [2 * nphys * 128, 512]); cv = din("cache_v", [2 * nphys * 128, 512])
    s_wkv = din("s_wkv", [2, NSS * 4 * 64, 64]); s_shift = din("s_shift", [2, NSS, 1024])
    s_conv = din("s_conv", [2, NSS * 2, 256]); ptab = din("ptab", [1, NSS * NPAGES], I32)
    pp = din("pp", [2, SEQ, 256]); psm = din("ps", [2, 128, 256])
    consts = din("consts", [128, NCONST])
    W = {}
    for name, shape in (("g_mix", [2, D]), ("w_in", [2, D, INC]), ("mu_shift", [2, D]), ("w0", [2, 256]),
                        ("w2", [2, 64, 256]), ("a0", [2, 256]), ("a2", [2, 64, 256]), ("g2", [2, 128, 256]),
                        ("k_k", [2, 256]), ("k_a", [2, 256]), ("r_k", [2, 256]), ("gn_w", [2, 256]),
                        ("gn_b", [2, 256]), ("conv_w", [2, 3, 256]), ("q_gain", [2, 64]), ("k_gain", [2, 64]),
                        ("sb_bias", [2, 8]), ("w_out", [2, D, D]), ("g_mlp", [2, D]), ("w_up", [2, D, 4 * D]),
                        ("w_down", [2, 4 * D, D]), ("g_ple", [2, D]), ("w_ple_gate", [2, D, D]),
                        ("w_ple_proj", [2, 256, D])):
        W[name] = din(name, shape)
    y_p = dout("y_p", [SEQ, D]); y_s = dout("y_s", [128, D])
    k_p = dout("k_p", [2, SEQ, 512]); v_p = dout("v_p", [2, SEQ, 512])
    wkv_p = dout("wkv_p", [2, 4 * 64, 64]); shift_p = dout("shift_p", [2, D]); conv_p = dout("conv_p", [2, 2, 256])
    k_s = dout("k_s", [2, 128, 512]); v_s = dout("v_s", [2, 128, 512])
    wkv_s = dout("wkv_s", [2, NSS * 4 * 64, 64]); shift_s = dout("shift_s", [2, NSS, D])
    conv_s = dout("conv_s", [2, NSS * 2, 256])
    NTOK = SEQ + 128
    x1s = nc.dram_tensor("x1s", [8, 128, NTOK], F32, kind="Internal").ap()
    xls = nc.dram_tensor("xls", [8, 128, NTOK], F32, kind="Internal").ap()

    kbd = KB(nc)
    op = kbd.op
    dma = kbd.dma
    glob = Scope(nc)

    cf, bcf = glob.sb("cf", [128, 129], F32)
    cb, bcb = glob.sb("cb", [128, NCONST], BF16)
    dma("sp", lambda e: e.dma_start(out=cf[:, 0:128], in_=consts[:, COFF["ident"][0]:COFF["ident"][0] + 128]), writes=[bcf])
    dma("sp", lambda e: e.dma_start(out=cf[:, 128:129], in_=consts[:, COFF["iota"][0]:COFF["iota"][0] + 1], allow_slow_non_contiguous=True), writes=[bcf])
    dma("pool", lambda e: e.dma_start(out=cb[:], in_=consts[:, :]), writes=[bcb])

    def C(name, bf=True, rows=slice(0, 128), cols=None):
        o, n = COFF[name]
        if not bf:
            assert name == "ident"
            return cf[rows, 0:128]
        if cols is None:
            return cb[rows, o:o + n]
        return cb[rows, o + cols.start:o + cols.stop]

    KC = [bcf, bcb]

    PS = [glob.ps("ps%d" % i, [128, 512], F32) for i in range(8)]
    psi = [0]

    def nps(banks=(0, 1, 2, 3, 4)):
        i = banks[psi[0] % len(banks)]
        psi[0] += 1
        return PS[i]

    def mm(out, lhsT, rhs, start, stop, reads, wbuf, inc):
        op("pe", lambda e: e.matmul(out, lhsT=lhsT, rhs=rhs, start=start, stop=stop, skip_group_check=True),
           reads=reads, writes=[wbuf], inc=inc)

    def rsqrt_inplace(t_ap, bt, eng_r="dve"):
        op("act", lambda e: e.activation(out=t_ap, in_=t_ap, func=AF.Sqrt), reads=[bt], writes=[bt])
        op(eng_r, lambda e: e.reciprocal(out=t_ap, in_=t_ap), reads=[bt], writes=[bt])

    NG = 16
    groups = [(g * 256, 256, 1, 256) for g in range(NG)] + [(SEQ, 128, NSS, DT)]
    if ngroups < 0:
        groups = groups[-1:]
    elif ngroups < 17:
        groups = groups[:ngroups]
    ktscr = nc.dram_tensor("ktscr", [32, 128, 512], BF16, kind="Internal").ap()
    vscr = nc.dram_tensor("vscr", [32, 128, 512], BF16, kind="Internal").ap()
    bkts = [Buf("kts%d" % i) for i in range(32)]
    bvss = [Buf("vss%d" % i) for i in range(32)]
    bx1s = [Buf("x1s%d" % i) for i in range(17)]
    bxls = [Buf("xls%d" % i) for i in range(17)]
    identf = C("ident", bf=False)
    identb = C("ident")

    def rmsnorm(NT, src_t, bsrc, gcol, bg, dst_t, bdst, sq, bsq, rstd, brstd):
        pr, bpr = nps()
        for c in range(8):
            op("act", lambda e, c=c: e.activation(out=sq[c % 2][:, 0:NT], in_=src_t[:, c, 0:NT], func=AF.Square),
               reads=[bsrc], writes=[bsq[c % 2]])
            mm(pr[:, 0:NT], C("nones"), sq[c % 2][:, 0:NT], c == 0, c == 7, [bsq[c % 2]] + KC, bpr, c == 7)
        op("dve", lambda e: e.tensor_scalar(out=rstd[:, 0:NT], in0=pr[:, 0:NT], scalar1=-1.0 / D,
                                            scalar2=RMS_EPS, op0=ALU.mult, op1=ALU.add),
           reads=[bpr], writes=[brstd])
        rsqrt_inplace(rstd[:, 0:NT], brstd)
        for c in range(8):
            op("dve", lambda e, c=c: e.scalar_tensor_tensor(
                out=dst_t[:, c, 0:NT], in0=src_t[:, c, 0:NT], scalar=gcol(c), in1=rstd[:, 0:NT],
                op0=ALU.mult, op1=ALU.mult), reads=[bsrc, brstd, bg], writes=[bdst])

    def rwkv_mix(l, gi, t0, NT, nseq, T, sample):
        C_ = 8 if sample else 64
        nch = NT // C_
        EXPM05 = float(np.exp(-0.5))
        rX, kX, vX = xsh[:, 0:2, 0:NT], xsh[:, 2:4, 0:NT], xsh[:, 4:6, 0:NT]

        def F(t):
            return t[:, :, 0:NT]

        op("act", lambda e: e.activation(out=dwa[0:64, 0:NT], in_=xsh[0:64, 6, 0:NT], func=AF.Tanh), reads=[bxsh], writes=[bdwa])
        op("act", lambda e: e.copy(out=dwa[64:128, 0:NT], in_=xsh[64:128, 6, 0:NT]), reads=[bxsh], writes=[bdwa])
        op("act", lambda e: e.activation(out=sdg[:, 0:NT], in_=xsh[:, 7, 0:NT], func=AF.Sigmoid), reads=[bxsh], writes=[bsdg])
        for j in range(2):
            pw_, bpw_ = nps()
            mm(pw_[:, 0:NT], w2t[0:64, j * 128:(j + 1) * 128], dwa[0:64, 0:NT], True, True, [bw2, bdwa], bpw_, True)
            op("act", lambda e: e.activation(out=F1[:, j, 0:NT], in_=pw_[:, 0:NT], func=AF.Sigmoid, bias=pc("w0", j), scale=1.0),
               reads=[bpw_, bpcol], writes=[bF1])
            pa_, bpa_ = nps()
            mm(pa_[:, 0:NT], w2t[64:128, j * 128:(j + 1) * 128], dwa[64:128, 0:NT], True, True, [bw2, bdwa], bpa_, True)
            op("act", lambda e: e.activation(out=F2[:, j, 0:NT], in_=pa_[:, 0:NT], func=AF.Sigmoid, bias=pc("a0", j), scale=1.0),
               reads=[bpa_, bpcol], writes=[bF2])
        op("dve", lambda e: e.tensor_scalar(out=F(F1), in0=F(F1), scalar1=-EXPM05, scalar2=None, op0=ALU.mult),
           reads=[bF1], writes=[bF1])
        for j in range(2):
            op("dve", lambda e: e.tensor_scalar(out=F4[:, j, 0:NT], in0=kX[:, j, :], scalar1=pc("k_k", j), scalar2=None,
                                                op0=ALU.mult), reads=[bxsh, bpcol], writes=[bF4])
            op("pool", lambda e: e.tensor_tensor(out=sq[0][:, 0:NT], in0=F4[:, j, 0:NT], in1=F4[:, j, 0:NT], op=ALU.mult),
               reads=[bF4], writes=[bsq[0]])
            pk_, bpk_ = nps()
            mm(pk_[:, 0:NT], C("blk64"), sq[0][:, 0:NT], True, True, [bsq[0]] + KC, bpk_, True)
            op("dve", lambda e: e.tensor_scalar(out=rstd[:, 0:NT], in0=pk_[:, 0:NT], scalar1=1e-12, scalar2=None, op0=ALU.add),
               reads=[bpk_], writes=[brstd])
            rsqrt_inplace(rstd[:, 0:NT], brstd)
            op("dve", lambda e: e.tensor_tensor(out=F4[:, j, 0:NT], in0=F4[:, j, 0:NT], in1=rstd[:, 0:NT], op=ALU.mult),
               reads=[bF4, brstd], writes=[bF4])
            op("dve", lambda e: e.tensor_scalar(out=F5[:, j, 0:NT], in0=F2[:, j, 0:NT], scalar1=pc("k_a", j), scalar2=pc("omka", j),
                                                op0=ALU.mult, op1=ALU.add), reads=[bF2, bpcol], writes=[bF5])
            op("dve", lambda e: e.tensor_tensor(out=F5[:, j, 0:NT], in0=F5[:, j, 0:NT], in1=kX[:, j, :], op=ALU.mult),
               reads=[bF5, bxsh], writes=[bF5])
        src_t, bsrc_ = F1, bF1
        pp_ = [(F6, bF6), (F7, bF7)]
        k_ = 0
        s_ = 1
        while s_ < C_:
            dst_t, bdst_ = pp_[k_ % 2]
            sv = src_t[:, :, 0:NT].rearrange("p c (n t) -> p c n t", t=C_)
            dv = dst_t[:, :, 0:NT].rearrange("p c (n t) -> p c n t", t=C_)
            op("pool", lambda e: e.tensor_copy(out=dst_t[:, :, 0:NT], in_=src_t[:, :, 0:NT]), reads=[bsrc_], writes=[bdst_])
            for j in range(2):
                op("pool", lambda e: e.tensor_tensor(out=dv[:, j, :, s_:C_], in0=sv[:, j, :, s_:C_], in1=sv[:, j, :, 0:C_ - s_],
                                                     op=ALU.add), reads=[bsrc_, bdst_], writes=[bdst_])
            src_t, bsrc_ = dst_t, bdst_
            k_ += 1
            s_ *= 2
        cl, bcl = src_t, bsrc_
        oth, both = pp_[k_ % 2]
        clv = cl[:, :, 0:NT].rearrange("p c (n t) -> p c n t", t=C_)
        ARv = AR[:, :, 0:2 * NT].rearrange("p c (n w t) -> p c n w t", w=2, t=C_)

        def v4(t):
            return t[:, :, 0:NT].rearrange("p c (n t) -> p c n t", t=C_)
        op("act", lambda e: e.activation(out=F(F8), in_=cl[:, :, 0:NT], func=AF.Exp), reads=[bcl], writes=[bF8])
        for j in range(2):
            op("dve", lambda e: e.tensor_tensor(out=ARv[:, j, :, 1, :], in0=v4(xsh[:, 0:2])[:, j], in1=v4(F8)[:, j], op=ALU.mult),
               reads=[bxsh, bF8], writes=[bAR])
        for j in range(2):
            op("dve", lambda e: e.tensor_tensor(
                out=dP[:, j, 0:nch, :], in0=C("identrep").unsqueeze(1).to_broadcast([128, nch, 64]),
                in1=v4(F8)[:, j, :, C_ - 1:C_].to_broadcast([128, nch, 64]), op=ALU.mult), reads=[bF8] + KC, writes=[bdP])
        op("dve", lambda e: e.tensor_tensor(out=F(oth), in0=F(F4), in1=F(F2), op=ALU.mult), reads=[bF4, bF2], writes=[both])
        op("act", lambda e: e.activation(out=F(F8), in_=cl[:, :, 0:NT], func=AF.Exp, scale=-1.0), reads=[bcl], writes=[bF8])
        op("dve", lambda e: e.tensor_tensor(out=F(btT), in0=F(oth), in1=F(F8), op=ALU.mult), reads=[both, bF8], writes=[bbtT])
        op("dve", lambda e: e.tensor_tensor(out=F(ktT), in0=F(F5), in1=F(F8), op=ALU.mult), reads=[bF5, bF8], writes=[bktT])
        for j in range(2):
            op("pool", lambda e: e.tensor_tensor(out=v4(F8)[:, j], in0=clv[:, j, :, C_ - 1:C_].to_broadcast([128, nch, C_]),
                                                 in1=clv[:, j], op=ALU.subtract), reads=[bcl], writes=[bF8])
        op("act", lambda e: e.activation(out=F(F8), in_=F(F8), func=AF.Exp), reads=[bF8], writes=[bF8])
        op("dve", lambda e: e.tensor_tensor(out=F(bhX), in0=F(oth), in1=F(F8), op=ALU.mult), reads=[both, bF8], writes=[bbhX])
        op("dve", lambda e: e.tensor_tensor(out=F(khT), in0=F(F5), in1=F(F8), op=ALU.mult), reads=[bF5, bF8], writes=[bkhT])
        op("pool", lambda e: e.tensor_tensor(out=F(F8), in0=cl[:, :, 0:NT], in1=F(F1), op=ALU.subtract), reads=[bcl, bF1], writes=[bF8])
        op("act", lambda e: e.activation(out=F(F8), in_=F(F8), func=AF.Exp), reads=[bF8], writes=[bF8])
        for j in range(2):
            op("dve", lambda e: e.scalar_tensor_tensor(out=ARv[:, j, :, 0, :], in0=v4(F4)[:, j], scalar=-1.0, in1=v4(F8)[:, j],
                                                       op0=ALU.mult, op1=ALU.mult), reads=[bF4, bF8], writes=[bAR])
        op("pool", lambda e: e.tensor_copy(out=F(vbT), in_=vX), reads=[bxsh], writes=[bvbT])
        if RWS == 1:
            op("pool", lambda e: e.memset(mixT[:, 0:2, 0:NT], 0.0), writes=[bmixT])
            return
        if not sample and gi == 0:
            op("dve", lambda e: e.memset(STa[0][:], 0.0), writes=[bSTa[0]])
        sti = rw_state["i"]

        m12 = C("m12")
        for c in range(nch):
            cs_ = slice(c * C_, (c + 1) * C_)
            if sample:
                dma("sp", lambda e: e.dma_start(out=s0t[:, 0:64], in_=s_wkv[l, c * 256:c * 256 + 128, :]), writes=[bs0t])
                dma("sp", lambda e: e.dma_start(out=s0t[:, 64:128], in_=s_wkv[l, c * 256 + 128:c * 256 + 256, :]), writes=[bs0t])
                op("pool", lambda e: e.tensor_copy(out=s0b[:, :], in_=s0t[:, :]), reads=[bs0t], writes=[bs0b])
                for hh in range(2):
                    ps0, bps0 = nps()
                    ps0v = ps0[:, 0:64].bitcast(BF16)
                    hb = hh * 64
                    for hp in range(2):
                        op("pe", lambda e: e.transpose(out=ps0v[hb:hb + 64, hp * 64:(hp + 1) * 64],
                                                       in_=s0b[hb:hb + 64, hp * 64:(hp + 1) * 64],
                                                       identity=identb[hb:hb + 64, hb:hb + 64]),
                           reads=[bs0b] + KC, writes=[bps0])
                    op("act", lambda e: e.copy(out=STa[sti % 2][hb:hb + 64, :, :],
                                               in_=ps0v[hb:hb + 64, :].rearrange("p (a v) -> p a v", a=2)),
                       reads=[bps0], writes=[bSTa[sti % 2]])
            ST_, bST_ = STa[sti % 2], bSTa[sti % 2]
            STn, bSTn = STa[(sti + 1) % 2], bSTa[(sti + 1) % 2]
            ptk, bptk = nps()
            ptkv = ptk[:, 0:512].bitcast(BF16)
            srcs = [(lambda hp: ARv[:, hp, c, 0, :], bAR), (lambda hp: bhX[:, hp, cs_], bbhX),
                    (lambda hp: khT[:, hp, cs_], bkhT), (lambda hp: vbT[:, hp, cs_], bvbT)]
            for qi, (sf, bsf) in enumerate(srcs):
                for hp in range(2):
                    o_ = (qi * 2 + hp) * 128
                    op("pe", lambda e: e.transpose(out=ptkv[0:C_, o_:o_ + 128], in_=sf(hp), identity=identb),
                       reads=[bsf] + KC, writes=[bptk], inc=(qi == 3 and hp == 1))
            op("act", lambda e: e.copy(out=tok[0:C_, :], in_=ptkv[0:C_, :]), reads=[bptk], writes=[btok])

            def TK(qi, h):
                o_ = (qi * 2 + h // 2) * 128 + (h % 2) * 64
                return tok[0:C_, o_:o_ + 64]
            if RWS == 2:
                continue
            mk12 = m12[0:C_, :].rearrange("p (w j) -> p w j", w=2)[:, :, 0:C_]
            for which in range(3):
                pE, bpE = nps(); pO, bpO = nps()
                for h in range(4):
                    hp, hb = h // 2, (h % 2) * 64
                    pp2, bpp2 = (pE, bpE) if h % 2 == 0 else (pO, bpO)
                    arr = ARv[hb:hb + 64, hp, c, :, :]
                    if which == 0:
                        mm(pp2[0:C_, hp * 2 * C_:(hp + 1) * 2 * C_], btT[hb:hb + 64, hp, cs_], arr, True, True, [bbtT, bAR], bpp2, True)
                    elif which == 1:
                        mm(pp2[0:C_, hp * 2 * C_:(hp + 1) * 2 * C_], ktT[hb:hb + 64, hp, cs_], arr, True, True, [bktT, bAR], bpp2, True)
                    else:
                        mm(pp2[0:C_, hp * C_:(hp + 1) * C_], ARv[hb:hb + 64, hp, c, 0, :], btT[hb:hb + 64, hp, cs_], True, True,
                           [bbtT, bAR], bpp2, True)
                for h in range(4):
                    hp = h // 2
                    pp2, bpp2 = (pE, bpE) if h % 2 == 0 else (pO, bpO)
                    if which < 2:
                        At, bAt = (A1, bA1) if which == 0 else (A2, bA2)
                        op("dve", lambda e: e.tensor_tensor(
                            out=At[0:C_, h, :, 0:C_], in0=pp2[0:C_, hp * 2 * C_:(hp + 1) * 2 * C_].rearrange("p (w j) -> p w j", w=2),
                            in1=mk12, op=ALU.mult), reads=[bpp2] + KC, writes=[bAt])
                    else:
                        op("dve", lambda e: e.tensor_tensor(
                            out=Nm[0][0:C_, h, 0:C_], in0=pp2[0:C_, hp * C_:(hp + 1) * C_], in1=C("mgt")[0:C_, 0:C_], op=ALU.mult),
                           reads=[bpp2] + KC, writes=[bNm[0]])
            if RWS == 3:
                continue
            identC = identb[0:C_, 0:C_]
            for h in range(4):
                op("pool", lambda e: e.tensor_tensor(out=TT[0][0:C_, h, 0:C_], in0=A1[0:C_, h, 0, 0:C_], in1=identC, op=ALU.add),
                   reads=[bA1] + KC, writes=[bTT[0]])
                op("pool", lambda e: e.tensor_copy(out=NmT[0][0:C_, h, 0:C_], in_=A1[0:C_, h, 0, 0:C_]), reads=[bA1], writes=[bNmT[0]])
            ti = 0; ni = 0
            m_ = 2
            while m_ < C_:
                if RW4 == 1:
                    break
                last = (2 * m_ >= C_)
                pn, bpn = nps(); pnt, bpnt = nps()
                for h in range(4):
                    mm(pn[0:C_, h * C_:(h + 1) * C_], NmT[ni][0:C_, h, 0:C_], Nm[ni][0:C_, h, 0:C_], True, True,
                       [bNm[ni], bNmT[ni]], bpn, h == 3)
                if not last:
                    for h in range(4):
                        mm(pnt[0:C_, h * C_:(h + 1) * C_], Nm[ni][0:C_, h, 0:C_], NmT[ni][0:C_, h, 0:C_], True, True,
                           [bNm[ni], bNmT[ni]], bpnt, h == 3)
                if RW4 == 2:
                    break
                n2 = 1 - ni
                pnv = pn[0:C_, 0:4 * C_].rearrange("p (h j) -> p h j", h=4)
                op("act", lambda e: e.copy(out=Nm[n2][0:C_, :, 0:C_], in_=pnv), reads=[bpn], writes=[bNm[n2]])
                op("dve", lambda e: e.tensor_tensor(out=Qm[0:C_, :, 0:C_], in0=Nm[n2][0:C_, :, 0:C_],
                                                    in1=identC.unsqueeze(1).to_broadcast([C_, 4, C_]), op=ALU.add),
                   reads=[bNm[n2]] + KC, writes=[bQm])
                if not last:
                    op("act", lambda e: e.copy(out=NmT[n2][0:C_, :, 0:C_],
                                               in_=pnt[0:C_, 0:4 * C_].rearrange("p (h j) -> p h j", h=4)),
                       reads=[bpnt], writes=[bNmT[n2]])
                if RW4 == 3:
                    break
                ptt, bptt = nps()
                for h in range(4):
                    mm(ptt[0:C_, h * C_:(h + 1) * C_], Qm[0:C_, h, 0:C_], TT[ti][0:C_, h, 0:C_], True, True, [bQm, bTT[ti]], bptt, h == 3)
                op("act", lambda e: e.copy(out=TT[1 - ti][0:C_, :, 0:C_],
                                           in_=ptt[0:C_, 0:4 * C_].rearrange("p (h j) -> p h j", h=4)),
                   reads=[bptt], writes=[bTT[1 - ti]])
                ti = 1 - ti
                ni = n2
                m_ *= 2
                if RW4 == 4:
                    break
            TTf, bTTf = TT[ti], bTT[ti]
            if RWS == 4:
                continue
            pw2, bpw2 = nps()
            for h in range(4):
                mm(pw2[0:C_, h * 64:(h + 1) * 64], A2[0:C_, h, 0, 0:C_], TK(3, h), True, True, [bA2, btok], bpw2, h == 3)
            op("act", lambda e: e.copy(out=XW[0:C_, :, 1, :], in_=pw2[0:C_, 0:256].rearrange("p (h v) -> p h v", h=4)),
               reads=[bpw2], writes=[bXW])
            op("pool", lambda e: e.tensor_copy(out=XW[0:C_, :, 0, :], in_=tok[0:C_, 0:256].rearrange("p (h k) -> p h k", h=4)),
               reads=[btok], writes=[bXW])
            pav, bpav = nps()
            for h in range(4):
                mm(pav[0:C_, h * 128:(h + 1) * 128], TTf[0:C_, h, 0:C_], XW[0:C_, h, :, :], True, True, [bTTf, bXW], bpav, h == 3)
            op("act", lambda e: e.copy(out=AV[0:C_, :, :], in_=pav[0:C_, 0:512].rearrange("p (h x) -> p h x", h=4)),
               reads=[bpav], writes=[bAV])
            if RWS == 5:
                continue
            pat, bpat = nps()
            for h in range(4):
                hp, hb = h // 2, (h % 2) * 64
                mm(pat[hb:hb + 64, hp * C_:(hp + 1) * C_], TK(0, h), TTf[0:C_, h, 0:C_], True, True, [btok, bTTf], bpat, h == 3)
            op("act", lambda e: e.copy(out=AhT[:, :, 0:C_], in_=pat[:, 0:2 * C_].rearrange("p (a t) -> p a t", a=2)),
               reads=[bpat], writes=[bAhT])
            pm_, bpm_ = nps()
            for h in range(4):
                hp, hb = h // 2, (h % 2) * 64
                mm(pm_[hb:hb + 64, hp * 64:(hp + 1) * 64], AV[0:C_, h, 0:64], TK(1, h), True, True, [bAV, btok], bpm_, h == 3)
            op("dve", lambda e: e.tensor_tensor(out=MT[:, :, :], in0=pm_[:, 0:128].rearrange("p (a k) -> p a k", a=2),
                                                in1=dP[:, :, c, :], op=ALU.add), reads=[bpm_, bdP], writes=[bMT])
            pnn, bpnn = nps()
            for h in range(4):
                hp, hb = h // 2, (h % 2) * 64
                mm(pnn[hb:hb + 64, hp * 64:(hp + 1) * 64], TK(1, h), AV[0:C_, h, 64:128], True, False, [bAV, btok], bpnn, False)
                mm(pnn[hb:hb + 64, hp * 64:(hp + 1) * 64], TK(2, h), TK(3, h), False, True, [btok], bpnn, h == 3)
            op("act", lambda e: e.copy(out=NNs[:, :, :], in_=pnn[:, 0:128].rearrange("p (a v) -> p a v", a=2)),
               reads=[bpnn], writes=[bNNs])
            if RWS == 6:
                continue
            puE, bpuE = nps(); puO, bpuO = nps()
            for h in range(4):
                hp, hb = h // 2, (h % 2) * 64
                pu, bpu = (puE, bpuE) if h % 2 == 0 else (puO, bpuO)
                mm(pu[0:C_, hp * 64:(hp + 1) * 64], AhT[hb:hb + 64, hp, 0:C_], ST_[hb:hb + 64, hp, :], True, True, [bAhT, bST_], bpu, True)
            for h in range(4):
                hp = h // 2
                pu, bpu = (puE, bpuE) if h % 2 == 0 else (puO, bpuO)
                op("dve", lambda e: e.tensor_tensor(out=Ub[0:C_, h, :], in0=pu[0:C_, hp * 64:(hp + 1) * 64],
                                                    in1=AV[0:C_, h, 64:128], op=ALU.add), reads=[bpu, bAV], writes=[bUb])
            for h in range(4):
                hp, hb = h // 2, (h % 2) * 64
                py1_, bpy1_ = (pY1, bpY1) if h % 2 == 0 else (pY1o, bpY1o)
                mm(py1_[hb:hb + 64, hp * 256 + c * C_:hp * 256 + (c + 1) * C_], ST_[hb:hb + 64, hp, :], ARv[hb:hb + 64, hp, c, 1, :],
                   True, True, [bST_, bAR], bpy1_, True)
            for h in range(4):
                hp, hb = h // 2, (h % 2) * 64
                oy = pY2[hb:hb + 64, hp * 256 + c * C_:hp * 256 + (c + 1) * C_]
                mm(oy, Ub[0:C_, h, :], A1[0:C_, h, 1, 0:C_], True, False, [bUb, bA1], bpY2, False)
                mm(oy, TK(3, h), A2[0:C_, h, 1, 0:C_], False, True, [btok, bA2], bpY2, h == 3)
            psE, bpsE = nps(); psO, bpsO = nps()
            for h in range(4):
                hp, hb = h // 2, (h % 2) * 64
                ps_, bps_ = (psE, bpsE) if h % 2 == 0 else (psO, bpsO)
                mm(ps_[hb:hb + 64, hp * 64:(hp + 1) * 64], MT[hb:hb + 64, hp, :], ST_[hb:hb + 64, hp, :], True, True, [bMT, bST_], bps_, True)
            need_out = sample or (gi == NG - 1 and c == nch - 1)
            for hh, (ps_, bps_) in enumerate(((psE, bpsE), (psO, bpsO))):
                prt = slice(hh * 64, hh * 64 + 64)
                psv = ps_[prt, 0:128].rearrange("p (a v) -> p a v", a=2)
                op("dve", lambda e: e.tensor_tensor(out=STn[prt, :, :], in0=psv, in1=NNs[prt, :, :], op=ALU.add),
                   reads=[bps_, bNNs], writes=[bSTn])
                if need_out:
                    op("dve", lambda e: e.tensor_tensor(out=STf[prt, :, :], in0=psv, in1=NNs[prt, :, :], op=ALU.add),
                       reads=[bps_, bNNs], writes=[bSTf])
            if need_out:
                poE, bpoE = nps(); poO, bpoO = nps()
                for h in range(4):
                    hp, hb = h // 2, (h % 2) * 64
                    po_, bpo_ = (poE, bpoE) if h % 2 == 0 else (poO, bpoO)
                    op("pe", lambda e: e.transpose(out=po_[0:64, hp * 64:(hp + 1) * 64], in_=STf[hb:hb + 64, hp, :],
                                                   identity=identf[hb:hb + 64, hb:hb + 64]),
                       reads=[bSTf] + KC, writes=[bpo_])
                for h in range(4):
                    hp = h // 2
                    po_, bpo_ = (poE, bpoE) if h % 2 == 0 else (poO, bpoO)
                    op("act", lambda e: e.copy(out=swo[:, h * 64:(h + 1) * 64], in_=po_[0:64, hp * 64:(hp + 1) * 64]),
                       reads=[bpo_], writes=[bswo])
                dsto = wkv_s[l, c * 256:(c + 1) * 256, :] if sample else wkv_p[l, :, :]
                for h in range(4):
                    dma("sp", lambda e: e.dma_start(out=dsto[h * 64:(h + 1) * 64, :], in_=swo[0:64, h * 64:(h + 1) * 64]),
                        reads=[bswo])
            sti += 1
        rw_state["i"] = sti
        if 2 <= RWS <= 6:
            op("pool", lambda e: e.memset(mixT[:, 0:2, 0:NT], 0.0), writes=[bmixT])
            return
        yT_, byT_ = F8, bF8
        op("act", lambda e: e.copy(out=yT_[0:64, :, 0:NT], in_=pY1[0:64, 0:512].rearrange("p (a t) -> p a t", a=2)[:, :, 0:NT]),
           reads=[bpY1], writes=[byT_])
        op("act", lambda e: e.copy(out=yT_[64:128, :, 0:NT], in_=pY1o[64:128, 0:512].rearrange("p (a t) -> p a t", a=2)[:, :, 0:NT]),
           reads=[bpY1o], writes=[byT_])
        op("dve", lambda e: e.tensor_tensor(out=yT_[:, :, 0:NT], in0=yT_[:, :, 0:NT],
                                            in1=pY2[:, 0:512].rearrange("p (a t) -> p a t", a=2)[:, :, 0:NT], op=ALU.add),
           reads=[bpY2, byT_], writes=[byT_])
        for j in range(2):
            pg_, bpg_ = nps()
            mm(pg_[:, 0:NT], g2t[:, j * 128:(j + 1) * 128], sdg[:, 0:NT], True, True, [bg2, bsdg], bpg_, True)
            op("act", lambda e: e.copy(out=F1[:, j, 0:NT], in_=pg_[:, 0:NT]), reads=[bpg_], writes=[bF1])
        for j in range(2):
            yj = yT_[:, j, 0:NT]
            op("pool", lambda e: e.tensor_copy(out=sq[0][:, 0:NT], in_=yj), reads=[byT_], writes=[bsq[0]])
            op("pool", lambda e: e.tensor_tensor(out=sq[1][:, 0:NT], in0=yj, in1=yj, op=ALU.mult), reads=[byT_], writes=[bsq[1]])
            pm1, bpm1 = nps(); pm2, bpm2 = nps()
            mm(pm1[:, 0:NT], C("blk64"), sq[0][:, 0:NT], True, True, [bsq[0]] + KC, bpm1, True)
            mm(pm2[:, 0:NT], C("blk64"), sq[1][:, 0:NT], True, True, [bsq[1]] + KC, bpm2, True)
            mu_ = F6[:, j, 0:NT]; var_ = F7[:, j, 0:NT]
            op("dve", lambda e: e.tensor_scalar(out=mu_, in0=pm1[:, 0:NT], scalar1=1.0 / 64, scalar2=None, op0=ALU.mult), reads=[bpm1], writes=[bF6])
            op("dve", lambda e: e.tensor_tensor(out=var_, in0=mu_, in1=mu_, op=ALU.mult), reads=[bF6], writes=[bF7])
            op("dve", lambda e: e.scalar_tensor_tensor(out=var_, in0=pm2[:, 0:NT], scalar=1.0 / 64, in1=var_,
                                                       op0=ALU.mult, op1=ALU.subtract), reads=[bpm2, bF7], writes=[bF7])
            op("dve", lambda e: e.tensor_scalar(out=var_, in0=var_, scalar1=GN_EPS, scalar2=None, op0=ALU.add), reads=[bF7], writes=[bF7])
            rsqrt_inplace(var_, bF7)
            op("dve", lambda e: e.tensor_tensor(out=yj, in0=yj, in1=mu_, op=ALU.subtract), reads=[byT_, bF6], writes=[byT_])
            op("dve", lambda e: e.tensor_tensor(out=yj, in0=yj, in1=var_, op=ALU.mult), reads=[byT_, bF7], writes=[byT_])
            op("dve", lambda e: e.tensor_scalar(out=yj, in0=yj, scalar1=pc("gn_w", j), scalar2=pc("gn_b", j), op0=ALU.mult, op1=ALU.add),
               reads=[byT_, bpcol], writes=[byT_])
            op("dve", lambda e: e.scalar_tensor_tensor(out=sq[0][:, 0:NT], in0=rX[:, j, :], scalar=pc("r_k", j), in1=F5[:, j, 0:NT],
                                                       op0=ALU.mult, op1=ALU.mult), reads=[bxsh, bF5, bpcol], writes=[bsq[0]])
            pb_, bpb_ = nps()
            mm(pb_[:, 0:NT], C("blk64"), sq[0][:, 0:NT], True, True, [bsq[0]] + KC, bpb_, True)
            op("dve", lambda e: e.tensor_tensor(out=mu_, in0=pb_[:, 0:NT], in1=vX[:, j, :], op=ALU.mult), reads=[bpb_, bxsh], writes=[bF6])
            op("dve", lambda e: e.tensor_tensor(out=yj, in0=yj, in1=mu_, op=ALU.add), reads=[byT_, bF6], writes=[byT_])
            op("dve", lambda e: e.tensor_tensor(out=mixT[:, j, 0:NT], in0=yj, in1=F1[:, j, 0:NT], op=ALU.mult),
               reads=[byT_, bF1], writes=[bmixT])

    def sample_attn(l):
        NT = 128
        Rflat = Rf[:].rearrange("p a b -> p (a b)")
        eS = Rflat[:, 0:1024]
        wS = Rflat[:, 1024:2048]
        Rbflat = Rb[:].rearrange("p a b -> p (a b)")
        lkN = Rbflat[:, 0:1024]
        wN = Rbflat[:, 1024:2048]
        cmask = xtok[:, 0:1024]
        dma("sp", lambda e: e.dma_start(out=pti[:, :], in_=ptab.partition_broadcast(128)), writes=[bpti])
        op("dve", lambda e: e.tensor_copy(out=x1[:, 0:256], in_=pti[:, :]), reads=[bpti], writes=[bx1])
        op("dve", lambda e: e.tensor_scalar(out=x1[:, 0:256], in0=x1[:, 0:256], scalar1=128.0, scalar2=cf[:, 128:129],
                                            op0=ALU.mult, op1=ALU.add), reads=[bx1] + KC, writes=[bx1])
        op("dve", lambda e: e.tensor_scalar(out=x1[:, 0:256], in0=x1[:, 0:256], scalar1=float(l * nphys * 128), scalar2=None,
                                            op0=ALU.add), reads=[bx1], writes=[bx1])
        op("dve", lambda e: e.tensor_copy(out=idxi[:, :], in_=x1[:, 0:256]), reads=[bx1], writes=[bidxi])
        op("act", lambda e: e.activation(out=cexp[:, :], in_=sbb[:, :], func=AF.Exp), reads=[bsbb], writes=[bcexp])
        op("dve", lambda e: e.tensor_tensor(out=cmask.rearrange("p (h c) -> p h c", h=8),
                                            in0=C("sm2").unsqueeze(1).to_broadcast([128, 8, 128]),
                                            in1=cexp[:, :].unsqueeze(2).to_broadcast([128, 8, 128]), op=ALU.mult),
           reads=[bcexp] + KC, writes=[bxtok])
        dma("sp", lambda e: e.dma_start(out=qT8[:, :, :].rearrange("p (a b) t -> p a b t", b=2)[:, :, 0, :], in_=qT[0:64, :, 0:128]),
            reads=[bqT], writes=[bqT8])
        dma("sp", lambda e: e.dma_start(out=qT8[:, :, :].rearrange("p (a b) t -> p a b t", b=2)[:, :, 1, :], in_=qT[64:128, :, 0:128]),
            reads=[bqT], writes=[bqT8])
        ptk_, bptk_ = nps()
        ptkv_ = ptk_[:, 0:512].bitcast(BF16)
        for h in range(8):
            op("pe", lambda e: e.transpose(out=ptkv_[0:64, h * 128:(h + 1) * 128], in_=kbf[:, h * 64:(h + 1) * 64], identity=identb),
               reads=[bkbf] + KC, writes=[bptk_])
        op("act", lambda e: e.copy(out=KTs8[:, :, :], in_=ptkv_[0:64, :].rearrange("p (h t) -> p h t", h=8)),
           reads=[bptk_], writes=[bKTs8])
        po, bpo = PS[6]
        mm(po[:, :], C("zrow", rows=slice(0, 1), cols=slice(0, 128)), C("zrow", rows=slice(0, 1)), True, True, KC, bpo, True)
        zb = [nps(), nps()]
        for h in range(8):
            pz, bpz = zb[h // 4]
            mm(pz[:, (h % 4) * 128:(h % 4 + 1) * 128], KTs8[:, h, :], qT8[:, h, :], True, True, [bKTs8, bqT8], bpz, True)
        for i, (pz, bpz) in enumerate(zb):
            op("act", lambda e: e.activation(out=eS[:, i * 512:(i + 1) * 512], in_=pz[:, :], func=AF.Exp), reads=[bpz], writes=[bRf])
        op("dve", lambda e: e.tensor_tensor(out=eS, in0=eS, in1=cmask, op=ALU.mult), reads=[bRf, bxtok], writes=[bRf])
        op("act", lambda e: e.activation(out=lkN, in_=eS, func=AF.Ln, bias=1.0, scale=1.0), reads=[bRf], writes=[bRb])
        lb = [nps(), nps()]
        for i, (pl_, bpl_) in enumerate(lb):
            mm(pl_[:, :], C("ntri"), lkN[:, i * 512:(i + 1) * 512], True, False, [bRb] + KC, bpl_, True)
        for h in range(8):
            pl_, bpl_ = lb[h // 4]
            mm(pl_[:, (h % 4) * 128:(h % 4 + 1) * 128], KTs8[:, h, :], qT8[:, h, :], False, True, [bKTs8, bqT8], bpl_, True)
        for i, (pl_, bpl_) in enumerate(lb):
            op("act", lambda e: e.activation(out=wS[:, i * 512:(i + 1) * 512], in_=pl_[:, :], func=AF.Exp), reads=[bpl_], writes=[bRf])
        op("dve", lambda e: e.tensor_tensor(out=wN, in0=wS, in1=cmask, op=ALU.mult), reads=[bRf, bxtok], writes=[bRb])
        for h in range(8):
            hp, hb = h // 2, (h % 2) * 64
            mm(po[hb:hb + 64, hp * 128:(hp + 1) * 128], Vs[:, h * 64:(h + 1) * 64], wN[:, h * 128:(h + 1) * 128], False, False,
               [bVs, bRb], bpo, True)
        lkv = lkN.rearrange("p (h s t) -> p h s t", h=8, s=NSS)
        ce3 = cexp[:, :].unsqueeze(2).to_broadcast([128, 8, 8])
        kpgs = [(ksq, bksq), (ktok, bktok)]
        pages = []
        for s_ in range(NSS):
            for j in range(NPAGES - 1, -1, -1):
                pages.append({"s": s_, "j": j, "i": len(pages)})

        def stageA(B):
            s_, j, pgi = B["s"], B["j"], B["i"]
            col = s_ * NPAGES + j
            kpg, bkpg = kpgs[pgi % 2]
            ktp, bktp = ktp2[pgi % 2]
            vb_, bvb_ = vl4[pgi % 3]
            eb, beb = eb3[pgi % 2]
            lk, blk = lk3[pgi % 3]
            B.update(ktp=ktp, bktp=bktp, vb=vb_, bvb=bvb_, eb=eb, beb=beb, lk=lk, blk=blk)
            dma("pool", lambda e: e.indirect_dma_start(out=kpg[:, :], out_offset=None, in_=ck[:, :],
                in_offset=bass.IndirectOffsetOnAxis(ap=idxi[:, col:col + 1], axis=0)), reads=[bidxi], writes=[bkpg])
            dma("pool", lambda e: e.indirect_dma_start(out=vtok[:, :], out_offset=None, in_=cv[:, :],
                in_offset=bass.IndirectOffsetOnAxis(ap=idxi[:, col:col + 1], axis=0)), reads=[bidxi], writes=[bvtok])
            op("pool", lambda e: e.tensor_copy(out=vb_[:, :], in_=vtok[:, :]), reads=[bvtok], writes=[bvb_])
            op("dve", lambda e: e.tensor_copy(out=kbf[:, :], in_=kpg[:, :]), reads=[bkpg], writes=[bkbf])
            ptp, bptp = nps()
            ptpv = ptp[:, 0:512].bitcast(BF16)
            for h in range(8):
                op("pe", lambda e: e.transpose(out=ptpv[0:64, h * 128:(h + 1) * 128], in_=kbf[:, h * 64:(h + 1) * 64],
                                               identity=identb), reads=[bkbf] + KC, writes=[bptp])
            op("act", lambda e: e.copy(out=ktp[:, :], in_=ptpv[0:64, :]), reads=[bptp], writes=[bktp])
            pz, bpz = nps()
            for h in range(8):
                mm(pz[:, h * 8:(h + 1) * 8], ktp[:, h * 128:(h + 1) * 128], qT8[:, h, s_ * 8:(s_ + 1) * 8], True, True,
                   [bktp, bqT8], bpz, True)
            op("act", lambda e: e.activation(out=eb[:, 0:64], in_=pz[:, 0:64], func=AF.Exp), reads=[bpz], writes=[beb])
            op("dve", lambda e: e.tensor_tensor(out=eb[:, 0:64].rearrange("p (h t) -> p h t", h=8),
                                                in0=eb[:, 0:64].rearrange("p (h t) -> p h t", h=8), in1=ce3, op=ALU.mult),
               reads=[beb, bcexp], writes=[beb])
            op("act", lambda e: e.activation(out=lk[:, 0:64], in_=eb[:, 0:64], func=AF.Ln, bias=1.0, scale=1.0),
               reads=[beb], writes=[blk])

        def stageB(B):
            s_, j, pgi = B["s"], B["j"], B["i"]
            ktp, bktp, vb_, bvb_, lk, blk = B["ktp"], B["bktp"], B["vb"], B["bvb"], B["lk"], B["blk"]
            eb, beb = eb3w[pgi % 2]
            wT, bwT = wT3[pgi % 3]
            if j == NPAGES - 1:
                op("pool", lambda e: e.tensor_copy(out=Rs[:, :].rearrange("p (h t) -> p h t", h=8), in_=lkv[:, :, s_, :]),
                   reads=[bRb], writes=[bRs])
                op("pool", lambda e: e.tensor_copy(out=Rsb[:, :], in_=Rs[:, :]), reads=[bRs], writes=[bRsb])
            pl_, bpl_ = nps()
            mm(pl_[:, 0:64], C("ntri"), lk[:, 0:64], True, False, [blk] + KC, bpl_, True)
            mm(pl_[:, 0:64], C("nones"), Rsb[:, :], False, False, [bRsb] + KC, bpl_, True)
            for h in range(8):
                mm(pl_[:, h * 8:(h + 1) * 8], ktp[:, h * 128:(h + 1) * 128], qT8[:, h, s_ * 8:(s_ + 1) * 8], False, h == 7,
                   [bktp, bqT8], bpl_, True)
            op("act", lambda e: e.activation(out=eb[:, 0:64], in_=pl_[:, 0:64], func=AF.Exp), reads=[bpl_], writes=[beb])
            op("dve", lambda e: e.tensor_tensor(out=wT[:, 0:64].rearrange("p (h t) -> p h t", h=8),
                                                in0=eb[:, 0:64].rearrange("p (h t) -> p h t", h=8), in1=ce3, op=ALU.mult),
               reads=[beb, bcexp], writes=[bwT])
            for h in range(8):
                hp, hb = h // 2, (h % 2) * 64
                mm(po[hb:hb + 64, hp * 128 + s_ * 8:hp * 128 + (s_ + 1) * 8], vb_[:, h * 64:(h + 1) * 64], wT[:, h * 8:(h + 1) * 8],
                   False, False, [bvb_, bwT], bpo, True)
            if j > 0:
                op("pool", lambda e: e.tensor_tensor(out=Rs[:, :], in0=Rs[:, :], in1=lk[:, 0:64], op=ALU.add),
                   reads=[bRs, blk], writes=[bRs])
                op("pool", lambda e: e.tensor_copy(out=Rsb[:, :], in_=Rs[:, :]), reads=[bRs], writes=[bRsb])

        SKP = 1
        for i in range(len(pages) + SKP):
            if i < len(pages):
                stageA(pages[i])
            if i - SKP >= 0:
                stageB(pages[i - SKP])
        for hp in range(4):
            op("act", lambda e: e.copy(out=mixT[:, 4 + hp, 0:NT], in_=po[:, hp * 128:(hp + 1) * 128]), reads=[bpo], writes=[bmixT])

    for l in range(nlayers):
        P1 = Scope(nc)
        win, bwin = P1.sb("win", [128, 8, INC], BF16)
        wout, bwout = P1.sb("wout", [128, 8, D], BF16)
        w2t, bw2 = P1.sb("w2t", [128, 256], BF16)
        g2t, bg2 = P1.sb("g2t", [128, 256], BF16)
        pcol, bpcol = P1.sb("pcol", [128, 64], F32)
        kgn, bkgn = P1.sb("kgn", [128, 64], F32)
        sbb, bsbb = P1.sb("sbb", [128, 8], F32)
        sbr, bsbr = P1.sb("sbr", [1, 64], F32)
        wsrc = W["w_in"][l].rearrange("(kc p) n -> p kc n", p=128)
        for kc in range(8):
            dma("pool", lambda e, kc=kc: e.dma_start(out=win[:, kc, :], in_=wsrc[:, kc, :]), writes=[bwin])
        wsrc2 = W["w_out"][l].rearrange("(kc p) n -> p kc n", p=128)
        for kc in range(0, 8, 4):
            dma("pool", lambda e, kc=kc: e.dma_start(out=wout[:, kc:kc + 4, :], in_=wsrc2[:, kc:kc + 4, :]), writes=[bwout])
        dma("pool", lambda e: e.dma_start(out=w2t[0:64, :], in_=W["w2"][l]), writes=[bw2])
        dma("pool", lambda e: e.dma_start(out=w2t[64:128, :], in_=W["a2"][l]), writes=[bw2])
        dma("pool", lambda e: e.dma_start(out=g2t[:, :], in_=W["g2"][l]), writes=[bg2])
        PCO = {"g_mix": 0, "mu": 8, "omu": 16, "w0": 24, "a0": 26, "k_k": 28, "k_a": 30, "r_k": 32, "gn_w": 34,
               "gn_b": 36, "conv": 38, "qg": 44, "omka": 46}

        def ldcol(name, off, n):
            src = W[name][l].rearrange("(c p) -> p c", p=128)
            dma("sp", lambda e: e.dma_start(out=pcol[:, off:off + n], in_=src, allow_slow_non_contiguous=True),
                writes=[bpcol])
        ldcol("g_mix", 0, 8); ldcol("mu_shift", 8, 8)
        for nm in ("w0", "a0", "k_k", "k_a", "r_k", "gn_w", "gn_b"):
            ldcol(nm, PCO[nm], 2)
        for j in range(3):
            srcj = W["conv_w"][l, j].rearrange("(c p) -> p c", p=128)
            dma("sp", lambda e, j=j, srcj=srcj: e.dma_start(out=pcol[:, 38 + 2 * j:40 + 2 * j], in_=srcj,
                                                          allow_slow_non_contiguous=True), writes=[bpcol])
        qgsrc = W["q_gain"][l].rearrange("(p o) -> p o", o=1)
        dma("sp", lambda e: e.dma_start(out=pcol[0:64, 44:45], in_=qgsrc), writes=[bpcol])
        dma("sp", lambda e: e.dma_start(out=pcol[64:128, 44:45], in_=qgsrc), writes=[bpcol])
        op("dve", lambda e: e.tensor_scalar(out=pcol[:, 16:24], in0=pcol[:, 8:16], scalar1=-1.0, scalar2=1.0,
                                            op0=ALU.mult, op1=ALU.add), reads=[bpcol], writes=[bpcol])
        op("dve", lambda e: e.tensor_scalar(out=pcol[:, 44:45], in0=pcol[:, 44:45], scalar1=0.125, scalar2=None,
                                            op0=ALU.mult), reads=[bpcol], writes=[bpcol])
        op("dve", lambda e: e.tensor_scalar(out=pcol[:, 46:48], in0=pcol[:, 30:32], scalar1=-1.0, scalar2=1.0,
                                            op0=ALU.mult, op1=ALU.add), reads=[bpcol], writes=[bpcol])
        dma("sp", lambda e: e.dma_start(out=kgn[:, :], in_=W["k_gain"][l:l + 1, :].partition_broadcast(128)), writes=[bkgn])
        dma("sp", lambda e: e.dma_start(out=sbb[:, :], in_=W["sb_bias"][l:l + 1, :].partition_broadcast(128)), writes=[bsbb])
        op("dve", lambda e: e.tensor_copy(out=sbr[0:1, :].rearrange("p (h t) -> p h t", t=8),
                                          in_=sbb[0:1, :].unsqueeze(2).to_broadcast([1, 8, 8])),
           reads=[bsbb], writes=[bsbr])

        def pc(name, c=0, n=1):
            o = PCO[name] + c
            return pcol[:, o:o + n]

        rwc, brwc = P1.sb("rwc", [128, 8], F32)
        cvc, bcvc = P1.sb("cvc", [128, 2, 2], F32)
        op("dve", lambda e: e.memset(rwc[:], 0.0), writes=[brwc])
        op("dve", lambda e: e.memset(cvc[:], 0.0), writes=[bcvc])

        G = Scope(nc)
        xT, bxT = G.sb("xT", [128, 8, 256], F32)
        hT, bhT = G.sb("hT", [128, 8, 256], BF16)
        sq2 = [G.sb("sq%d" % i, [128, 256], BF16) for i in range(2)]
        sq, bsq = [t for t, _ in sq2], [b for _, b in sq2]
        rstd, brstd = G.sb("rstd", [128, 256], F32)
        xtok, bxtok = G.sb("xtok", [128, D], F32)
        rwt2 = [G.sb("rwt%d" % i, [128, 264], F32) for i in range(2)]
        sttT, bsttT = G.sb("sttT", [128, 8, NSS], F32)
        shs, bshs = G.sb("shs", [128, 8, NSS], F32)
        xsh, bxsh = G.sb("xsh", [128, 8, 256], F32)
        cvb, bcvb = G.sb("cvb", [128, 2, 256], F32)
        cvu, bcvu = G.sb("cvu", [128, 2, 320], F32)
        cvt, bcvt = G.sb("cvt", [128, 2, 256], F32)
        cst, bcst = G.sb("cst", [NSS * 2, 256], F32)
        qT, bqT = G.sb("qT", [128, 4, 256], BF16)
        qf, bqf = G.sb("qf", [128, 256], F32)
        mixT, bmixT = G.sb("mixT", [128, 8, 256], BF16)
        ktok, bktok = G.sb("ktok", [128, 512], F32)
        ksq, bksq = G.sb("ksq", [128, 512], F32)
        kss, bkss = G.sb("kss", [128, 8], F32)
        kbf, bkbf = G.sb("kbf", [128, 512], BF16)
        ktb, bktb = G.sb("ktb", [128, 512], BF16)
        vtok, bvtok = G.sb("vtok", [128, 512], F32)
        vbf, bvbf = G.sb("vbf", [128, 512], BF16)
        KTs, bKTs = G.sb("KTs", [128, 512], BF16)
        Vs, bVs = G.sb("Vs", [128, 512], BF16)
        x1, bx1 = G.sb("x1", [128, 256], F32)
        eb3 = [G.sb("ebuf%d" % i, [128, 256], F32) for i in range(2)]
        lk3 = [G.sb("lk%d" % i, [128, 256], BF16) for i in range(3)]
        wT3 = [G.sb("wT%d" % i, [128, 256], BF16) for i in range(3)]
        Rf, bRf = G.sb("Rf", [128, 8, 256], F32)
        Rb, bRb = G.sb("Rb", [128, 8, 256], BF16)
        bRfh = [Buf("Rfh%d" % i) for i in range(8)]
        bRbh = [Buf("Rbh%d" % i) for i in range(8)]
        ktl4 = [G.sb("ktl%d" % i, [128, 512], BF16) for i in range(3)]
        vl4 = [G.sb("vl%d" % i, [128, 512], BF16) for i in range(3)]
        rot = {"e": 0, "l": 0, "w": 0, "k": 0}
        rw_state = {"i": 0}
        pti, bpti = G.sb("pti", [128, 256], I32)
        idxi, bidxi = G.sb("idxi", [128, 256], I32)
        cexp, bcexp = G.sb("cexp", [128, 8], F32)
        qT8, bqT8 = G.sb("qT8", [64, 8, 128], BF16)
        KTs8, bKTs8 = G.sb("KTs8", [64, 8, 128], BF16)
        ktp2 = [G.sb("ktp%d" % i, [64, 1024], BF16) for i in range(2)]
        Rs, bRs = G.sb("Rs", [128, 64], F32)
        eb3w = [G.sb("ebw%d" % i, [128, 64], F32) for i in range(2)]
        Rsb, bRsb = G.sb("Rsb", [128, 64], BF16)
        pY1, bpY1 = PS[6]
        pY2, bpY2 = PS[7]
        pY1o, bpY1o = PS[5]
        dwa, bdwa = G.sb("dwa", [128, 256], BF16)
        sdg, bsdg = G.sb("sdg", [128, 256], BF16)
        F1, bF1 = G.sb("F1", [128, 2, 256], F32); F2, bF2 = G.sb("F2", [128, 2, 256], F32)
        F4, bF4 = G.sb("F4", [128, 2, 256], F32); F5, bF5 = G.sb("F5", [128, 2, 256], F32)
        F6, bF6 = G.sb("F6", [128, 2, 256], F32); F7, bF7 = G.sb("F7", [128, 2, 256], F32)
        F8, bF8 = G.sb("F8", [128, 2, 256], F32)
        AR, bAR = G.sb("AR", [128, 2, 512], BF16)
        btT, bbtT = G.sb("btT", [128, 2, 256], BF16); ktT, bktT = G.sb("ktT", [128, 2, 256], BF16)
        bhX, bbhX = G.sb("bhX", [128, 2, 256], BF16); khT, bkhT = G.sb("khT", [128, 2, 256], BF16)
        vbT, bvbT = G.sb("vbT", [128, 2, 256], BF16)
        dP, bdP = G.sb("dP", [128, 2, 16, 64], BF16)
        tok, btok = G.sb("tok", [64, 1024], BF16)
        A1, bA1 = G.sb("A1", [64, 4, 2, 64], BF16); A2, bA2 = G.sb("A2", [64, 4, 2, 64], BF16)
        _n = [G.sb("Nm%d" % i, [64, 4, 64], BF16) for i in range(2)]; Nm = [t for t, _ in _n]; bNm = [b for _, b in _n]
        _n = [G.sb("NmT%d" % i, [64, 4, 64], BF16) for i in range(2)]; NmT = [t for t, _ in _n]; bNmT = [b for _, b in _n]
        _n = [G.sb("TT%d" % i, [64, 4, 64], BF16) for i in range(2)]; TT = [t for t, _ in _n]; bTT = [b for _, b in _n]
        Qm, bQm = G.sb("Qm", [64, 4, 64], BF16)
        XW, bXW = G.sb("XW", [64, 4, 2, 64], BF16)
        AV, bAV = G.sb("AV", [64, 4, 128], BF16)
        AhT, bAhT = G.sb("AhT", [128, 2, 64], BF16)
        MT, bMT = G.sb("MT", [128, 2, 64], BF16)
        NNs, bNNs = G.sb("NNs", [128, 2, 64], F32)
        Ub, bUb = G.sb("Ub", [64, 4, 64], BF16)
        _n = [G.sb("STa%d" % i, [128, 2, 64], BF16) for i in range(2)]; STa = [t for t, _ in _n]; bSTa = [b for _, b in _n]
        STf, bSTf = G.sb("STf", [128, 2, 64], F32)
        swo, bswo = G.sb("swo", [64, 256], F32)
        s0t, bs0t = G.sb("s0t", [128, 128], F32)
        s0b, bs0b = G.sb("s0b", [128, 128], BF16)

        for gi, (t0, NT, nseq, T) in enumerate(groups):
            sample = nseq > 1
            ntile = NT // 128
            if l == 0:
                src = xs if sample else xp
                for tt in range(ntile):
                    r0 = tt * 128 if sample else t0 + tt * 128
                    dma("sp", lambda e, r0=r0, src=src: e.dma_start(out=xtok[:], in_=src[r0:r0 + 128, :]), writes=[bxtok])
                    for c4 in range(2):
                        pt, bpt = nps()
                        for c in range(4):
                            cc = c4 * 4 + c
                            op("pe", lambda e, c=c, cc=cc, pt=pt: e.transpose(out=pt[:, c * 128:(c + 1) * 128],
                               in_=xtok[:, cc * 128:(cc + 1) * 128], identity=identf),
                               reads=[bxtok] + KC, writes=[bpt], inc=(c == 3))
                        op("act", lambda e, pt=pt, c4=c4, tt=tt: e.copy(
                            out=xT[:, c4 * 4:c4 * 4 + 4, tt * 128:(tt + 1) * 128],
                            in_=pt[:, :].rearrange("p (c t) -> p c t", c=4)), reads=[bpt], writes=[bxT])
            else:
                for c in range(8):
                    dma("sp", lambda e, c=c: e.dma_start(out=xT[:, c, 0:NT], in_=xls[c, :, t0:t0 + NT]),
                        reads=[bxls[gi]], writes=[bxT])

            rmsnorm(NT, xT, bxT, lambda c: pc("g_mix", c), bpcol, hT, bhT, sq, bsq, rstd, brstd)

            def proj_chunk(cc):
                pt, bpt = nps()
                for kc in range(8):
                    mm(pt[:, 0:NT], win[:, kc, cc * 128:(cc + 1) * 128], hT[:, kc, 0:NT], kc == 0, kc == 7,
                       [bwin, bhT], bpt, kc == 7)
                return pt, bpt

            if sample:
                dma("sp", lambda e: e.dma_start(out=xtok[0:NSS, :], in_=s_shift[l]), writes=[bxtok])
                pt, bpt = nps()
                for c in range(8):
                    op("pe", lambda e, c=c, pt=pt: e.transpose(out=pt[:, c * 16:(c + 1) * 16],
                       in_=xtok[0:NSS, c * 128:(c + 1) * 128], identity=identf[0:NSS, 0:NSS]),
                       reads=[bxtok] + KC, writes=[bpt], inc=(c == 7))
                op("act", lambda e, pt=pt: e.copy(out=sttT[:, :, :], in_=pt[:, 0:128].rearrange("p (c s) -> p c s", c=8)),
                   reads=[bpt], writes=[bsttT])
            xshv = xsh[:, :, 0:NT].rearrange("p c (s t) -> p c s t", s=nseq)
            for cc in range(8):
                rwt, brwt = rwt2[cc % 2]
                rwv = rwt[:, 0:nseq * (T + 1)].rearrange("p (s t) -> p s t", s=nseq)
                pt, bpt = proj_chunk(cc)
                op("act", lambda e, pt=pt, rwv=rwv: e.copy(
                    out=rwv[:, :, 1:T + 1], in_=pt[:, 0:NT].rearrange("p (s t) -> p s t", s=nseq)),
                   reads=[bpt], writes=[brwt])
                if sample:
                    op("act", lambda e, rwv=rwv, cc=cc: e.copy(out=rwv[:, :, 0], in_=sttT[:, cc, :]),
                       reads=[bsttT], writes=[brwt])
                    op("pool", lambda e, rwv=rwv, cc=cc: e.tensor_copy(out=shs[:, cc, :], in_=rwv[:, :, T]),
                       reads=[brwt], writes=[bshs])
                else:
                    op("act", lambda e, rwv=rwv, cc=cc: e.copy(out=rwv[:, 0, 0:1], in_=rwc[:, cc:cc + 1]),
                       reads=[brwc], writes=[brwt])
                    op("pool", lambda e, rwv=rwv, cc=cc: e.tensor_copy(out=rwc[:, cc:cc + 1], in_=rwv[:, 0, T:T + 1]),
                       reads=[brwt], writes=[brwc])
                eng = "dve"
                op("pool", lambda e, cc=cc, rwv=rwv: e.tensor_scalar(out=xshv[:, cc], in0=rwv[:, :, 0:T], scalar1=pc("mu", cc),
                                                                 scalar2=None, op0=ALU.mult),
                   reads=[brwt, bpcol], writes=[bxsh])
                op(eng, lambda e, cc=cc, rwv=rwv: e.scalar_tensor_tensor(
                    out=xshv[:, cc], in0=rwv[:, :, 1:T + 1], scalar=pc("omu", cc), in1=xshv[:, cc],
                    op0=ALU.mult, op1=ALU.add), reads=[brwt, bxsh, bpcol], writes=[bxsh])
            if sample:
                for c in range(8):
                    dma("sp", lambda e, c=c: e.dma_start(
                        out=shift_s[l, :, c * 128:(c + 1) * 128].rearrange("s p -> p s"), in_=shs[:, c, :],
                        allow_slow_non_contiguous=True), reads=[bshs])
            elif gi == NG - 1:
                dma("sp", lambda e: e.dma_start(out=shift_p[l].rearrange("(c p) -> p c", p=128),
                                                in_=rwc[:, :], allow_slow_non_contiguous=True), reads=[brwc])

            cuv = cvu[:, :, 0:nseq * (T + 2)].rearrange("p c (s t) -> p c s t", s=nseq)
            if sample:
                dma("sp", lambda e: e.dma_start(out=cst[:, :], in_=s_conv[l]), writes=[bcst])
                pt, bpt = nps()
                for c in range(2):
                    op("pe", lambda e, c=c, pt=pt: e.transpose(out=pt[:, c * 32:(c + 1) * 32],
                       in_=cst[0:32, c * 128:(c + 1) * 128], identity=identf[0:32, 0:32]),
                       reads=[bcst] + KC, writes=[bpt], inc=(c == 1))
                op("act", lambda e, pt=pt: e.copy(out=cuv[:, :, :, 0:2],
                                                  in_=pt[:, 0:64].rearrange("p (c s j) -> p c s j", c=2, j=2)),
                   reads=[bpt], writes=[bcvu])
            else:
                op("act", lambda e: e.copy(out=cuv[:, :, 0, 0:2], in_=cvc[:, :, :]), reads=[bcvc], writes=[bcvu])
            for c in range(2):
                pt, bpt = proj_chunk(8 + c)
                op("act", lambda e, pt=pt, c=c: e.copy(out=cvb[:, c, 0:NT], in_=pt[:, 0:NT]), reads=[bpt], writes=[bcvb])
            for c in range(2):
                pt, bpt = proj_chunk(10 + c)
                op("act", lambda e, pt=pt, c=c: e.copy(out=cvt[:, c, 0:NT], in_=pt[:, 0:NT]), reads=[bpt], writes=[bcvt])
                pt2, bpt2 = proj_chunk(12 + c)
                op("dve", lambda e, pt2=pt2, c=c: e.tensor_tensor(
                    out=cuv[:, c, :, 2:T + 2], in0=cvt[:, c, 0:NT].rearrange("p (s t) -> p s t", s=nseq),
                    in1=pt2[:, 0:NT].rearrange("p (s t) -> p s t", s=nseq), op=ALU.mult),
                   reads=[bpt2, bcvt], writes=[bcvu])
            cvtv = cvt[:, :, 0:NT].rearrange("p c (s t) -> p c s t", s=nseq)
            for c in range(2):
                eng = "dve"
                op(eng, lambda e, c=c: e.tensor_scalar(out=cvtv[:, c], in0=cuv[:, c, :, 0:T], scalar1=pc("conv", 0 + c),
                                                       scalar2=None, op0=ALU.mult), reads=[bcvu, bpcol], writes=[bcvt])
                for j in (1, 2):
                    op(eng, lambda e, c=c, j=j: e.scalar_tensor_tensor(
                        out=cvtv[:, c], in0=cuv[:, c, :, j:j + T], scalar=pc("conv", 2 * j + c), in1=cvtv[:, c],
                        op0=ALU.mult, op1=ALU.add), reads=[bcvu, bcvt, bpcol], writes=[bcvt])
                op(eng, lambda e, c=c: e.tensor_tensor(out=mixT[:, 2 + c, 0:NT], in0=cvt[:, c, 0:NT],
                                                       in1=cvb[:, c, 0:NT], op=ALU.mult),
                   reads=[bcvt, bcvb], writes=[bmixT])
            if sample:
                for c in range(2):
                    for j in range(2):
                        dma("sp", lambda e, c=c, j=j: e.dma_start(
                            out=conv_s[l].rearrange("(s j) f -> j f s", j=2)[j, c * 128:(c + 1) * 128, :],
                            in_=cuv[:, c, :, T + j], allow_slow_non_contiguous=True), reads=[bcvu])
            else:
                op("pool", lambda e: e.tensor_copy(out=cvc[:, :, :], in_=cuv[:, :, 0, T:T + 2]), reads=[bcvu], writes=[bcvc])
                if gi == NG - 1:
                    for j in range(2):
                        dma("sp", lambda e, j=j: e.dma_start(out=conv_p[l, j].rearrange("(c p) -> p c", p=128),
                                                             in_=cvc[:, :, j], allow_slow_non_contiguous=True),
                            reads=[bcvc])

            for hp in range(4):
                pt, bpt = proj_chunk(14 + hp)
                op("act", lambda e, pt=pt: e.copy(out=qf[:, 0:NT], in_=pt[:, 0:NT]), reads=[bpt], writes=[bqf])
                op("act", lambda e, pt=pt: e.activation(out=sq[0][:, 0:NT], in_=pt[:, 0:NT], func=AF.Square),
                   reads=[bpt], writes=[bsq[0]])
                pr, bpr = nps()
                mm(pr[:, 0:NT], C("blk64"), sq[0][:, 0:NT], True, True, [bsq[0]] + KC, bpr, True)
                op("dve", lambda e, pr=pr: e.tensor_scalar(out=rstd[:, 0:NT], in0=pr[:, 0:NT], scalar1=1.0 / 64,
                                                           scalar2=RMS_EPS, op0=ALU.mult, op1=ALU.add),
                   reads=[bpr], writes=[brstd])
                rsqrt_inplace(rstd[:, 0:NT], brstd)
                op("dve", lambda e, hp=hp: e.scalar_tensor_tensor(out=qT[:, hp, 0:NT], in0=qf[:, 0:NT], scalar=pc("qg"),
                                                                  in1=rstd[:, 0:NT], op0=ALU.mult, op1=ALU.mult),
                   reads=[bqf, brstd, bpcol], writes=[bqT])

            for tt in range(ntile):
                tok0 = t0 + tt * 128
                kbi = tok0 // 128
                for which in range(2):
                    pt, bpt = nps()
                    cbase = 2304 if which == 0 else 2816
                    for kc in range(8):
                        mm(pt[:, :], hT[:, kc, tt * 128:(tt + 1) * 128], win[:, kc, cbase:cbase + 512], kc == 0, kc == 7,
                           [bwin, bhT], bpt, kc == 7)
                    if which == 0:
                        op("act", lambda e, pt=pt: e.copy(out=ktok[:, :], in_=pt[:, :]), reads=[bpt], writes=[bktok])
                        op("pool", lambda e: e.tensor_tensor(out=ksq[:, :], in0=ktok[:, :], in1=ktok[:, :], op=ALU.mult),
                           reads=[bktok], writes=[bksq])
                        op("dve", lambda e: e.tensor_reduce(out=kss[:, :], in_=ksq[:, :].rearrange("p (h d) -> p h d", d=64),
                                                            axis=AX.X, op=ALU.add), reads=[bksq], writes=[bkss])
                        op("dve", lambda e: e.tensor_scalar(out=kss[:, :], in0=kss[:, :], scalar1=1.0 / 64, scalar2=RMS_EPS,
                                                            op0=ALU.mult, op1=ALU.add), reads=[bkss], writes=[bkss])
                        rsqrt_inplace(kss[:, :], bkss)
                        kv3 = ktok[:, :].rearrange("p (h d) -> p h d", d=64)
                        op("dve", lambda e, kv3=kv3: e.tensor_tensor(out=kv3, in0=kv3,
                                                                     in1=kss[:, :].unsqueeze(2).to_broadcast([128, 8, 64]),
                                                                     op=ALU.mult), reads=[bktok, bkss], writes=[bktok])
                        op("dve", lambda e, kv3=kv3: e.tensor_tensor(out=kv3, in0=kv3,
                                                                     in1=kgn[:, :].unsqueeze(1).to_broadcast([128, 8, 64]),
                                                                     op=ALU.mult), reads=[bktok, bkgn], writes=[bktok])
                        dst = k_s[l, :, :] if sample else k_p[l, tok0:tok0 + 128, :]
                        dma("sp", lambda e, dst=dst: e.dma_start(out=dst, in_=ktok[:, :]), reads=[bktok])
                        op("pool", lambda e: e.tensor_copy(out=kbf[:, :], in_=ktok[:, :]), reads=[bktok], writes=[bkbf])
                        ptb, bptb = nps()
                        ptbv = ptb[:, 0:256].bitcast(BF16)
                        for hp in range(4):
                            op("pe", lambda e, hp=hp, ptbv=ptbv: e.transpose(out=ptbv[:, hp * 128:(hp + 1) * 128],
                               in_=kbf[:, hp * 128:(hp + 1) * 128], identity=identb),
                               reads=[bkbf] + KC, writes=[bptb], inc=(hp == 3))
                        if sample:
                            op("act", lambda e, ptbv=ptbv: e.copy(out=KTs[:, :], in_=ptbv), reads=[bptb], writes=[bKTs])
                        else:
                            op("act", lambda e, ptbv=ptbv: e.copy(out=ktb[:, :], in_=ptbv), reads=[bptb], writes=[bktb])
                            dma("sp", lambda e, kbi=kbi: e.dma_start(out=ktscr[kbi], in_=ktb[:, :]),
                                reads=[bktb], writes=[bkts[kbi]])
                    else:
                        op("act", lambda e, pt=pt: e.copy(out=vtok[:, :], in_=pt[:, :]), reads=[bpt], writes=[bvtok])
                        dst = v_s[l, :, :] if sample else v_p[l, tok0:tok0 + 128, :]
                        dma("sp", lambda e, dst=dst: e.dma_start(out=dst, in_=vtok[:, :]), reads=[bvtok])
                        if sample:
                            op("pool", lambda e: e.tensor_copy(out=Vs[:, :], in_=vtok[:, :]), reads=[bvtok], writes=[bVs])
                        else:
                            op("pool", lambda e: e.tensor_copy(out=vbf[:, :], in_=vtok[:, :]), reads=[bvtok], writes=[bvbf])
                            dma("sp", lambda e, kbi=kbi: e.dma_start(out=vscr[kbi], in_=vbf[:, :]),
                                reads=[bvbf], writes=[bvss[kbi]])

            psO = [PS[6], PS[7]]
            if not sample:
                g = gi
                for (po, bpo) in psO:
                    mm(po[:, :], C("zrow", rows=slice(0, 1), cols=slice(0, 128)), C("zrow", rows=slice(0, 1)), True, True,
                       KC, bpo, True)
                op("pool", lambda e: e.memset(Rf[:], 0.0), writes=[bRf] + bRfh)
                op("pool", lambda e: e.memset(Rb[:], 0.0), writes=[bRb] + bRbh)
                kbmax = 2 * g + 1
                blks = []
                for kb in range(kbmax, -1, -1):
                    for h in range(8):
                        blks.append({"kb": kb, "h": h})

                def stage1(B):
                    kb, h = B["kb"], B["h"]
                    j = kb - 2 * g
                    cs = max(0, j) * 128
                    if h == 0:
                        ktl, bktl = ktl4[rot["k"] % 3]
                        vl, bvl = vl4[rot["k"] % 3]
                        rot["k"] += 1
                        dma("sp", lambda e: e.dma_start(out=ktl[:, :], in_=ktscr[kb]), reads=[bkts[kb]], writes=[bktl])
                        dma("sp", lambda e: e.dma_start(out=vl[:, :], in_=vscr[kb]), reads=[bvss[kb]], writes=[bvl])
                        rot["cur"] = (ktl, bktl, vl, bvl)
                    ktl, bktl, vl, bvl = rot["cur"]
                    hp, hb = h // 2, (h % 2) * 64
                    eb, beb = eb3[rot["e"] % 2]; rot["e"] += 1
                    lk, blk = lk3[rot["l"] % 3]; rot["l"] += 1
                    kop = ktl[hb:hb + 64, hp * 128:(hp + 1) * 128]
                    qop = qT[hb:hb + 64, hp, cs:NT]
                    B.update(j=j, cs=cs, ktl=ktl, bktl=bktl, vl=vl, bvl=bvl, lk=lk, blk=blk, kop=kop, qop=qop, hp=hp, hb=hb)
                    pa, bpa = nps()
                    mm(pa[:, cs:NT], kop, qop, True, True, [bktl, bqT], bpa, True)
                    op("act", lambda e: e.activation(out=eb[:, cs:NT], in_=pa[:, cs:NT], func=AF.Exp, bias=sbb[:, h:h + 1], scale=1.0),
                       reads=[bpa, bsbb], writes=[beb])
                    op("act", lambda e: e.activation(out=lk[:, cs:NT], in_=eb[:, cs:NT], func=AF.Ln, bias=1.0, scale=1.0),
                       reads=[beb], writes=[blk])
                    if j >= 0:
                        op("dve", lambda e: e.tensor_tensor(out=lk[:, cs:cs + 128], in0=lk[:, cs:cs + 128], in1=C("mlt"), op=ALU.mult),
                           reads=[blk] + KC, writes=[blk])

                def stage2(B):
                    kb, h, j, cs, hp, hb = B["kb"], B["h"], B["j"], B["cs"], B["hp"], B["hb"]
                    lk, blk, kop, qop = B["lk"], B["blk"], B["kop"], B["qop"]
                    wT, bwT = wT3[rot["w"] % 3]; rot["w"] += 1
                    pb, bpb = nps()
                    mm(pb[:, cs:NT], C("ntri"), lk[:, cs:NT], True, False, [blk] + KC, bpb, False)
                    if kb < kbmax:
                        mm(pb[:, cs:NT], C("nones"), Rb[:, h, cs:NT], False, False, [bRbh[h]] + KC, bpb, False)
                    mm(pb[:, cs:NT], kop, qop, False, True, [B["bktl"], bqT], bpb, True)
                    op("act", lambda e: e.activation(out=wT[:, cs:NT], in_=pb[:, cs:NT], func=AF.Exp, bias=sbb[:, h:h + 1], scale=1.0),
                       reads=[bpb, bsbb], writes=[bwT])
                    if j >= 0:
                        op("dve", lambda e: e.tensor_tensor(out=wT[:, cs:cs + 128], in0=wT[:, cs:cs + 128], in1=C("mlt"), op=ALU.mult),
                           reads=[bwT] + KC, writes=[bwT])
                    po, bpo = psO[hp // 2]
                    co = (hp % 2) * 256
                    mm(po[hb:hb + 64, co + cs:co + NT], B["vl"][:, h * 64:(h + 1) * 64], wT[:, cs:NT], False, False,
                       [B["bvl"], bwT], bpo, True)
                    if kb > 0:
                        op("pool", lambda e: e.tensor_tensor(out=Rf[:, h, cs:NT], in0=Rf[:, h, cs:NT], in1=lk[:, cs:NT], op=ALU.add),
                           reads=[bRfh[h], blk], writes=[bRfh[h]])
                        op("pool", lambda e: e.tensor_copy(out=Rb[:, h, cs:NT], in_=Rf[:, h, cs:NT]),
                           reads=[bRfh[h]], writes=[bRbh[h]])

                SK = 2
                for i in range(len(blks) + SK):
                    if i < len(blks):
                        stage1(blks[i])
                    if i - SK >= 0:
                        stage2(blks[i - SK])
                op("pool", lambda e: e.memset(Rs[:, 0:1], 0.0), reads=bRfh + bRbh, writes=[bRf, bRb, bRs])
                for hp in range(4):
                    po, bpo = psO[hp // 2]
                    co = (hp % 2) * 256
                    op("act", lambda e, po=po, co=co, hp=hp: e.copy(out=mixT[:, 4 + hp, 0:NT], in_=po[:, co:co + NT]),
                       reads=[bpo], writes=[bmixT])
            else:
                if USE_CACHE:
                    sample_attn(l)
                else:
                    sample_attention(l, locals())

            rwkv_mix(l, gi, t0, NT, nseq, T, sample)

            for j in range(8):
                pt, bpt = nps()
                for kc in range(8):
                    mm(pt[:, 0:NT], wout[:, kc, j * 128:(j + 1) * 128], mixT[:, kc, 0:NT], kc == 0, kc == 7,
                       [bwout, bmixT], bpt, kc == 7)
                op("dve", lambda e, pt=pt, j=j: e.tensor_tensor(out=x1[:, 0:NT], in0=pt[:, 0:NT], in1=xT[:, j, 0:NT],
                                                                op=ALU.add), reads=[bpt, bxT], writes=[bx1])
                dma("sp", lambda e, j=j: e.dma_start(out=x1s[j, :, t0:t0 + NT], in_=x1[:, 0:NT]), reads=[bx1],
                    writes=[bx1s[gi]])
        kbd.barrier()
        G.close()
        P1.close()
        if not do_phase2:
            continue
        P2 = Scope(nc)
        wup, bwup = P2.sb("wup", [128, 8, 4 * D], BF16)
        wdn, bwdn = P2.sb("wdn", [128, 32, D], BF16)
        wpg, bwpg = P2.sb("wpg", [128, 8, D], BF16)
        wpp, bwpp = P2.sb("wpp", [128, 2, D], BF16)
        pc2, bpc2 = P2.sb("pc2", [128, 16], F32)
        s_up = W["w_up"][l].rearrange("(kc p) n -> p kc n", p=128)
        for kc in range(8):
            dma("pool", lambda e, kc=kc: e.dma_start(out=wup[:, kc, :], in_=s_up[:, kc, :]), writes=[bwup])
        s_dn = W["w_down"][l].rearrange("(kc p) n -> p kc n", p=128)
        for kc in range(0, 32, 4):
            dma("pool", lambda e, kc=kc: e.dma_start(out=wdn[:, kc:kc + 4, :], in_=s_dn[:, kc:kc + 4, :]), writes=[bwdn])
        s_pg = W["w_ple_gate"][l].rearrange("(kc p) n -> p kc n", p=128)
        for kc in range(0, 8, 4):
            dma("pool", lambda e, kc=kc: e.dma_start(out=wpg[:, kc:kc + 4, :], in_=s_pg[:, kc:kc + 4, :]), writes=[bwpg])
        s_pp = W["w_ple_proj"][l].rearrange("(kc p) n -> p kc n", p=128)
        dma("pool", lambda e: e.dma_start(out=wpp[:, :, :], in_=s_pp), writes=[bwpp])
        for nm, off in (("g_mlp", 0), ("g_ple", 8)):
            srcc = W[nm][l].rearrange("(c p) -> p c", p=128)
            dma("sp", lambda e, srcc=srcc, off=off: e.dma_start(out=pc2[:, off:off + 8], in_=srcc,
                                                               allow_slow_non_contiguous=True), writes=[bpc2])
        G2 = Scope(nc)
        xT, bxT = G2.sb("xT2", [128, 8, 256], F32)
        hT, bhT = G2.sb("hT2", [128, 8, 256], BF16)
        sq2 = [G2.sb("sq2%d" % i, [128, 256], BF16) for i in range(2)]
        sq, bsq = [t for t, _ in sq2], [b for _, b in sq2]
        rstd, brstd = G2.sb("rstd2", [128, 256], F32)
        actT, bactT = G2.sb("actT", [128, 32, 256], BF16)
        rl2 = [G2.sb("rl%d" % i, [128, 256], BF16) for i in range(2)]
        ptok, bptok = G2.sb("ptok", [128, 256], F32)
        pT, bpT = G2.sb("pT", [128, 2, 256], BF16)
        gt2 = [G2.sb("gt%d" % i, [128, 256], F32) for i in range(2)]
        ytok, bytok = G2.sb("ytok", [128, D], F32)
        for gi, (t0, NT, nseq, T) in enumerate(groups):
            sample = nseq > 1
            ntile = NT // 128
            for c in range(8):
                dma("sp", lambda e, c=c: e.dma_start(out=xT[:, c, 0:NT], in_=x1s[c, :, t0:t0 + NT]),
                    reads=[bx1s[gi]], writes=[bxT])
            rmsnorm(NT, xT, bxT, lambda c: pc2[:, c:c + 1], bpc2, hT, bhT, sq, bsq, rstd, brstd)
            for fc in range(32):
                pt, bpt = nps((0, 1, 2, 3, 4, 5, 6, 7))
                for kc in range(8):
                    mm(pt[:, 0:NT], wup[:, kc, fc * 128:(fc + 1) * 128], hT[:, kc, 0:NT], kc == 0, kc == 7,
                       [bwup, bhT], bpt, kc == 7)
                rl, brl = rl2[fc % 2]
                op("act", lambda e, pt=pt, rl=rl: e.activation(out=rl[:, 0:NT], in_=pt[:, 0:NT], func=AF.Relu),
                   reads=[bpt], writes=[brl])
                op("pool" if fc % 2 == 0 else "dve", lambda e, rl=rl, fc=fc: e.tensor_tensor(
                    out=actT[:, fc, 0:NT], in0=rl[:, 0:NT], in1=rl[:, 0:NT], op=ALU.mult), reads=[brl], writes=[bactT])
            for j in range(8):
                pt, bpt = nps((0, 1, 2, 3, 4, 5, 6, 7))
                for fc in range(32):
                    mm(pt[:, 0:NT], wdn[:, fc, j * 128:(j + 1) * 128], actT[:, fc, 0:NT], fc == 0, fc == 31,
                       [bwdn, bactT], bpt, fc == 31)
                op("dve", lambda e, pt=pt, j=j: e.tensor_tensor(out=xT[:, j, 0:NT], in0=pt[:, 0:NT], in1=xT[:, j, 0:NT],
                                                                op=ALU.add), reads=[bpt, bxT], writes=[bxT])
            rmsnorm(NT, xT, bxT, lambda c: pc2[:, 8 + c:9 + c], bpc2, hT, bhT, sq, bsq, rstd, brstd)
            psrc = psm[l] if sample else pp[l]
            for tt in range(ntile):
                r0 = tt * 128 if sample else t0 + tt * 128
                dma("sp", lambda e, r0=r0, psrc=psrc: e.dma_start(out=ptok[:, :], in_=psrc[r0:r0 + 128, :]), writes=[bptok])
                pt, bpt = nps((0, 1, 2, 3, 4, 5, 6, 7))
                for c in range(2):
                    op("pe", lambda e, c=c, pt=pt: e.transpose(out=pt[:, c * 128:(c + 1) * 128],
                       in_=ptok[:, c * 128:(c + 1) * 128], identity=identf), reads=[bptok] + KC, writes=[bpt], inc=(c == 1))
                op("act", lambda e, pt=pt, tt=tt: e.copy(out=pT[:, :, tt * 128:(tt + 1) * 128],
                                                         in_=pt[:, 0:256].rearrange("p (c t) -> p c t", c=2)),
                   reads=[bpt], writes=[bpT])
            for j in range(8):
                pg, bpg = nps((0, 1, 2, 3, 4, 5, 6, 7))
                for kc in range(8):
                    mm(pg[:, 0:NT], wpg[:, kc, j * 128:(j + 1) * 128], hT[:, kc, 0:NT], kc == 0, kc == 7,
                       [bwpg, bhT], bpg, kc == 7)
                gt, bgt = gt2[j % 2]
                op("act", lambda e, pg=pg, gt=gt: e.activation(out=gt[:, 0:NT], in_=pg[:, 0:NT], func=AF.Sigmoid),
                   reads=[bpg], writes=[bgt])
                pq, bpq = nps((0, 1, 2, 3, 4, 5, 6, 7))
                for c in range(2):
                    mm(pq[:, 0:NT], wpp[:, c, j * 128:(j + 1) * 128], pT[:, c, 0:NT], c == 0, c == 1,
                       [bwpp, bpT], bpq, c == 1)
                op("dve", lambda e, pq=pq, gt=gt: e.tensor_tensor(out=gt[:, 0:NT], in0=gt[:, 0:NT], in1=pq[:, 0:NT],
                                                                  op=ALU.mult), reads=[bgt, bpq], writes=[bgt])
                op("pool", lambda e, gt=gt, j=j: e.tensor_tensor(out=xT[:, j, 0:NT], in0=xT[:, j, 0:NT], in1=gt[:, 0:NT],
                                                                 op=ALU.add), reads=[bgt, bxT], writes=[bxT])
            if l < nlayers - 1:
                for c in range(8):
                    dma("sp", lambda e, c=c: e.dma_start(out=xls[c, :, t0:t0 + NT], in_=xT[:, c, 0:NT]),
                        reads=[bxT], writes=[bxls[gi]])
            else:
                for tt in range(ntile):
                    for c4 in range(2):
                        pt, bpt = nps((0, 1, 2, 3, 4, 5, 6, 7))
                        for c in range(4):
                            cc = c4 * 4 + c
                            op("pe", lambda e, c=c, cc=cc, pt=pt, tt=tt: e.transpose(
                                out=pt[:, c * 128:(c + 1) * 128], in_=xT[:, cc, tt * 128:(tt + 1) * 128], identity=identf),
                               reads=[bxT] + KC, writes=[bpt], inc=(c == 3))
                        op("act", lambda e, pt=pt, c4=c4: e.copy(out=ytok[:, c4 * 512:(c4 + 1) * 512], in_=pt[:, :]),
                           reads=[bpt], writes=[bytok])
                    dst = y_s[tt * 128:(tt + 1) * 128, :] if sample else y_p[t0 + tt * 128:t0 + (tt + 1) * 128, :]
                    dma("sp", lambda e, dst=dst: e.dma_start(out=dst, in_=ytok[:, :]), reads=[bytok])
        kbd.barrier()
        G2.close()
        P2.close()
    kbd.finish()
    glob.close()
    return nc


def sample_attention(l, L):
    kb = L["kbd"]; mixT = L["mixT"]; bmixT = L["bmixT"]
    kb.op("pool", lambda e: e.memset(mixT[:, 4:8, 0:128], 0.0), writes=[bmixT])


def rwkv(l, gi, t0, NT, nseq, T, sample, L):
    kb = L["kbd"]; mixT = L["mixT"]; bmixT = L["bmixT"]
    kb.op("pool", lambda e: e.memset(mixT[:, 0:2, 0:NT], 0.0), writes=[bmixT])


def kernel(**inp):
    f32 = np.float32
    nphys = int(inp["cache_k"].shape[1])
    nc = build_program(nphys=nphys)
    ck = np.ascontiguousarray(inp["cache_k"], dtype=f32).reshape(2 * nphys * 128, 512)
    cv = np.ascontiguousarray(inp["cache_v"], dtype=f32).reshape(2 * nphys * 128, 512)
    wnames = ["g_mix", "w_in", "mu_shift", "w0", "w2", "a0", "a2", "g2", "k_k", "k_a", "r_k", "gn_w", "gn_b", "conv_w",
              "q_gain", "k_gain", "sb_bias", "w_out", "g_mlp", "w_up", "w_down", "g_ple", "w_ple_gate", "w_ple_proj"]
    wd = {n: np.ascontiguousarray(inp[n], dtype=f32) for n in wnames}
    wd["r_k"] = wd["r_k"].reshape(2, 256)
    in_maps = []
    for c in range(NCORES):
        b = c % 4
        s0 = c * NSS
        m = dict(wd)
        m["xp"] = np.ascontiguousarray(inp["x_prompt"][b])
        m["xs"] = np.ascontiguousarray(inp["x_sample"][s0:s0 + NSS]).reshape(128, D)
        if USE_CACHE:
            m["cache_k"] = ck
            m["cache_v"] = cv
        m["s_wkv"] = np.ascontiguousarray(inp["state_wkv"][:, s0:s0 + NSS]).reshape(2, NSS * 4 * 64, 64)
        m["s_shift"] = np.ascontiguousarray(inp["state_shift"][:, s0:s0 + NSS])
        m["s_conv"] = np.ascontiguousarray(inp["state_conv"][:, s0:s0 + NSS]).reshape(2, NSS * 2, 256)
        m["ptab"] = np.ascontiguousarray(inp["page_table"][s0:s0 + NSS]).reshape(1, NSS * NPAGES).astype(np.int32)
        m["pp"] = np.ascontiguousarray(inp["p_prompt"][:, b])
        m["ps"] = np.ascontiguousarray(inp["p_sample"][:, s0:s0 + NSS]).reshape(2, 128, 256)
        m["consts"] = CONSTS
        in_maps.append(m)
    res = run_bass_kernel_spmd(nc, in_maps, core_ids=list(range(NCORES)))
    R = res.results
    y_prompt = np.stack([R[b]["y_p"] for b in range(4)])
    y_sample = np.concatenate([R[c]["y_s"].reshape(NSS, DT, D) for c in range(NCORES)], 0)
    k_prompt = np.stack([R[b]["k_p"] for b in range(4)], 1).reshape(2, 4, SEQ, 8, 64)
    v_prompt = np.stack([R[b]["v_p"] for b in range(4)], 1).reshape(2, 4, SEQ, 8, 64)
    wkv_prompt = np.stack([R[b]["wkv_p"] for b in range(4)], 1).reshape(2, 4, 4, 64, 64)
    shift_prompt = np.stack([R[b]["shift_p"] for b in range(4)], 1)
    conv_prompt = np.stack([R[b]["conv_p"] for b in range(4)], 1)
    k_sample = np.concatenate([R[c]["k_s"].reshape(2, NSS, DT, 8, 64) for c in range(NCORES)], 1)
    v_sample = np.concatenate([R[c]["v_s"].reshape(2, NSS, DT, 8, 64) for c in range(NCORES)], 1)
    wkv_sample = np.concatenate([R[c]["wkv_s"].reshape(2, NSS, 4, 64, 64) for c in range(NCORES)], 1)
    shift_sample = np.concatenate([R[c]["shift_s"] for c in range(NCORES)], 1)
    conv_sample = np.concatenate([R[c]["conv_s"].reshape(2, NSS, 2, 256) for c in range(NCORES)], 1)
    outs = (y_prompt, y_sample, k_prompt, v_prompt, wkv_prompt, shift_prompt, conv_prompt,
            k_sample, v_sample, wkv_sample, shift_sample, conv_sample)
    return tuple(np.ascontiguousarray(o, dtype=f32) for o in outs)
```

```python
import numpy as np
import concourse.bass as bass
import concourse.mybir as mybir
from concourse.bass_utils import run_bass_kernel_spmd

F32 = mybir.dt.float32
BF16 = mybir.dt.bfloat16
I32 = mybir.dt.int32
AF = mybir.ActivationFunctionType
ALU = mybir.AluOpType
AX = mybir.AxisListType

SAME_ENGINE_SYNC = True
USE_CACHE = True
RWS = 0
RW4 = 0
NCORES = 8
D = 1024
SEQ = 4096
NSS = 16
DT = 8
NPAGES = 16
NPHYS = 2560
INC = 3328
RMS_EPS = 1e-6
GN_EPS = 64e-5


class Buf:
    __slots__ = ("name", "w", "r")

    def __init__(self, name):
        self.name = name
        self.w = None
        self.r = {}


class _Rec:
    def __getattr__(self, name):
        def f(*a, **k):
            self.__dict__["call"] = (name, a, k)
            return self
        return f


def _bind(fn):
    r = _Rec()
    fn(r)
    name, a, k = r.__dict__["call"]
    return lambda e: getattr(e, name)(*a, **k)


class KB:
    ENGS = ("pe", "act", "dve", "pool", "sp")

    def __init__(self, nc, n_dma_sems=40):
        self.nc = nc
        self.q = {e: [] for e in self.ENGS}
        self.cnt = {e: 0 for e in self.ENGS}
        self.known = {e: {} for e in self.ENGS}
        self.pend = {}
        self.sems = {}
        self._ctx = []
        for e in ("pe", "act", "dve", "pool"):
            self._sem("s_" + e)
        self.dpool = {}
        for qn, n in (("sp", n_dma_sems), ("pool", n_dma_sems), ("act", 8)):
            self.dpool[qn] = {"sems": [self._sem(f"d_{qn}{i}") for i in range(n)],
                              "tot": [0] * n, "next": 0}

    def _sem(self, name):
        cm = self.nc.semaphore(name)
        s = cm.__enter__()
        self._ctx.append(cm)
        self.sems[name] = s
        return name

    def _deps(self, eng, reads, writes):
        deps = {}

        def add(ev, ev_eng):
            k, v = ev[0], ev[1]
            if ev_eng == eng and not k.startswith("d_"):
                if eng == "pe" or not SAME_ENGINE_SYNC:
                    return
            if deps.get(k, 0) < v:
                deps[k] = v

        for b in reads:
            if b.w is not None:
                add(b.w, b.w[2])
        for b in writes:
            if b.w is not None:
                add(b.w, b.w[2])
            for re_, ev in b.r.items():
                add(ev, re_.split("#")[0])
        out = []
        kn = self.known[eng]
        for k, v in deps.items():
            if kn.get(k, 0) < v:
                kn[k] = v
                out.append((k, v))
        return out

    def op(self, eng, fn, reads=(), writes=(), inc=True):
        inc = True
        waits = self._deps(eng, reads, writes)
        fn = _bind(fn)
        self.q[eng].append((waits, fn, inc))
        pr, pw = self.pend.setdefault(eng, ([], []))
        if inc:
            self.cnt[eng] += 1
            ev = ("s_" + eng, self.cnt[eng], eng)
            for b in list(reads) + pr:
                b.r[eng] = (ev[0], ev[1])
            for b in list(writes) + pw:
                b.w = ev
                b.r = {}
            del pr[:]
            del pw[:]
        else:
            pr.extend(reads)
            pw.extend(writes)

    def dma(self, qn, fn, reads=(), writes=()):
        pool = self.dpool[qn]
        i = pool["next"]
        pool["next"] = (i + 1) % len(pool["sems"])
        key = pool["sems"][i]
        waits = self._deps(qn, reads, writes)
        prev = pool["tot"][i]
        kn = self.known[qn]
        if prev > 0 and kn.get(key, 0) < prev:
            kn[key] = prev
            waits.append((key, prev))
        val = prev + 16
        pool["tot"][i] = val
        fn = _bind(fn)
        self.q[qn].append((waits, fn, ("dma", key)))
        ev = (key, val, "dma")
        for b in reads:
            b.r["dma#%s%d" % (qn, i)] = (key, val)
        for b in writes:
            b.w = ev
            b.r = {}

    def barrier(self):
        for e in self.ENGS:
            waits = []
            kn = self.known[e]
            for e2 in ("pe", "act", "dve", "pool"):
                if e2 == e:
                    continue
                v = self.cnt[e2]
                k = "s_" + e2
                if v > 0 and kn.get(k, 0) < v:
                    kn[k] = v
                    waits.append((k, v))
            for qn, pool in self.dpool.items():
                for i, k in enumerate(pool["sems"]):
                    v = pool["tot"][i]
                    if v > 0 and kn.get(k, 0) < v:
                        kn[k] = v
                        waits.append((k, v))
            if waits:
                self.q[e].append((waits, None, False))

    def finish(self):
        self.barrier()
        nc = self.nc
        sems = self.sems
        q = self.q

        def emit(e, eobj):
            for waits, fn, inc in q[e]:
                for k, v in waits:
                    eobj.wait_ge(sems[k], v)
                if fn is None:
                    continue
                ins = fn(eobj)
                if inc is True:
                    ins.then_inc(sems["s_" + e], 1)
                elif inc:
                    ins.then_inc(sems[inc[1]], 16)

        with nc.Block() as block:
            @block.tensor
            def _(t):
                emit("pe", t)

            @block.scalar
            def _(t):
                emit("act", t)

            @block.vector
            def _(t):
                emit("dve", t)

            @block.gpsimd
            def _(t):
                emit("pool", t)

            @block.sync
            def _(t):
                emit("sp", t)
        for cm in reversed(self._ctx):
            cm.__exit__(None, None, None)
        self._ctx = []


class Scope:
    def __init__(self, nc):
        self.nc = nc
        self.stack = []

    UID = [0]

    def sb(self, name, shape, dt):
        Scope.UID[0] += 1
        name = "%s_%d" % (name, Scope.UID[0])
        cm = self.nc.sbuf_tensor(name, list(shape), dt)
        t = cm.__enter__()
        self.stack.append(cm)
        return t, Buf(name)

    def ps(self, name, shape, dt=F32):
        Scope.UID[0] += 1
        name = "%s_%d" % (name, Scope.UID[0])
        cm = self.nc.psum_tensor(name, list(shape), dt)
        t = cm.__enter__()
        self.stack.append(cm)
        return t, Buf(name)

    def close(self):
        for cm in reversed(self.stack):
            cm.__exit__(None, None, None)
        self.stack = []


def make_consts():
    p = np.arange(128)[:, None]
    f = np.arange(128)[None, :]
    c = {}
    c["ident"] = (p == f)
    c["ntri"] = -(p >= f).astype(np.float32)
    c["nones"] = -np.ones((128, 128), np.float32)
    c["mlt"] = (p < f)
    c["mle"] = (p <= f)
    c["mgt"] = (f < p)
    c["blk64"] = ((p // 64) == (f // 64))
    c["identrep"] = ((p % 64) == np.arange(64)[None, :])
    sp_, tp_ = p // 8, p % 8
    cols = np.arange(NSS * 64)[None, :]
    cs, ct = cols // 64, cols % 8
    c["smask"] = ((sp_ == cs) & (tp_ < ct))
    c["iota"] = p.astype(np.float32)
    c["zrow"] = np.zeros((128, 512), np.float32)
    c["sm2"] = ((p // 8) == (f // 8)) & ((p % 8) < (f % 8))
    c["m12"] = np.concatenate([(p < f)[:, 0:64], (p <= f)[:, 0:64]], axis=1)
    offs = {}
    o = 0
    arrs = []
    for k, v in c.items():
        v = np.asarray(v, np.float32)
        offs[k] = (o, v.shape[1])
        o += v.shape[1]
        arrs.append(v)
    return np.ascontiguousarray(np.concatenate(arrs, axis=1)), offs


CONSTS, COFF = make_consts()
NCONST = CONSTS.shape[1]


def build_program(nlayers=2, do_phase2=True, ngroups=17, nphys=NPHYS):
    nc = bass.Bass("TRN2", target_bir_lowering=False)

    def din(name, shape, dt=F32):
        return nc.dram_tensor(name, list(shape), dt, kind="ExternalInput").ap()

    def dout(name, shape):
        return nc.dram_tensor(name, list(shape), F32, kind="ExternalOutput").ap()

    xp = din("xp", [SEQ, D]); xs = din("xs", [128, D])
    if USE_CACHE:
        ck = din("cache_k", [2 * nphys * 128, 512]); cv = din("cache_v", [2 * nphys * 128, 512])
    s_wkv = din("s_wkv", [2, NSS * 4 * 64, 64]); s_shift = din("s_shift", [2, NSS, 1024])
    s_conv = din("s_conv", [2, NSS * 2, 256]); ptab = din("ptab", [1, NSS * NPAGES], I32)
    pp = din("pp", [2, SEQ, 256]); psm = din("ps", [2, 128, 256])
    consts = din("consts", [128, NCONST])
    W = {}
    for name, shape in (("g_mix", [2, D]), ("w_in", [2, D, INC]), ("mu_shift", [2, D]), ("w0", [2, 256]),
                        ("w2", [2, 64, 256]), ("a0", [2, 256]), ("a2", [2, 64, 256]), ("g2", [2, 128, 256]),
                        ("k_k", [2, 256]), ("k_a", [2, 256]), ("r_k", [2, 256]), ("gn_w", [2, 256]),
                        ("gn_b", [2, 256]), ("conv_w", [2, 3, 256]), ("q_gain", [2, 64]), ("k_gain", [2, 64]),
                        ("sb_bias", [2, 8]), ("w_out", [2, D, D]), ("g_mlp", [2, D]), ("w_up", [2, D, 4 * D]),
                        ("w_down", [2, 4 * D, D]), ("g_ple", [2, D]), ("w_ple_gate", [2, D, D]),
                        ("w_ple_proj", [2, 256, D])):
        W[name] = din(name, shape)
    y_p = dout("y_p", [SEQ, D]); y_s = dout("y_s", [128, D])
    k_p = dout("k_p", [2, SEQ, 512]); v_p = dout("v_p", [2, SEQ, 512])
    wkv_p = dout("wkv_p", [2, 4 * 64, 64]); shift_p = dout("shift_p", [2, D]); conv_p = dout("conv_p", [2, 2, 256])
    k_s = dout("k_s", [2, 128, 512]); v_s = dout("v_s", [2, 128, 512])
    wkv_s = dout("wkv_s", [2, NSS * 4 * 64, 64]); shift_s = dout("shift_s", [2, NSS, D])
    conv_s = dout("conv_s", [2, NSS * 2, 256])
    NTOK = SEQ + 128
    x1s = nc.dram_tensor("x1s", [8, 128, NTOK], F32, kind="Internal").ap()
    xls = nc.dram_tensor("xls", [8, 128, NTOK], F32, kind="Internal").ap()

    kbd = KB(nc)
    op = kbd.op
    dma = kbd.dma
    glob = Scope(nc)

    cf, bcf = glob.sb("cf", [128, 129], F32)
    cb, bcb = glob.sb("cb", [128, NCONST], BF16)
    dma("sp", lambda e: e.dma_start(out=cf[:, 0:128], in_=consts[:, COFF["ident"][0]:COFF["ident"][0] + 128]), writes=[bcf])
    dma("sp", lambda e: e.dma_start(out=cf[:, 128:129], in_=consts[:, COFF["iota"][0]:COFF["iota"][0] + 1], allow_slow_non_contiguous=True), writes=[bcf])
    dma("pool", lambda e: e.dma_start(out=cb[:], in_=consts[:, :]), writes=[bcb])

    def C(name, bf=True, rows=slice(0, 128), cols=None):
        o, n = COFF[name]
        if not bf:
            assert name == "ident"
            return cf[rows, 0:128]
        if cols is None:
            return cb[rows, o:o + n]
        return cb[rows, o + cols.start:o + cols.stop]

    KC = [bcf, bcb]

    PS = [glob.ps("ps%d" % i, [128, 512], F32) for i in range(8)]
    psi = [0]

    def nps(banks=(0, 1, 2, 3, 4)):
        i = banks[psi[0] % len(banks)]
        psi[0] += 1
        return PS[i]

    def mm(out, lhsT, rhs, start, stop, reads, wbuf, inc):
        op("pe", lambda e: e.matmul(out, lhsT=lhsT, rhs=rhs, start=start, stop=stop, skip_group_check=True),
           reads=reads, writes=[wbuf], inc=inc)

    def rsqrt_inplace(t_ap, bt, eng_r="dve"):
        op("act", lambda e: e.activation(out=t_ap, in_=t_ap, func=AF.Sqrt), reads=[bt], writes=[bt])
        op(eng_r, lambda e: e.reciprocal(out=t_ap, in_=t_ap), reads=[bt], writes=[bt])

    NG = 16
    groups = [(g * 256, 256, 1, 256) for g in range(NG)] + [(SEQ, 128, NSS, DT)]
    if ngroups < 0:
        groups = groups[-1:]
    elif ngroups < 17:
        groups = groups[:ngroups]
    ktscr = nc.dram_tensor("ktscr", [32, 128, 512], BF16, kind="Internal").ap()
    vscr = nc.dram_tensor("vscr", [32, 128, 512], BF16, kind="Internal").ap()
    bkts = [Buf("kts%d" % i) for i in range(32)]
    bvss = [Buf("vss%d" % i) for i in range(32)]
    bx1s = [Buf("x1s%d" % i) for i in range(17)]
    bxls = [Buf("xls%d" % i) for i in range(17)]
    identf = C("ident", bf=False)
    identb = C("ident")

    def rmsnorm(NT, src_t, bsrc, gcol, bg, dst_t, bdst, sq, bsq, rstd, brstd):
        pr, bpr = nps()
        for c in range(8):
            op("act", lambda e, c=c: e.activation(out=sq[c % 2][:, 0:NT], in_=src_t[:, c, 0:NT], func=AF.Square),
               reads=[bsrc], writes=[bsq[c % 2]])
            mm(pr[:, 0:NT], C("nones"), sq[c % 2][:, 0:NT], c == 0, c == 7, [bsq[c % 2]] + KC, bpr, c == 7)
        op("dve", lambda e: e.tensor_scalar(out=rstd[:, 0:NT], in0=pr[:, 0:NT], scalar1=-1.0 / D,
                                            scalar2=RMS_EPS, op0=ALU.mult, op1=ALU.add),
           reads=[bpr], writes=[brstd])
        rsqrt_inplace(rstd[:, 0:NT], brstd)
        for c in range(8):
            op("dve", lambda e, c=c: e.scalar_tensor_tensor(
                out=dst_t[:, c, 0:NT], in0=src_t[:, c, 0:NT], scalar=gcol(c), in1=rstd[:, 0:NT],
                op0=ALU.mult, op1=ALU.mult), reads=[bsrc, brstd, bg], writes=[bdst])

    def rwkv_mix(l, gi, t0, NT, nseq, T, sample):
        C_ = 8 if sample else 64
        nch = NT // C_
        EXPM05 = float(np.exp(-0.5))
        rX, kX, vX = xsh[:, 0:2, 0:NT], xsh[:, 2:4, 0:NT], xsh[:, 4:6, 0:NT]

        def F(t):
            return t[:, :, 0:NT]

        op("act", lambda e: e.activation(out=dwa[0:64, 0:NT], in_=xsh[0:64, 6, 0:NT], func=AF.Tanh), reads=[bxsh], writes=[bdwa])
        op("act", lambda e: e.copy(out=dwa[64:128, 0:NT], in_=xsh[64:128, 6, 0:NT]), reads=[bxsh], writes=[bdwa])
        op("act", lambda e: e.activation(out=sdg[:, 0:NT], in_=xsh[:, 7, 0:NT], func=AF.Sigmoid), reads=[bxsh], writes=[bsdg])
        for j in range(2):
            pw_, bpw_ = nps()
            mm(pw_[:, 0:NT], w2t[0:64, j * 128:(j + 1) * 128], dwa[0:64, 0:NT], True, True, [bw2, bdwa], bpw_, True)
            op("act", lambda e: e.activation(out=F1[:, j, 0:NT], in_=pw_[:, 0:NT], func=AF.Sigmoid, bias=pc("w0", j), scale=1.0),
               reads=[bpw_, bpcol], writes=[bF1])
            pa_, bpa_ = nps()
            mm(pa_[:, 0:NT], w2t[64:128, j * 128:(j + 1) * 128], dwa[64:128, 0:NT], True, True, [bw2, bdwa], bpa_, True)
            op("act", lambda e: e.activation(out=F2[:, j, 0:NT], in_=pa_[:, 0:NT], func=AF.Sigmoid, bias=pc("a0", j), scale=1.0),
               reads=[bpa_, bpcol], writes=[bF2])
        op("dve", lambda e: e.tensor_scalar(out=F(F1), in0=F(F1), scalar1=-EXPM05, scalar2=None, op0=ALU.mult),
           reads=[bF1], writes=[bF1])
        for j in range(2):
            op("dve", lambda e: e.tensor_scalar(out=F4[:, j, 0:NT], in0=kX[:, j, :], scalar1=pc("k_k", j), scalar2=None,
                                                op0=ALU.mult), reads=[bxsh, bpcol], writes=[bF4])
            op("pool", lambda e: e.tensor_tensor(out=sq[0][:, 0:NT], in0=F4[:, j, 0:NT], in1=F4[:, j, 0:NT], op=ALU.mult),
               reads=[bF4], writes=[bsq[0]])
            pk_, bpk_ = nps()
            mm(pk_[:, 0:NT], C("blk64"), sq[0][:, 0:NT], True, True, [bsq[0]] + KC, bpk_, True)
            op("dve", lambda e: e.tensor_scalar(out=rstd[:, 0:NT], in0=pk_[:, 0:NT], scalar1=1e-12, scalar2=None, op0=ALU.add),
               reads=[bpk_], writes=[brstd])
            rsqrt_inplace(rstd[:, 0:NT], brstd)
            op("dve", lambda e: e.tensor_tensor(out=F4[:, j, 0:NT], in0=F4[:, j, 0:NT], in1=rstd[:, 0:NT], op=ALU.mult),
               reads=[bF4, brstd], writes=[bF4])
            op("dve", lambda e: e.tensor_scalar(out=F5[:, j, 0:NT], in0=F2[:, j, 0:NT], scalar1=pc("k_a", j), scalar2=pc("omka", j),
                                                op0=ALU.mult, op1=ALU.add), reads=[bF2, bpcol], writes=[bF5])
            op("dve", lambda e: e.tensor_tensor(out=F5[:, j, 0:NT], in0=F5[:, j, 0:NT], in1=kX[:, j, :], op=ALU.mult),
               reads=[bF5, bxsh], writes=[bF5])
        src_t, bsrc_ = F1, bF1
        pp_ = [(F6, bF6), (F7, bF7)]
        k_ = 0
        s_ = 1
        while s_ < C_:
            dst_t, bdst_ = pp_[k_ % 2]
            sv = src_t[:, :, 0:NT].rearrange("p c (n t) -> p c n t", t=C_)
            dv = dst_t[:, :, 0:NT].rearrange("p c (n t) -> p c n t", t=C_)
            op("pool", lambda e: e.tensor_copy(out=dst_t[:, :, 0:NT], in_=src_t[:, :, 0:NT]), reads=[bsrc_], writes=[bdst_])
            for j in range(2):
                op("pool", lambda e: e.tensor_tensor(out=dv[:, j, :, s_:C_], in0=sv[:, j, :, s_:C_], in1=sv[:, j, :, 0:C_ - s_],
                                                     op=ALU.add), reads=[bsrc_, bdst_], writes=[bdst_])
            src_t, bsrc_ = dst_t, bdst_
            k_ += 1
            s_ *= 2
        cl, bcl = src_t, bsrc_
        oth, both = pp_[k_ % 2]
        clv = cl[:, :, 0:NT].rearrange("p c (n t) -> p c n t", t=C_)
        ARv = AR[:, :, 0:2 * NT].rearrange("p c (n w t) -> p c n w t", w=2, t=C_)

        def v4(t):
            return t[:, :, 0:NT].rearrange("p c (n t) -> p c n t", t=C_)
        op("act", lambda e: e.activation(out=F(F8), in_=cl[:, :, 0:NT], func=AF.Exp), reads=[bcl], writes=[bF8])
        for j in range(2):
            op("dve", lambda e: e.tensor_tensor(out=ARv[:, j, :, 1, :], in0=v4(xsh[:, 0:2])[:, j], in1=v4(F8)[:, j], op=ALU.mult),
               reads=[bxsh, bF8], writes=[bAR])
        for j in range(2):
            op("dve", lambda e: e.tensor_tensor(
                out=dP[:, j, 0:nch, :], in0=C("identrep").unsqueeze(1).to_broadcast([128, nch, 64]),
                in1=v4(F8)[:, j, :, C_ - 1:C_].to_broadcast([128, nch, 64]), op=ALU.mult), reads=[bF8] + KC, writes=[bdP])
        op("dve", lambda e: e.tensor_tensor(out=F(oth), in0=F(F4), in1=F(F2), op=ALU.mult), reads=[bF4, bF2], writes=[both])
        op("act", lambda e: e.activation(out=F(F8), in_=cl[:, :, 0:NT], func=AF.Exp, scale=-1.0), reads=[bcl], writes=[bF8])
        op("dve", lambda e: e.tensor_tensor(out=F(btT), in0=F(oth), in1=F(F8), op=ALU.mult), reads=[both, bF8], writes=[bbtT])
        op("dve", lambda e: e.tensor_tensor(out=F(ktT), in0=F(F5), in1=F(F8), op=ALU.mult), reads=[bF5, bF8], writes=[bktT])
        for j in range(2):
            op("pool", lambda e: e.tensor_tensor(out=v4(F8)[:, j], in0=clv[:, j, :, C_ - 1:C_].to_broadcast([128, nch, C_]),
                                                 in1=clv[:, j], op=ALU.subtract), reads=[bcl], writes=[bF8])
        op("act", lambda e: e.activation(out=F(F8), in_=F(F8), func=AF.Exp), reads=[bF8], writes=[bF8])
        op("dve", lambda e: e.tensor_tensor(out=F(bhX), in0=F(oth), in1=F(F8), op=ALU.mult), reads=[both, bF8], writes=[bbhX])
        op("dve", lambda e: e.tensor_tensor(out=F(khT), in0=F(F5), in1=F(F8), op=ALU.mult), reads=[bF5, bF8], writes=[bkhT])
        op("pool", lambda e: e.tensor_tensor(out=F(F8), in0=cl[:, :, 0:NT], in1=F(F1), op=ALU.subtract), reads=[bcl, bF1], writes=[bF8])
        op("act", lambda e: e.activation(out=F(F8), in_=F(F8), func=AF.Exp), reads=[bF8], writes=[bF8])
        for j in range(2):
            op("dve", lambda e: e.scalar_tensor_tensor(out=ARv[:, j, :, 0, :], in0=v4(F4)[:, j], scalar=-1.0, in1=v4(F8)[:, j],
                                                       op0=ALU.mult, op1=ALU.mult), reads=[bF4, bF8], writes=[bAR])
        op("pool", lambda e: e.tensor_copy(out=F(vbT), in_=vX), reads=[bxsh], writes=[bvbT])
        if RWS == 1:
            op("pool", lambda e: e.memset(mixT[:, 0:2, 0:NT], 0.0), writes=[bmixT])
            return
        if not sample and gi == 0:
            op("dve", lambda e: e.memset(STa[0][:], 0.0), writes=[bSTa[0]])
        sti = rw_state["i"]

        m12 = C("m12")
        for c in range(nch):
            cs_ = slice(c * C_, (c + 1) * C_)
            if sample:
                dma("sp", lambda e: e.dma_start(out=s0t[:, 0:64], in_=s_wkv[l, c * 256:c * 256 + 128, :]), writes=[bs0t])
                dma("sp", lambda e: e.dma_start(out=s0t[:, 64:128], in_=s_wkv[l, c * 256 + 128:c * 256 + 256, :]), writes=[bs0t])
                op("pool", lambda e: e.tensor_copy(out=s0b[:, :], in_=s0t[:, :]), reads=[bs0t], writes=[bs0b])
                for hh in range(2):
                    ps0, bps0 = nps()
                    ps0v = ps0[:, 0:64].bitcast(BF16)
                    hb = hh * 64
                    for hp in range(2):
                        op("pe", lambda e: e.transpose(out=ps0v[hb:hb + 64, hp * 64:(hp + 1) * 64],
                                                       in_=s0b[hb:hb + 64, hp * 64:(hp + 1) * 64],
                                                       identity=identb[hb:hb + 64, hb:hb + 64]),
                           reads=[bs0b] + KC, writes=[bps0])
                    op("act", lambda e: e.copy(out=STa[sti % 2][hb:hb + 64, :, :],
                                               in_=ps0v[hb:hb + 64, :].rearrange("p (a v) -> p a v", a=2)),
                       reads=[bps0], writes=[bSTa[sti % 2]])
            ST_, bST_ = STa[sti % 2], bSTa[sti % 2]
            STn, bSTn = STa[(sti + 1) % 2], bSTa[(sti + 1) % 2]
            ptk, bptk = nps()
            ptkv = ptk[:, 0:512].bitcast(BF16)
            srcs = [(lambda hp: ARv[:, hp, c, 0, :], bAR), (lambda hp: bhX[:, hp, cs_], bbhX),
                    (lambda hp: khT[:, hp, cs_], bkhT), (lambda hp: vbT[:, hp, cs_], bvbT)]
            for qi, (sf, bsf) in enumerate(srcs):
                for hp in range(2):
                    o_ = (qi * 2 + hp) * 128
                    op("pe", lambda e: e.transpose(out=ptkv[0:C_, o_:o_ + 128], in_=sf(hp), identity=identb),
                       reads=[bsf] + KC, writes=[bptk], inc=(qi == 3 and hp == 1))
            op("act", lambda e: e.copy(out=tok[0:C_, :], in_=ptkv[0:C_, :]), reads=[bptk], writes=[btok])

            def TK(qi, h):
                o_ = (qi * 2 + h // 2) * 128 + (h % 2) * 64
                return tok[0:C_, o_:o_ + 64]
            if RWS == 2:
                continue
            mk12 = m12[0:C_, :].rearrange("p (w j) -> p w j", w=2)[:, :, 0:C_]
            for which in range(3):
                pE, bpE = nps(); pO, bpO = nps()
                for h in range(4):
                    hp, hb = h // 2, (h % 2) * 64
                    pp2, bpp2 = (pE, bpE) if h % 2 == 0 else (pO, bpO)
                    arr = ARv[hb:hb + 64, hp, c, :, :]
                    if which == 0:
                        mm(pp2[0:C_, hp * 2 * C_:(hp + 1) * 2 * C_], btT[hb:hb + 64, hp, cs_], arr, True, True, [bbtT, bAR], bpp2, True)
                    elif which == 1:
                        mm(pp2[0:C_, hp * 2 * C_:(hp + 1) * 2 * C_], ktT[hb:hb + 64, hp, cs_], arr, True, True, [bktT, bAR], bpp2, True)
                    else:
                        mm(pp2[0:C_, hp * C_:(hp + 1) * C_], ARv[hb:hb + 64, hp, c, 0, :], btT[hb:hb + 64, hp, cs_], True, True,
                           [bbtT, bAR], bpp2, True)
                for h in range(4):
                    hp = h // 2
                    pp2, bpp2 = (pE, bpE) if h % 2 == 0 else (pO, bpO)
                    if which < 2:
                        At, bAt = (A1, bA1) if which == 0 else (A2, bA2)
                        op("dve", lambda e: e.tensor_tensor(
                            out=At[0:C_, h, :, 0:C_], in0=pp2[0:C_, hp * 2 * C_:(hp + 1) * 2 * C_].rearrange("p (w j) -> p w j", w=2),
                            in1=mk12, op=ALU.mult), reads=[bpp2] + KC, writes=[bAt])
                    else:
                        op("dve", lambda e: e.tensor_tensor(
                            out=Nm[0][0:C_, h, 0:C_], in0=pp2[0:C_, hp * C_:(hp + 1) * C_], in1=C("mgt")[0:C_, 0:C_], op=ALU.mult),
                           reads=[bpp2] + KC, writes=[bNm[0]])
            if RWS == 3:
                continue
            identC = identb[0:C_, 0:C_]
            for h in range(4):
                op("pool", lambda e: e.tensor_tensor(out=TT[0][0:C_, h, 0:C_], in0=A1[0:C_, h, 0, 0:C_], in1=identC, op=ALU.add),
                   reads=[bA1] + KC, writes=[bTT[0]])
                op("pool", lambda e: e.tensor_copy(out=NmT[0][0:C_, h, 0:C_], in_=A1[0:C_, h, 0, 0:C_]), reads=[bA1], writes=[bNmT[0]])
            ti = 0; ni = 0
            m_ = 2
            while m_ < C_:
                if RW4 == 1:
                    break
                last = (2 * m_ >= C_)
                pn, bpn = nps(); pnt, bpnt = nps()
                for h in range(4):
                    mm(pn[0:C_, h * C_:(h + 1) * C_], NmT[ni][0:C_, h, 0:C_], Nm[ni][0:C_, h, 0:C_], True, True,
                       [bNm[ni], bNmT[ni]], bpn, h == 3)
                if not last:
                    for h in range(4):
                        mm(pnt[0:C_, h * C_:(h + 1) * C_], Nm[ni][0:C_, h, 0:C_], NmT[ni][0:C_, h, 0:C_], True, True,
                           [bNm[ni], bNmT[ni]], bpnt, h == 3)
                if RW4 == 2:
                    break
                n2 = 1 - ni
                pnv = pn[0:C_, 0:4 * C_].rearrange("p (h j) -> p h j", h=4)
                op("act", lambda e: e.copy(out=Nm[n2][0:C_, :, 0:C_], in_=pnv), reads=[bpn], writes=[bNm[n2]])
                op("dve", lambda e: e.tensor_tensor(out=Qm[0:C_, :, 0:C_], in0=Nm[n2][0:C_, :, 0:C_],
                                                    in1=identC.unsqueeze(1).to_broadcast([C_, 4, C_]), op=ALU.add),
                   reads=[bNm[n2]] + KC, writes=[bQm])
                if not last:
                    op("act", lambda e: e.copy(out=NmT[n2][0:C_, :, 0:C_],
                                               in_=pnt[0:C_, 0:4 * C_].rearrange("p (h j) -> p h j", h=4)),
                       reads=[bpnt], writes=[bNmT[n2]])
                if RW4 == 3:
                    break
                ptt, bptt = nps()
                for h in range(4):
                    mm(ptt[0:C_, h * C_:(h + 1) * C_], Qm[0:C_, h, 0:C_], TT[ti][0:C_, h, 0:C_], True, True, [bQm, bTT[ti]], bptt, h == 3)
                op("act", lambda e: e.copy(out=TT[1 - ti][0:C_, :, 0:C_],
                                           in_=ptt[0:C_, 0:4 * C_].rearrange("p (h j) -> p h j", h=4)),
                   reads=[bptt], writes=[bTT[1 - ti]])
                ti = 1 - ti
                ni = n2
                m_ *= 2
                if RW4 == 4:
                    break
            TTf, bTTf = TT[ti], bTT[ti]
            if RWS == 4:
                continue
            pw2, bpw2 = nps()
            for h in range(4):
                mm(pw2[0:C_, h * 64:(h + 1) * 64], A2[0:C_, h, 0, 0:C_], TK(3, h), True, True, [bA2, btok], bpw2, h == 3)
            op("act", lambda e: e.copy(out=XW[0:C_, :, 1, :], in_=pw2[0:C_, 0:256].rearrange("p (h v) -> p h v", h=4)),
               reads=[bpw2], writes=[bXW])
            op("pool", lambda e: e.tensor_copy(out=XW[0:C_, :, 0, :], in_=tok[0:C_, 0:256].rearrange("p (h k) -> p h k", h=4)),
               reads=[btok], writes=[bXW])
            pav, bpav = nps()
            for h in range(4):
                mm(pav[0:C_, h * 128:(h + 1) * 128], TTf[0:C_, h, 0:C_], XW[0:C_, h, :, :], True, True, [bTTf, bXW], bpav, h == 3)
            op("act", lambda e: e.copy(out=AV[0:C_, :, :], in_=pav[0:C_, 0:512].rearrange("p (h x) -> p h x", h=4)),
               reads=[bpav], writes=[bAV])
            if RWS == 5:
                continue
            pat, bpat = nps()
            for h in range(4):
                hp, hb = h // 2, (h % 2) * 64
                mm(pat[hb:hb + 64, hp * C_:(hp + 1) * C_], TK(0, h), TTf[0:C_, h, 0:C_], True, True, [btok, bTTf], bpat, h == 3)
            op("act", lambda e: e.copy(out=AhT[:, :, 0:C_], in_=pat[:, 0:2 * C_].rearrange("p (a t) -> p a t", a=2)),
               reads=[bpat], writes=[bAhT])
            pm_, bpm_ = nps()
            for h in range(4):
                hp, hb = h // 2, (h % 2) * 64
                mm(pm_[hb:hb + 64, hp * 64:(hp + 1) * 64], AV[0:C_, h, 0:64], TK(1, h), True, True, [bAV, btok], bpm_, h == 3)
            op("dve", lambda e: e.tensor_tensor(out=MT[:, :, :], in0=pm_[:, 0:128].rearrange("p (a k) -> p a k", a=2),
                                                in1=dP[:, :, c, :], op=ALU.add), reads=[bpm_, bdP], writes=[bMT])
            pnn, bpnn = nps()
            for h in range(4):
                hp, hb = h // 2, (h % 2) * 64
                mm(pnn[hb:hb + 64, hp * 64:(hp + 1) * 64], TK(1, h), AV[0:C_, h, 64:128], True, False, [bAV, btok], bpnn, False)
                mm(pnn[hb:hb + 64, hp * 64:(hp + 1) * 64], TK(2, h), TK(3, h), False, True, [btok], bpnn, h == 3)
            op("act", lambda e: e.copy(out=NNs[:, :, :], in_=pnn[:, 0:128].rearrange("p (a v) -> p a v", a=2)),
               reads=[bpnn], writes=[bNNs])
            if RWS == 6:
                continue
            puE, bpuE = nps(); puO, bpuO = nps()
            for h in range(4):
                hp, hb = h // 2, (h % 2) * 64
                pu, bpu = (puE, bpuE) if h % 2 == 0 else (puO, bpuO)
                mm(pu[0:C_, hp * 64:(hp + 1) * 64], AhT[hb:hb + 64, hp, 0:C_], ST_[hb:hb + 64, hp, :], True, True, [bAhT, bST_], bpu, True)
            for h in range(4):
                hp = h // 2
                pu, bpu = (puE, bpuE) if h % 2 == 0 else (puO, bpuO)
                op("dve", lambda e: e.tensor_tensor(out=Ub[0:C_, h, :], in0=pu[0:C_, hp * 64:(hp + 1) * 64],
                                                    in1=AV[0:C_, h, 64:128], op=ALU.add), reads=[bpu, bAV], writes=[bUb])
            for h in range(4):
                hp, hb = h // 2, (h % 2) * 64
                py1_, bpy1_ = (pY1, bpY1) if h % 2 == 0 else (pY1o, bpY1o)
                mm(py1_[hb:hb + 64, hp * 256 + c * C_:hp * 256 + (c + 1) * C_], ST_[hb:hb + 64, hp, :], ARv[hb:hb + 64, hp, c, 1, :],
                   True, True, [bST_, bAR], bpy1_, True)
            for h in range(4):
                hp, hb = h // 2, (h % 2) * 64
                oy = pY2[hb:hb + 64, hp * 256 + c * C_:hp * 256 + (c + 1) * C_]
                mm(oy, Ub[0:C_, h, :], A1[0:C_, h, 1, 0:C_], True, False, [bUb, bA1], bpY2, False)
                mm(oy, TK(3, h), A2[0:C_, h, 1, 0:C_], False, True, [btok, bA2], bpY2, h == 3)
            psE, bpsE = nps(); psO, bpsO = nps()
            for h in range(4):
                hp, hb = h // 2, (h % 2) * 64
                ps_, bps_ = (psE, bpsE) if h % 2 == 0 else (psO, bpsO)
                mm(ps_[hb:hb + 64, hp * 64:(hp + 1) * 64], MT[hb:hb + 64, hp, :], ST_[hb:hb + 64, hp, :], True, True, [bMT, bST_], bps_, True)
            need_out = sample or (gi == NG - 1 and c == nch - 1)
            for hh, (ps_, bps_) in enumerate(((psE, bpsE), (psO, bpsO))):
                prt = slice(hh * 64, hh * 64 + 64)
                psv = ps_[prt, 0:128].rearrange("p (a v) -> p a v", a=2)
                op("dve", lambda e: e.tensor_tensor(out=STn[prt, :, :], in0=psv, in1=NNs[prt, :, :], op=ALU.add),
                   reads=[bps_, bNNs], writes=[bSTn])
                if need_out:
                    op("dve", lambda e: e.tensor_tensor(out=STf[prt, :, :], in0=psv, in1=NNs[prt, :, :], op=ALU.add),
                       reads=[bps_, bNNs], writes=[bSTf])
            if need_out:
                poE, bpoE = nps(); poO, bpoO = nps()
                for h in range(4):
                    hp, hb = h // 2, (h % 2) * 64
                    po_, bpo_ = (poE, bpoE) if h % 2 == 0 else (poO, bpoO)
                    op("pe", lambda e: e.transpose(out=po_[0:64, hp * 64:(hp + 1) * 64], in_=STf[hb:hb + 64, hp, :],
                                                   identity=identf[hb:hb + 64, hb:hb + 64]),
                       reads=[bSTf] + KC, writes=[bpo_])
                for h in range(4):
                    hp = h // 2
                    po_, bpo_ = (poE, bpoE) if h % 2 == 0 else (poO, bpoO)
                    op("act", lambda e: e.copy(out=swo[:, h * 64:(h + 1) * 64], in_=po_[0:64, hp * 64:(hp + 1) * 64]),
                       reads=[bpo_], writes=[bswo])
                dsto = wkv_s[l, c * 256:(c + 1) * 256, :] if sample else wkv_p[l, :, :]
                for h in range(4):
                    dma("sp", lambda e: e.dma_start(out=dsto[h * 64:(h + 1) * 64, :], in_=swo[0:64, h * 64:(h + 1) * 64]),
                        reads=[bswo])
            sti += 1
        rw_state["i"] = sti
        if 2 <= RWS <= 6:
            op("pool", lambda e: e.memset(mixT[:, 0:2, 0:NT], 0.0), writes=[bmixT])
            return
        yT_, byT_ = F8, bF8
        op("act", lambda e: e.copy(out=yT_[0:64, :, 0:NT], in_=pY1[0:64, 0:512].rearrange("p (a t) -> p a t", a=2)[:, :, 0:NT]),
           reads=[bpY1], writes=[byT_])
        op("act", lambda e: e.copy(out=yT_[64:128, :, 0:NT], in_=pY1o[64:128, 0:512].rearrange("p (a t) -> p a t", a=2)[:, :, 0:NT]),
           reads=[bpY1o], writes=[byT_])
        op("dve", lambda e: e.tensor_tensor(out=yT_[:, :, 0:NT], in0=yT_[:, :, 0:NT],
                                            in1=pY2[:, 0:512].rearrange("p (a t) -> p a t", a=2)[:, :, 0:NT], op=ALU.add),
           reads=[bpY2, byT_], writes=[byT_])
        for j in range(2):
            pg_, bpg_ = nps()
            mm(pg_[:, 0:NT], g2t[:, j * 128:(j + 1) * 128], sdg[:, 0:NT], True, True, [bg2, bsdg], bpg_, True)
            op("act", lambda e: e.copy(out=F1[:, j, 0:NT], in_=pg_[:, 0:NT]), reads=[bpg_], writes=[bF1])
        for j in range(2):
            yj = yT_[:, j, 0:NT]
            op("pool", lambda e: e.tensor_copy(out=sq[0][:, 0:NT], in_=yj), reads=[byT_], writes=[bsq[0]])
            op("pool", lambda e: e.tensor_tensor(out=sq[1][:, 0:NT], in0=yj, in1=yj, op=ALU.mult), reads=[byT_], writes=[bsq[1]])
            pm1, bpm1 = nps(); pm2, bpm2 = nps()
            mm(pm1[:, 0:NT], C("blk64"), sq[0][:, 0:NT], True, True, [bsq[0]] + KC, bpm1, True)
            mm(pm2[:, 0:NT], C("blk64"), sq[1][:, 0:NT], True, True, [bsq[1]] + KC, bpm2, True)
            mu_ = F6[:, j, 0:NT]; var_ = F7[:, j, 0:NT]
            op("dve", lambda e: e.tensor_scalar(out=mu_, in0=pm1[:, 0:NT], scalar1=1.0 / 64, scalar2=None, op0=ALU.mult), reads=[bpm1], writes=[bF6])
            op("dve", lambda e: e.tensor_tensor(out=var_, in0=mu_, in1=mu_, op=ALU.mult), reads=[bF6], writes=[bF7])
            op("dve", lambda e: e.scalar_tensor_tensor(out=var_, in0=pm2[:, 0:NT], scalar=1.0 / 64, in1=var_,
                                                       op0=ALU.mult, op1=ALU.subtract), reads=[bpm2, bF7], writes=[bF7])
            op("dve", lambda e: e.tensor_scalar(out=var_, in0=var_, scalar1=GN_EPS, scalar2=None, op0=ALU.add), reads=[bF7], writes=[bF7])
            rsqrt_inplace(var_, bF7)
            op("dve", lambda e: e.tensor_tensor(out=yj, in0=yj, in1=mu_, op=ALU.subtract), reads=[byT_, bF6], writes=[byT_])
            op("dve", lambda e: e.tensor_tensor(out=yj, in0=yj, in1=var_, op=ALU.mult), reads=[byT_, bF7], writes=[byT_])
            op("dve", lambda e: e.tensor_scalar(out=yj, in0=yj, scalar1=pc("gn_w", j), scalar2=pc("gn_b", j), op0=ALU.mult, op1=ALU.add),
               reads=[byT_, bpcol], writes=[byT_])
            op("dve", lambda e: e.scalar_tensor_tensor(out=sq[0][:, 0:NT], in0=rX[:, j, :], scalar=pc("r_k", j), in1=F5[:, j, 0:NT],
                                                       op0=ALU.mult, op1=ALU.mult), reads=[bxsh, bF5, bpcol], writes=[bsq[0]])
            pb_, bpb_ = nps()
            mm(pb_[:, 0:NT], C("blk64"), sq[0][:, 0:NT], True, True, [bsq[0]] + KC, bpb_, True)
            op("dve", lambda e: e.tensor_tensor(out=mu_, in0=pb_[:, 0:NT], in1=vX[:, j, :], op=ALU.mult), reads=[bpb_, bxsh], writes=[bF6])
            op("dve", lambda e: e.tensor_tensor(out=yj, in0=yj, in1=mu_, op=ALU.add), reads=[byT_, bF6], writes=[byT_])
            op("dve", lambda e: e.tensor_tensor(out=mixT[:, j, 0:NT], in0=yj, in1=F1[:, j, 0:NT], op=ALU.mult),
               reads=[byT_, bF1], writes=[bmixT])

    def sample_attn(l):
        NT = 128
        Rflat = Rf[:].rearrange("p a b -> p (a b)")
        eS = Rflat[:, 0:1024]
        wS = Rflat[:, 1024:2048]
        Rbflat = Rb[:].rearrange("p a b -> p (a b)")
        lkN = Rbflat[:, 0:1024]
        wN = Rbflat[:, 1024:2048]
        cmask = xtok[:, 0:1024]
        dma("sp", lambda e: e.dma_start(out=pti[:, :], in_=ptab.partition_broadcast(128)), writes=[bpti])
        op("dve", lambda e: e.tensor_copy(out=x1[:, 0:256], in_=pti[:, :]), reads=[bpti], writes=[bx1])
        op("dve", lambda e: e.tensor_scalar(out=x1[:, 0:256], in0=x1[:, 0:256], scalar1=128.0, scalar2=cf[:, 128:129],
                                            op0=ALU.mult, op1=ALU.add), reads=[bx1] + KC, writes=[bx1])
        op("dve", lambda e: e.tensor_scalar(out=x1[:, 0:256], in0=x1[:, 0:256], scalar1=float(l * nphys * 128), scalar2=None,
                                            op0=ALU.add), reads=[bx1], writes=[bx1])
        op("dve", lambda e: e.tensor_copy(out=idxi[:, :], in_=x1[:, 0:256]), reads=[bx1], writes=[bidxi])
        op("act", lambda e: e.activation(out=cexp[:, :], in_=sbb[:, :], func=AF.Exp), reads=[bsbb], writes=[bcexp])
        op("dve", lambda e: e.tensor_tensor(out=cmask.rearrange("p (h c) -> p h c", h=8),
                                            in0=C("sm2").unsqueeze(1).to_broadcast([128, 8, 128]),
                                            in1=cexp[:, :].unsqueeze(2).to_broadcast([128, 8, 128]), op=ALU.mult),
           reads=[bcexp] + KC, writes=[bxtok])
        dma("sp", lambda e: e.dma_start(out=qT8[:, :, :].rearrange("p (a b) t -> p a b t", b=2)[:, :, 0, :], in_=qT[0:64, :, 0:128]),
            reads=[bqT], writes=[bqT8])
        dma("sp", lambda e: e.dma_start(out=qT8[:, :, :].rearrange("p (a b) t -> p a b t", b=2)[:, :, 1, :], in_=qT[64:128, :, 0:128]),
            reads=[bqT], writes=[bqT8])
        ptk_, bptk_ = nps()
        ptkv_ = ptk_[:, 0:512].bitcast(BF16)
        for h in range(8):
            op("pe", lambda e: e.transpose(out=ptkv_[0:64, h * 128:(h + 1) * 128], in_=kbf[:, h * 64:(h + 1) * 64], identity=identb),
               reads=[bkbf] + KC, writes=[bptk_])
        op("act", lambda e: e.copy(out=KTs8[:, :, :], in_=ptkv_[0:64, :].rearrange("p (h t) -> p h t", h=8)),
           reads=[bptk_], writes=[bKTs8])
        po, bpo = PS[6]
        mm(po[:, :], C("zrow", rows=slice(0, 1), cols=slice(0, 128)), C("zrow", rows=slice(0, 1)), True, True, KC, bpo, True)
        zb = [nps(), nps()]
        for h in range(8):
            pz, bpz = zb[h // 4]
            mm(pz[:, (h % 4) * 128:(h % 4 + 1) * 128], KTs8[:, h, :], qT8[:, h, :], True, True, [bKTs8, bqT8], bpz, True)
        for i, (pz, bpz) in enumerate(zb):
            op("act", lambda e: e.activation(out=eS[:, i * 512:(i + 1) * 512], in_=pz[:, :], func=AF.Exp), reads=[bpz], writes=[bRf])
        op("dve", lambda e: e.tensor_tensor(out=eS, in0=eS, in1=cmask, op=ALU.mult), reads=[bRf, bxtok], writes=[bRf])
        op("act", lambda e: e.activation(out=lkN, in_=eS, func=AF.Ln, bias=1.0, scale=1.0), reads=[bRf], writes=[bRb])
        lb = [nps(), nps()]
        for i, (pl_, bpl_) in enumerate(lb):
            mm(pl_[:, :], C("ntri"), lkN[:, i * 512:(i + 1) * 512], True, False, [bRb] + KC, bpl_, True)
        for h in range(8):
            pl_, bpl_ = lb[h // 4]
            mm(pl_[:, (h % 4) * 128:(h % 4 + 1) * 128], KTs8[:, h, :], qT8[:, h, :], False, True, [bKTs8, bqT8], bpl_, True)
        for i, (pl_, bpl_) in enumerate(lb):
            op("act", lambda e: e.activation(out=wS[:, i * 512:(i + 1) * 512], in_=pl_[:, :], func=AF.Exp), reads=[bpl_], writes=[bRf])
        op("dve", lambda e: e.tensor_tensor(out=wN, in0=wS, in1=cmask, op=ALU.mult), reads=[bRf, bxtok], writes=[bRb])
        for h in range(8):
            hp, hb = h // 2, (h % 2) * 64
            mm(po[hb:hb + 64, hp * 128:(hp + 1) * 128], Vs[:, h * 64:(h + 1) * 64], wN[:, h * 128:(h + 1) * 128], False, False,
               [bVs, bRb], bpo, True)
        lkv = lkN.rearrange("p (h s t) -> p h s t", h=8, s=NSS)
        ce3 = cexp[:, :].unsqueeze(2).to_broadcast([128, 8, 8])
        kpgs = [(ksq, bksq), (ktok, bktok)]
        pages = []
        for s_ in range(NSS):
            for j in range(NPAGES - 1, -1, -1):
                pages.append({"s": s_, "j": j, "i": len(pages)})

        def stageA(B):
            s_, j, pgi = B["s"], B["j"], B["i"]
            col = s_ * NPAGES + j
            kpg, bkpg = kpgs[pgi % 2]
            ktp, bktp = ktp2[pgi % 2]
            vb_, bvb_ = vl4[pgi % 3]
            eb, beb = eb3[pgi % 2]
            lk, blk = lk3[pgi % 3]
            B.update(ktp=ktp, bktp=bktp, vb=vb_, bvb=bvb_, eb=eb, beb=beb, lk=lk, blk=blk)
            dma("pool", lambda e: e.indirect_dma_start(out=kpg[:, :], out_offset=None, in_=ck[:, :],
                in_offset=bass.IndirectOffsetOnAxis(ap=idxi[:, col:col + 1], axis=0)), reads=[bidxi], writes=[bkpg])
            dma("pool", lambda e: e.indirect_dma_start(out=vtok[:, :], out_offset=None, in_=cv[:, :],
                in_offset=bass.IndirectOffsetOnAxis(ap=idxi[:, col:col + 1], axis=0)), reads=[bidxi], writes=[bvtok])
            op("pool", lambda e: e.tensor_copy(out=vb_[:, :], in_=vtok[:, :]), reads=[bvtok], writes=[bvb_])
            op("dve", lambda e: e.tensor_copy(out=kbf[:, :], in_=kpg[:, :]), reads=[bkpg], writes=[bkbf])
            ptp, bptp = nps()
            ptpv = ptp[:, 0:512].bitcast(BF16)
            for h in range(8):
                op("pe", lambda e: e.transpose(out=ptpv[0:64, h * 128:(h + 1) * 128], in_=kbf[:, h * 64:(h + 1) * 64],
                                               identity=identb), reads=[bkbf] + KC, writes=[bptp])
            op("act", lambda e: e.copy(out=ktp[:, :], in_=ptpv[0:64, :]), reads=[bptp], writes=[bktp])
            pz, bpz = nps()
            for h in range(8):
                mm(pz[:, h * 8:(h + 1) * 8], ktp[:, h * 128:(h + 1) * 128], qT8[:, h, s_ * 8:(s_ + 1) * 8], True, True,
                   [bktp, bqT8], bpz, True)
            op("act", lambda e: e.activation(out=eb[:, 0:64], in_=pz[:, 0:64], func=AF.Exp), reads=[bpz], writes=[beb])
            op("dve", lambda e: e.tensor_tensor(out=eb[:, 0:64].rearrange("p (h t) -> p h t", h=8),
                                                in0=eb[:, 0:64].rearrange("p (h t) -> p h t", h=8), in1=ce3, op=ALU.mult),
               reads=[beb, bcexp], writes=[beb])
            op("act", lambda e: e.activation(out=lk[:, 0:64], in_=eb[:, 0:64], func=AF.Ln, bias=1.0, scale=1.0),
               reads=[beb], writes=[blk])

        def stageB(B):
            s_, j, pgi = B["s"], B["j"], B["i"]
            ktp, bktp, vb_, bvb_, lk, blk = B["ktp"], B["bktp"], B["vb"], B["bvb"], B["lk"], B["blk"]
            eb, beb = eb3w[pgi % 2]
            wT, bwT = wT3[pgi % 3]
            if j == NPAGES - 1:
                op("pool", lambda e: e.tensor_copy(out=Rs[:, :].rearrange("p (h t) -> p h t", h=8), in_=lkv[:, :, s_, :]),
                   reads=[bRb], writes=[bRs])
                op("pool", lambda e: e.tensor_copy(out=Rsb[:, :], in_=Rs[:, :]), reads=[bRs], writes=[bRsb])
            pl_, bpl_ = nps()
            mm(pl_[:, 0:64], C("ntri"), lk[:, 0:64], True, False, [blk] + KC, bpl_, True)
            mm(pl_[:, 0:64], C("nones"), Rsb[:, :], False, False, [bRsb] + KC, bpl_, True)
            for h in range(8):
                mm(pl_[:, h * 8:(h + 1) * 8], ktp[:, h * 128:(h + 1) * 128], qT8[:, h, s_ * 8:(s_ + 1) * 8], False, h == 7,
                   [bktp, bqT8], bpl_, True)
            op("act", lambda e: e.activation(out=eb[:, 0:64], in_=pl_[:, 0:64], func=AF.Exp), reads=[bpl_], writes=[beb])
            op("dve", lambda e: e.tensor_tensor(out=wT[:, 0:64].rearrange("p (h t) -> p h t", h=8),
                                                in0=eb[:, 0:64].rearrange("p (h t) -> p h t", h=8), in1=ce3, op=ALU.mult),
               reads=[beb, bcexp], writes=[bwT])
            for h in range(8):
                hp, hb = h // 2, (h % 2) * 64
                mm(po[hb:hb + 64, hp * 128 + s_ * 8:hp * 128 + (s_ + 1) * 8], vb_[:, h * 64:(h + 1) * 64], wT[:, h * 8:(h + 1) * 8],
                   False, False, [bvb_, bwT], bpo, True)
            if j > 0:
                op("pool", lambda e: e.tensor_tensor(out=Rs[:, :], in0=Rs[:, :], in1=lk[:, 0:64], op=ALU.add),
                   reads=[bRs, blk], writes=[bRs])
                op("pool", lambda e: e.tensor_copy(out=Rsb[:, :], in_=Rs[:, :]), reads=[bRs], writes=[bRsb])

        SKP = 1
        for i in range(len(pages) + SKP):
            if i < len(pages):
                stageA(pages[i])
            if i - SKP >= 0:
                stageB(pages[i - SKP])
        for hp in range(4):
            op("act", lambda e: e.copy(out=mixT[:, 4 + hp, 0:NT], in_=po[:, hp * 128:(hp + 1) * 128]), reads=[bpo], writes=[bmixT])

    for l in range(nlayers):
        P1 = Scope(nc)
        win, bwin = P1.sb("win", [128, 8, INC], BF16)
        wout, bwout = P1.sb("wout", [128, 8, D], BF16)
        w2t, bw2 = P1.sb("w2t", [128, 256], BF16)
        g2t, bg2 = P1.sb("g2t", [128, 256], BF16)
        pcol, bpcol = P1.sb("pcol", [128, 64], F32)
        kgn, bkgn = P1.sb("kgn", [128, 64], F32)
        sbb, bsbb = P1.sb("sbb", [128, 8], F32)
        sbr, bsbr = P1.sb("sbr", [1, 64], F32)
        wsrc = W["w_in"][l].rearrange("(kc p) n -> p kc n", p=128)
        for kc in range(8):
            dma("pool", lambda e, kc=kc: e.dma_start(out=win[:, kc, :], in_=wsrc[:, kc, :]), writes=[bwin])
        wsrc2 = W["w_out"][l].rearrange("(kc p) n -> p kc n", p=128)
        for kc in range(0, 8, 4):
            dma("pool", lambda e, kc=kc: e.dma_start(out=wout[:, kc:kc + 4, :], in_=wsrc2[:, kc:kc + 4, :]), writes=[bwout])
        dma("pool", lambda e: e.dma_start(out=w2t[0:64, :], in_=W["w2"][l]), writes=[bw2])
        dma("pool", lambda e: e.dma_start(out=w2t[64:128, :], in_=W["a2"][l]), writes=[bw2])
        dma("pool", lambda e: e.dma_start(out=g2t[:, :], in_=W["g2"][l]), writes=[bg2])
        PCO = {"g_mix": 0, "mu": 8, "omu": 16, "w0": 24, "a0": 26, "k_k": 28, "k_a": 30, "r_k": 32, "gn_w": 34,
               "gn_b": 36, "conv": 38, "qg": 44, "omka": 46}

        def ldcol(name, off, n):
            src = W[name][l].rearrange("(c p) -> p c", p=128)
            dma("sp", lambda e: e.dma_start(out=pcol[:, off:off + n], in_=src, allow_slow_non_contiguous=True),
                writes=[bpcol])
        ldcol("g_mix", 0, 8); ldcol("mu_shift", 8, 8)
        for nm in ("w0", "a0", "k_k", "k_a", "r_k", "gn_w", "gn_b"):
            ldcol(nm, PCO[nm], 2)
        for j in range(3):
            srcj = W["conv_w"][l, j].rearrange("(c p) -> p c", p=128)
            dma("sp", lambda e, j=j, srcj=srcj: e.dma_start(out=pcol[:, 38 + 2 * j:40 + 2 * j], in_=srcj,
                                                          allow_slow_non_contiguous=True), writes=[bpcol])
        qgsrc = W["q_gain"][l].rearrange("(p o) -> p o", o=1)
        dma("sp", lambda e: e.dma_start(out=pcol[0:64, 44:45], in_=qgsrc), writes=[bpcol])
        dma("sp", lambda e: e.dma_start(out=pcol[64:128, 44:45], in_=qgsrc), writes=[bpcol])
        op("dve", lambda e: e.tensor_scalar(out=pcol[:, 16:24], in0=pcol[:, 8:16], scalar1=-1.0, scalar2=1.0,
                                            op0=ALU.mult, op1=ALU.add), reads=[bpcol], writes=[bpcol])
        op("dve", lambda e: e.tensor_scalar(out=pcol[:, 44:45], in0=pcol[:, 44:45], scalar1=0.125, scalar2=None,
                                            op0=ALU.mult), reads=[bpcol], writes=[bpcol])
        op("dve", lambda e: e.tensor_scalar(out=pcol[:, 46:48], in0=pcol[:, 30:32], scalar1=-1.0, scalar2=1.0,
                                            op0=ALU.mult, op1=ALU.add), reads=[bpcol], writes=[bpcol])
        dma("sp", lambda e: e.dma_start(out=kgn[:, :], in_=W["k_gain"][l:l + 1, :].partition_broadcast(128)), writes=[bkgn])
        dma("sp", lambda e: e.dma_start(out=sbb[:, :], in_=W["sb_bias"][l:l + 1, :].partition_broadcast(128)), writes=[bsbb])
        op("dve", lambda e: e.tensor_copy(out=sbr[0:1, :].rearrange("p (h t) -> p h t", t=8),
                                          in_=sbb[0:1, :].unsqueeze(2).to_broadcast([1, 8, 8])),
           reads=[bsbb], writes=[bsbr])

        def pc(name, c=0, n=1):
            o = PCO[name] + c
            return pcol[:, o:o + n]

        rwc, brwc = P1.sb("rwc", [128, 8], F32)
        cvc, bcvc = P1.sb("cvc", [128, 2, 2], F32)
        op("dve", lambda e: e.memset(rwc[:], 0.0), writes=[brwc])
        op("dve", lambda e: e.memset(cvc[:], 0.0), writes=[bcvc])

        G = Scope(nc)
        xT, bxT = G.sb("xT", [128, 8, 256], F32)
        hT, bhT = G.sb("hT", [128, 8, 256], BF16)
        sq2 = [G.sb("sq%d" % i, [128, 256], BF16) for i in range(2)]
        sq, bsq = [t for t, _ in sq2], [b for _, b in sq2]
        rstd, brstd = G.sb("rstd", [128, 256], F32)
        xtok, bxtok = G.sb("xtok", [128, D], F32)
        rwt2 = [G.sb("rwt%d" % i, [128, 264], F32) for i in range(2)]
        sttT, bsttT = G.sb("sttT", [128, 8, NSS], F32)
        shs, bshs = G.sb("shs", [128, 8, NSS], F32)
        xsh, bxsh = G.sb("xsh", [128, 8, 256], F32)
        cvb, bcvb = G.sb("cvb", [128, 2, 256], F32)
        cvu, bcvu = G.sb("cvu", [128, 2, 320], F32)
        cvt, bcvt = G.sb("cvt", [128, 2, 256], F32)
        cst, bcst = G.sb("cst", [NSS * 2, 256], F32)
        qT, bqT = G.sb("qT", [128, 4, 256], BF16)
        qf, bqf = G.sb("qf", [128, 256], F32)
        mixT, bmixT = G.sb("mixT", [128, 8, 256], BF16)
        ktok, bktok = G.sb("ktok", [128, 512], F32)
        ksq, bksq = G.sb("ksq", [128, 512], F32)
        kss, bkss = G.sb("kss", [128, 8], F32)
        kbf, bkbf = G.sb("kbf", [128, 512], BF16)
        ktb, bktb = G.sb("ktb", [128, 512], BF16)
        vtok, bvtok = G.sb("vtok", [128, 512], F32)
        vbf, bvbf = G.sb("vbf", [128, 512], BF16)
        KTs, bKTs = G.sb("KTs", [128, 512], BF16)
        Vs, bVs = G.sb("Vs", [128, 512], BF16)
        x1, bx1 = G.sb("x1", [128, 256], F32)
        eb3 = [G.sb("ebuf%d" % i, [128, 256], F32) for i in range(2)]
        lk3 = [G.sb("lk%d" % i, [128, 256], BF16) for i in range(5)]
        wT3 = [G.sb("wT%d" % i, [128, 256], BF16) for i in range(3)]
        Rf, bRf = G.sb("Rf", [128, 8, 256], F32)
        Rb, bRb = G.sb("Rb", [128, 8, 256], BF16)
        bRfh = [Buf("Rfh%d" % i) for i in range(8)]
        bRbh = [Buf("Rbh%d" % i) for i in range(8)]
        ktl4 = [G.sb("ktl%d" % i, [128, 512], BF16) for i in range(3)]
        vl4 = [G.sb("vl%d" % i, [128, 512], BF16) for i in range(3)]
        rot = {"e": 0, "l": 0, "w": 0, "k": 0}
        rw_state = {"i": 0}
        pti, bpti = G.sb("pti", [128, 256], I32)
        idxi, bidxi = G.sb("idxi", [128, 256], I32)
        cexp, bcexp = G.sb("cexp", [128, 8], F32)
        qT8, bqT8 = G.sb("qT8", [64, 8, 128], BF16)
        KTs8, bKTs8 = G.sb("KTs8", [64, 8, 128], BF16)
        ktp2 = [G.sb("ktp%d" % i, [64, 1024], BF16) for i in range(2)]
        Rs, bRs = G.sb("Rs", [128, 64], F32)
        eb3w = [G.sb("ebw%d" % i, [128, 64], F32) for i in range(2)]
        Rsb, bRsb = G.sb("Rsb", [128, 64], BF16)
        pY1, bpY1 = PS[6]
        pY2, bpY2 = PS[7]
        pY1o, bpY1o = PS[5]
        dwa, bdwa = G.sb("dwa", [128, 256], BF16)
        sdg, bsdg = G.sb("sdg", [128, 256], BF16)
        F1, bF1 = G.sb("F1", [128, 2, 256], F32); F2, bF2 = G.sb("F2", [128, 2, 256], F32)
        F4, bF4 = G.sb("F4", [128, 2, 256], F32); F5, bF5 = G.sb("F5", [128, 2, 256], F32)
        F6, bF6 = G.sb("F6", [128, 2, 256], F32); F7, bF7 = G.sb("F7", [128, 2, 256], F32)
        F8, bF8 = G.sb("F8", [128, 2, 256], F32)
        AR, bAR = G.sb("AR", [128, 2, 512], BF16)
        btT, bbtT = G.sb("btT", [128, 2, 256], BF16); ktT, bktT = G.sb("ktT", [128, 2, 256], BF16)
        bhX, bbhX = G.sb("bhX", [128, 2, 256], BF16); khT, bkhT = G.sb("khT", [128, 2, 256], BF16)
        vbT, bvbT = G.sb("vbT", [128, 2, 256], BF16)
        dP, bdP = G.sb("dP", [128, 2, 16, 64], BF16)
        tok, btok = G.sb("tok", [64, 1024], BF16)
        A1, bA1 = G.sb("A1", [64, 4, 2, 64], BF16); A2, bA2 = G.sb("A2", [64, 4, 2, 64], BF16)
        _n = [G.sb("Nm%d" % i, [64, 4, 64], BF16) for i in range(2)]; Nm = [t for t, _ in _n]; bNm = [b for _, b in _n]
        _n = [G.sb("NmT%d" % i, [64, 4, 64], BF16) for i in range(2)]; NmT = [t for t, _ in _n]; bNmT = [b for _, b in _n]
        _n = [G.sb("TT%d" % i, [64, 4, 64], BF16) for i in range(2)]; TT = [t for t, _ in _n]; bTT = [b for _, b in _n]
        Qm, bQm = G.sb("Qm", [64, 4, 64], BF16)
        XW, bXW = G.sb("XW", [64, 4, 2, 64], BF16)
        AV, bAV = G.sb("AV", [64, 4, 128], BF16)
        AhT, bAhT = G.sb("AhT", [128, 2, 64], BF16)
        MT, bMT = G.sb("MT", [128, 2, 64], BF16)
        NNs, bNNs = G.sb("NNs", [128, 2, 64], F32)
        Ub, bUb = G.sb("Ub", [64, 4, 64], BF16)
        _n = [G.sb("STa%d" % i, [128, 2, 64], BF16) for i in range(2)]; STa = [t for t, _ in _n]; bSTa = [b for _, b in _n]
        STf, bSTf = G.sb("STf", [128, 2, 64], F32)
        swo, bswo = G.sb("swo", [64, 256], F32)
        s0t, bs0t = G.sb("s0t", [128, 128], F32)
        s0b, bs0b = G.sb("s0b", [128, 128], BF16)

        for gi, (t0, NT, nseq, T) in enumerate(groups):
            sample = nseq > 1
            ntile = NT // 128
            if l == 0:
                src = xs if sample else xp
                for tt in range(ntile):
                    r0 = tt * 128 if sample else t0 + tt * 128
                    dma("sp", lambda e, r0=r0, src=src: e.dma_start(out=xtok[:], in_=src[r0:r0 + 128, :]), writes=[bxtok])
                    for c4 in range(2):
                        pt, bpt = nps()
                        for c in range(4):
                            cc = c4 * 4 + c
                            op("pe", lambda e, c=c, cc=cc, pt=pt: e.transpose(out=pt[:, c * 128:(c + 1) * 128],
                               in_=xtok[:, cc * 128:(cc + 1) * 128], identity=identf),
                               reads=[bxtok] + KC, writes=[bpt], inc=(c == 3))
                        op("act", lambda e, pt=pt, c4=c4, tt=tt: e.copy(
                            out=xT[:, c4 * 4:c4 * 4 + 4, tt * 128:(tt + 1) * 128],
                            in_=pt[:, :].rearrange("p (c t) -> p c t", c=4)), reads=[bpt], writes=[bxT])
            else:
                for c in range(8):
                    dma("sp", lambda e, c=c: e.dma_start(out=xT[:, c, 0:NT], in_=xls[c, :, t0:t0 + NT]),
                        reads=[bxls[gi]], writes=[bxT])

            rmsnorm(NT, xT, bxT, lambda c: pc("g_mix", c), bpcol, hT, bhT, sq, bsq, rstd, brstd)

            def proj_chunk(cc):
                pt, bpt = nps()
                for kc in range(8):
                    mm(pt[:, 0:NT], win[:, kc, cc * 128:(cc + 1) * 128], hT[:, kc, 0:NT], kc == 0, kc == 7,
                       [bwin, bhT], bpt, kc == 7)
                return pt, bpt

            if sample:
                dma("sp", lambda e: e.dma_start(out=xtok[0:NSS, :], in_=s_shift[l]), writes=[bxtok])
                pt, bpt = nps()
                for c in range(8):
                    op("pe", lambda e, c=c, pt=pt: e.transpose(out=pt[:, c * 16:(c + 1) * 16],
                       in_=xtok[0:NSS, c * 128:(c + 1) * 128], identity=identf[0:NSS, 0:NSS]),
                       reads=[bxtok] + KC, writes=[bpt], inc=(c == 7))
                op("act", lambda e, pt=pt: e.copy(out=sttT[:, :, :], in_=pt[:, 0:128].rearrange("p (c s) -> p c s", c=8)),
                   reads=[bpt], writes=[bsttT])
            xshv = xsh[:, :, 0:NT].rearrange("p c (s t) -> p c s t", s=nseq)
            for cc in range(8):
                rwt, brwt = rwt2[cc % 2]
                rwv = rwt[:, 0:nseq * (T + 1)].rearrange("p (s t) -> p s t", s=nseq)
                pt, bpt = proj_chunk(cc)
                op("act", lambda e, pt=pt, rwv=rwv: e.copy(
                    out=rwv[:, :, 1:T + 1], in_=pt[:, 0:NT].rearrange("p (s t) -> p s t", s=nseq)),
                   reads=[bpt], writes=[brwt])
                if sample:
                    op("act", lambda e, rwv=rwv, cc=cc: e.copy(out=rwv[:, :, 0], in_=sttT[:, cc, :]),
                       reads=[bsttT], writes=[brwt])
                    op("pool", lambda e, rwv=rwv, cc=cc: e.tensor_copy(out=shs[:, cc, :], in_=rwv[:, :, T]),
                       reads=[brwt], writes=[bshs])
                else:
                    op("act", lambda e, rwv=rwv, cc=cc: e.copy(out=rwv[:, 0, 0:1], in_=rwc[:, cc:cc + 1]),
                       reads=[brwc], writes=[brwt])
                    op("pool", lambda e, rwv=rwv, cc=cc: e.tensor_copy(out=rwc[:, cc:cc + 1], in_=rwv[:, 0, T:T + 1]),
                       reads=[brwt], writes=[brwc])
                eng = "dve"
                op("pool", lambda e, cc=cc, rwv=rwv: e.tensor_scalar(out=xshv[:, cc], in0=rwv[:, :, 0:T], scalar1=pc("mu", cc),
                                                                 scalar2=None, op0=ALU.mult),
                   reads=[brwt, bpcol], writes=[bxsh])
                op(eng, lambda e, cc=cc, rwv=rwv: e.scalar_tensor_tensor(
                    out=xshv[:, cc], in0=rwv[:, :, 1:T + 1], scalar=pc("omu", cc), in1=xshv[:, cc],
                    op0=ALU.mult, op1=ALU.add), reads=[brwt, bxsh, bpcol], writes=[bxsh])
            if sample:
                for c in range(8):
                    dma("sp", lambda e, c=c: e.dma_start(
                        out=shift_s[l, :, c * 128:(c + 1) * 128].rearrange("s p -> p s"), in_=shs[:, c, :],
                        allow_slow_non_contiguous=True), reads=[bshs])
            elif gi == NG - 1:
                dma("sp", lambda e: e.dma_start(out=shift_p[l].rearrange("(c p) -> p c", p=128),
                                                in_=rwc[:, :], allow_slow_non_contiguous=True), reads=[brwc])

            cuv = cvu[:, :, 0:nseq * (T + 2)].rearrange("p c (s t) -> p c s t", s=nseq)
            if sample:
                dma("sp", lambda e: e.dma_start(out=cst[:, :], in_=s_conv[l]), writes=[bcst])
                pt, bpt = nps()
                for c in range(2):
                    op("pe", lambda e, c=c, pt=pt: e.transpose(out=pt[:, c * 32:(c + 1) * 32],
                       in_=cst[0:32, c * 128:(c + 1) * 128], identity=identf[0:32, 0:32]),
                       reads=[bcst] + KC, writes=[bpt], inc=(c == 1))
                op("act", lambda e, pt=pt: e.copy(out=cuv[:, :, :, 0:2],
                                                  in_=pt[:, 0:64].rearrange("p (c s j) -> p c s j", c=2, j=2)),
                   reads=[bpt], writes=[bcvu])
            else:
                op("act", lambda e: e.copy(out=cuv[:, :, 0, 0:2], in_=cvc[:, :, :]), reads=[bcvc], writes=[bcvu])
            for c in range(2):
                pt, bpt = proj_chunk(8 + c)
                op("act", lambda e, pt=pt, c=c: e.copy(out=cvb[:, c, 0:NT], in_=pt[:, 0:NT]), reads=[bpt], writes=[bcvb])
            for c in range(2):
                pt, bpt = proj_chunk(10 + c)
                op("act", lambda e, pt=pt, c=c: e.copy(out=cvt[:, c, 0:NT], in_=pt[:, 0:NT]), reads=[bpt], writes=[bcvt])
                pt2, bpt2 = proj_chunk(12 + c)
                op("dve", lambda e, pt2=pt2, c=c: e.tensor_tensor(
                    out=cuv[:, c, :, 2:T + 2], in0=cvt[:, c, 0:NT].rearrange("p (s t) -> p s t", s=nseq),
                    in1=pt2[:, 0:NT].rearrange("p (s t) -> p s t", s=nseq), op=ALU.mult),
                   reads=[bpt2, bcvt], writes=[bcvu])
            cvtv = cvt[:, :, 0:NT].rearrange("p c (s t) -> p c s t", s=nseq)
            for c in range(2):
                eng = "dve"
                op(eng, lambda e, c=c: e.tensor_scalar(out=cvtv[:, c], in0=cuv[:, c, :, 0:T], scalar1=pc("conv", 0 + c),
                                                       scalar2=None, op0=ALU.mult), reads=[bcvu, bpcol], writes=[bcvt])
                for j in (1, 2):
                    op(eng, lambda e, c=c, j=j: e.scalar_tensor_tensor(
                        out=cvtv[:, c], in0=cuv[:, c, :, j:j + T], scalar=pc("conv", 2 * j + c), in1=cvtv[:, c],
                        op0=ALU.mult, op1=ALU.add), reads=[bcvu, bcvt, bpcol], writes=[bcvt])
                op(eng, lambda e, c=c: e.tensor_tensor(out=mixT[:, 2 + c, 0:NT], in0=cvt[:, c, 0:NT],
                                                       in1=cvb[:, c, 0:NT], op=ALU.mult),
                   reads=[bcvt, bcvb], writes=[bmixT])
            if sample:
                for c in range(2):
                    for j in range(2):
                        dma("sp", lambda e, c=c, j=j: e.dma_start(
                            out=conv_s[l].rearrange("(s j) f -> j f s", j=2)[j, c * 128:(c + 1) * 128, :],
                            in_=cuv[:, c, :, T + j], allow_slow_non_contiguous=True), reads=[bcvu])
            else:
                op("pool", lambda e: e.tensor_copy(out=cvc[:, :, :], in_=cuv[:, :, 0, T:T + 2]), reads=[bcvu], writes=[bcvc])
                if gi == NG - 1:
                    for j in range(2):
                        dma("sp", lambda e, j=j: e.dma_start(out=conv_p[l, j].rearrange("(c p) -> p c", p=128),
                                                             in_=cvc[:, :, j], allow_slow_non_contiguous=True),
                            reads=[bcvc])

            for hp in range(4):
                pt, bpt = proj_chunk(14 + hp)
                op("act", lambda e, pt=pt: e.copy(out=qf[:, 0:NT], in_=pt[:, 0:NT]), reads=[bpt], writes=[bqf])
                op("act", lambda e, pt=pt: e.activation(out=sq[0][:, 0:NT], in_=pt[:, 0:NT], func=AF.Square),
                   reads=[bpt], writes=[bsq[0]])
                pr, bpr = nps()
                mm(pr[:, 0:NT], C("blk64"), sq[0][:, 0:NT], True, True, [bsq[0]] + KC, bpr, True)
                op("dve", lambda e, pr=pr: e.tensor_scalar(out=rstd[:, 0:NT], in0=pr[:, 0:NT], scalar1=1.0 / 64,
                                                           scalar2=RMS_EPS, op0=ALU.mult, op1=ALU.add),
                   reads=[bpr], writes=[brstd])
                rsqrt_inplace(rstd[:, 0:NT], brstd)
                op("dve", lambda e, hp=hp: e.scalar_tensor_tensor(out=qT[:, hp, 0:NT], in0=qf[:, 0:NT], scalar=pc("qg"),
                                                                  in1=rstd[:, 0:NT], op0=ALU.mult, op1=ALU.mult),
                   reads=[bqf, brstd, bpcol], writes=[bqT])

            for tt in range(ntile):
                tok0 = t0 + tt * 128
                kbi = tok0 // 128
                for which in range(2):
                    pt, bpt = nps()
                    cbase = 2304 if which == 0 else 2816
                    for kc in range(8):
                        mm(pt[:, :], hT[:, kc, tt * 128:(tt + 1) * 128], win[:, kc, cbase:cbase + 512], kc == 0, kc == 7,
                           [bwin, bhT], bpt, kc == 7)
                    if which == 0:
                        op("act", lambda e, pt=pt: e.copy(out=ktok[:, :], in_=pt[:, :]), reads=[bpt], writes=[bktok])
                        op("pool", lambda e: e.tensor_tensor(out=ksq[:, :], in0=ktok[:, :], in1=ktok[:, :], op=ALU.mult),
                           reads=[bktok], writes=[bksq])
                        op("dve", lambda e: e.tensor_reduce(out=kss[:, :], in_=ksq[:, :].rearrange("p (h d) -> p h d", d=64),
                                                            axis=AX.X, op=ALU.add), reads=[bksq], writes=[bkss])
                        op("dve", lambda e: e.tensor_scalar(out=kss[:, :], in0=kss[:, :], scalar1=1.0 / 64, scalar2=RMS_EPS,
                                                            op0=ALU.mult, op1=ALU.add), reads=[bkss], writes=[bkss])
                        rsqrt_inplace(kss[:, :], bkss)
                        kv3 = ktok[:, :].rearrange("p (h d) -> p h d", d=64)
                        op("dve", lambda e, kv3=kv3: e.tensor_tensor(out=kv3, in0=kv3,
                                                                     in1=kss[:, :].unsqueeze(2).to_broadcast([128, 8, 64]),
                                                                     op=ALU.mult), reads=[bktok, bkss], writes=[bktok])
                        op("dve", lambda e, kv3=kv3: e.tensor_tensor(out=kv3, in0=kv3,
                                                                     in1=kgn[:, :].unsqueeze(1).to_broadcast([128, 8, 64]),
                                                                     op=ALU.mult), reads=[bktok, bkgn], writes=[bktok])
                        dst = k_s[l, :, :] if sample else k_p[l, tok0:tok0 + 128, :]
                        dma("sp", lambda e, dst=dst: e.dma_start(out=dst, in_=ktok[:, :]), reads=[bktok])
                        op("pool", lambda e: e.tensor_copy(out=kbf[:, :], in_=ktok[:, :]), reads=[bktok], writes=[bkbf])
                        ptb, bptb = nps()
                        ptbv = ptb[:, 0:256].bitcast(BF16)
                        for hp in range(4):
                            op("pe", lambda e, hp=hp, ptbv=ptbv: e.transpose(out=ptbv[:, hp * 128:(hp + 1) * 128],
                               in_=kbf[:, hp * 128:(hp + 1) * 128], identity=identb),
                               reads=[bkbf] + KC, writes=[bptb], inc=(hp == 3))
                        if sample:
                            op("act", lambda e, ptbv=ptbv: e.copy(out=KTs[:, :], in_=ptbv), reads=[bptb], writes=[bKTs])
                        else:
                            op("act", lambda e, ptbv=ptbv: e.copy(out=ktb[:, :], in_=ptbv), reads=[bptb], writes=[bktb])
                            dma("sp", lambda e, kbi=kbi: e.dma_start(out=ktscr[kbi], in_=ktb[:, :]),
                                reads=[bktb], writes=[bkts[kbi]])
                    else:
                        op("act", lambda e, pt=pt: e.copy(out=vtok[:, :], in_=pt[:, :]), reads=[bpt], writes=[bvtok])
                        dst = v_s[l, :, :] if sample else v_p[l, tok0:tok0 + 128, :]
                        dma("sp", lambda e, dst=dst: e.dma_start(out=dst, in_=vtok[:, :]), reads=[bvtok])
                        if sample:
                            op("pool", lambda e: e.tensor_copy(out=Vs[:, :], in_=vtok[:, :]), reads=[bvtok], writes=[bVs])
                        else:
                            op("pool", lambda e: e.tensor_copy(out=vbf[:, :], in_=vtok[:, :]), reads=[bvtok], writes=[bvbf])
                            dma("sp", lambda e, kbi=kbi: e.dma_start(out=vscr[kbi], in_=vbf[:, :]),
                                reads=[bvbf], writes=[bvss[kbi]])

            psO = [PS[6], PS[7]]
            if not sample:
                g = gi
                for (po, bpo) in psO:
                    mm(po[:, :], C("zrow", rows=slice(0, 1), cols=slice(0, 128)), C("zrow", rows=slice(0, 1)), True, True,
                       KC, bpo, True)
                op("pool", lambda e: e.memset(Rf[:], 0.0), writes=[bRf] + bRfh)
                op("pool", lambda e: e.memset(Rb[:], 0.0), writes=[bRb] + bRbh)
                kbmax = 2 * g + 1
                blks = []
                for kb in range(kbmax, -1, -1):
                    for h in range(8):
                        blks.append({"kb": kb, "h": h})

                def stage1(B):
                    kb, h = B["kb"], B["h"]
                    j = kb - 2 * g
                    cs = max(0, j) * 128
                    if h == 0:
                        ktl, bktl = ktl4[rot["k"] % 3]
                        vl, bvl = vl4[rot["k"] % 3]
                        rot["k"] += 1
                        dma("sp", lambda e: e.dma_start(out=ktl[:, :], in_=ktscr[kb]), reads=[bkts[kb]], writes=[bktl])
                        dma("sp", lambda e: e.dma_start(out=vl[:, :], in_=vscr[kb]), reads=[bvss[kb]], writes=[bvl])
                        rot["cur"] = (ktl, bktl, vl, bvl)
                    ktl, bktl, vl, bvl = rot["cur"]
                    hp, hb = h // 2, (h % 2) * 64
                    eb, beb = eb3[rot["e"] % 2]; rot["e"] += 1
                    lk, blk = lk3[rot["l"] % 5]; rot["l"] += 1
                    kop = ktl[hb:hb + 64, hp * 128:(hp + 1) * 128]
                    qop = qT[hb:hb + 64, hp, cs:NT]
                    B.update(j=j, cs=cs, ktl=ktl, bktl=bktl, vl=vl, bvl=bvl, lk=lk, blk=blk, kop=kop, qop=qop, hp=hp, hb=hb)
                    pa, bpa = nps()
                    mm(pa[:, cs:NT], kop, qop, True, True, [bktl, bqT], bpa, True)
                    op("act", lambda e: e.activation(out=eb[:, cs:NT], in_=pa[:, cs:NT], func=AF.Exp, bias=sbb[:, h:h + 1], scale=1.0),
                       reads=[bpa, bsbb], writes=[beb])
                    op("act", lambda e: e.activation(out=lk[:, cs:NT], in_=eb[:, cs:NT], func=AF.Ln, bias=1.0, scale=1.0),
                       reads=[beb], writes=[blk])
                    if j >= 0:
                        op("dve", lambda e: e.tensor_tensor(out=lk[:, cs:cs + 128], in0=lk[:, cs:cs + 128], in1=C("mlt"), op=ALU.mult),
                           reads=[blk] + KC, writes=[blk])

                def stage2(B):
                    kb, h, j, cs, hp, hb = B["kb"], B["h"], B["j"], B["cs"], B["hp"], B["hb"]
                    lk, blk, kop, qop = B["lk"], B["blk"], B["kop"], B["qop"]
                    wT, bwT = wT3[rot["w"] % 3]; rot["w"] += 1
                    pb, bpb = nps()
                    mm(pb[:, cs:NT], C("ntri"), lk[:, cs:NT], True, False, [blk] + KC, bpb, False)
                    if kb < kbmax:
                        mm(pb[:, cs:NT], C("nones"), Rb[:, h, cs:NT], False, False, [bRbh[h]] + KC, bpb, False)
                    mm(pb[:, cs:NT], kop, qop, False, True, [B["bktl"], bqT], bpb, True)
                    op("act", lambda e: e.activation(out=wT[:, cs:NT], in_=pb[:, cs:NT], func=AF.Exp, bias=sbb[:, h:h + 1], scale=1.0),
                       reads=[bpb, bsbb], writes=[bwT])
                    if j >= 0:
                        op("dve", lambda e: e.tensor_tensor(out=wT[:, cs:cs + 128], in0=wT[:, cs:cs + 128], in1=C("mlt"), op=ALU.mult),
                           reads=[bwT] + KC, writes=[bwT])
                    po, bpo = psO[hp // 2]
                    co = (hp % 2) * 256
                    mm(po[hb:hb + 64, co + cs:co + NT], B["vl"][:, h * 64:(h + 1) * 64], wT[:, cs:NT], False, False,
                       [B["bvl"], bwT], bpo, True)
                    if kb > 0:
                        op("pool", lambda e: e.tensor_tensor(out=Rf[:, h, cs:NT], in0=Rf[:, h, cs:NT], in1=lk[:, cs:NT], op=ALU.add),
                           reads=[bRfh[h], blk], writes=[bRfh[h]])
                        op("pool", lambda e: e.tensor_copy(out=Rb[:, h, cs:NT], in_=Rf[:, h, cs:NT]),
                           reads=[bRfh[h]], writes=[bRbh[h]])

                SK = 4
                for i in range(len(blks) + SK):
                    if i < len(blks):
                        stage1(blks[i])
                    if i - SK >= 0:
                        stage2(blks[i - SK])
                op("pool", lambda e: e.memset(Rs[:, 0:1], 0.0), reads=bRfh + bRbh, writes=[bRf, bRb, bRs])
                for hp in range(4):
                    po, bpo = psO[hp // 2]
                    co = (hp % 2) * 256
                    op("act", lambda e, po=po, co=co, hp=hp: e.copy(out=mixT[:, 4 + hp, 0:NT], in_=po[:, co:co + NT]),
                       reads=[bpo], writes=[bmixT])
            else:
                if USE_CACHE:
                    sample_attn(l)
                else:
                    sample_attention(l, locals())

            rwkv_mix(l, gi, t0, NT, nseq, T, sample)

            for j in range(8):
                pt, bpt = nps()
                for kc in range(8):
                    mm(pt[:, 0:NT], wout[:, kc, j * 128:(j + 1) * 128], mixT[:, kc, 0:NT], kc == 0, kc == 7,
                       [bwout, bmixT], bpt, kc == 7)
                op("dve", lambda e, pt=pt, j=j: e.tensor_tensor(out=x1[:, 0:NT], in0=pt[:, 0:NT], in1=xT[:, j, 0:NT],
                                                                op=ALU.add), reads=[bpt, bxT], writes=[bx1])
                dma("sp", lambda e, j=j: e.dma_start(out=x1s[j, :, t0:t0 + NT], in_=x1[:, 0:NT]), reads=[bx1],
                    writes=[bx1s[gi]])
        kbd.barrier()
        G.close()
        P1.close()
        if not do_phase2:
            continue
        P2 = Scope(nc)
        wup, bwup = P2.sb("wup", [128, 8, 4 * D], BF16)
        wdn, bwdn = P2.sb("wdn", [128, 32, D], BF16)
        wpg, bwpg = P2.sb("wpg", [128, 8, D], BF16)
        wpp, bwpp = P2.sb("wpp", [128, 2, D], BF16)
        pc2, bpc2 = P2.sb("pc2", [128, 16], F32)
        s_up = W["w_up"][l].rearrange("(kc p) n -> p kc n", p=128)
        for kc in range(8):
            dma("pool", lambda e, kc=kc: e.dma_start(out=wup[:, kc, :], in_=s_up[:, kc, :]), writes=[bwup])
        s_dn = W["w_down"][l].rearrange("(kc p) n -> p kc n", p=128)
        for kc in range(0, 32, 4):
            dma("pool", lambda e, kc=kc: e.dma_start(out=wdn[:, kc:kc + 4, :], in_=s_dn[:, kc:kc + 4, :]), writes=[bwdn])
        s_pg = W["w_ple_gate"][l].rearrange("(kc p) n -> p kc n", p=128)
        for kc in range(0, 8, 4):
            dma("pool", lambda e, kc=kc: e.dma_start(out=wpg[:, kc:kc + 4, :], in_=s_pg[:, kc:kc + 4, :]), writes=[bwpg])
        s_pp = W["w_ple_proj"][l].rearrange("(kc p) n -> p kc n", p=128)
        dma("pool", lambda e: e.dma_start(out=wpp[:, :, :], in_=s_pp), writes=[bwpp])
        for nm, off in (("g_mlp", 0), ("g_ple", 8)):
            srcc = W[nm][l].rearrange("(c p) -> p c", p=128)
            dma("sp", lambda e, srcc=srcc, off=off: e.dma_start(out=pc2[:, off:off + 8], in_=srcc,
                                                               allow_slow_non_contiguous=True), writes=[bpc2])
        G2 = Scope(nc)
        xT, bxT = G2.sb("xT2", [128, 8, 256], F32)
        hT, bhT = G2.sb("hT2", [128, 8, 256], BF16)
        sq2 = [G2.sb("sq2%d" % i, [128, 256], BF16) for i in range(2)]
        sq, bsq = [t for t, _ in sq2], [b for _, b in sq2]
        rstd, brstd = G2.sb("rstd2", [128, 256], F32)
        actT, bactT = G2.sb("actT", [128, 32, 256], BF16)
        rl2 = [G2.sb("rl%d" % i, [128, 256], BF16) for i in range(2)]
        ptok, bptok = G2.sb("ptok", [128, 256], F32)
        pT, bpT = G2.sb("pT", [128, 2, 256], BF16)
        gt2 = [G2.sb("gt%d" % i, [128, 256], F32) for i in range(2)]
        ytok, bytok = G2.sb("ytok", [128, D], F32)
        for gi, (t0, NT, nseq, T) in enumerate(groups):
            sample = nseq > 1
            ntile = NT // 128
            for c in range(8):
                dma("sp", lambda e, c=c: e.dma_start(out=xT[:, c, 0:NT], in_=x1s[c, :, t0:t0 + NT]),
                    reads=[bx1s[gi]], writes=[bxT])
            rmsnorm(NT, xT, bxT, lambda c: pc2[:, c:c + 1], bpc2, hT, bhT, sq, bsq, rstd, brstd)
            for fc in range(32):
                pt, bpt = nps((0, 1, 2, 3, 4, 5, 6, 7))
                for kc in range(8):
                    mm(pt[:, 0:NT], wup[:, kc, fc * 128:(fc + 1) * 128], hT[:, kc, 0:NT], kc == 0, kc == 7,
                       [bwup, bhT], bpt, kc == 7)
                rl, brl = rl2[fc % 2]
                op("act", lambda e, pt=pt, rl=rl: e.activation(out=rl[:, 0:NT], in_=pt[:, 0:NT], func=AF.Relu),
                   reads=[bpt], writes=[brl])
                op("pool" if fc % 2 == 0 else "dve", lambda e, rl=rl, fc=fc: e.tensor_tensor(
                    out=actT[:, fc, 0:NT], in0=rl[:, 0:NT], in1=rl[:, 0:NT], op=ALU.mult), reads=[brl], writes=[bactT])
            for j in range(8):
                pt, bpt = nps((0, 1, 2, 3, 4, 5, 6, 7))
                for fc in range(32):
                    mm(pt[:, 0:NT], wdn[:, fc, j * 128:(j + 1) * 128], actT[:, fc, 0:NT], fc == 0, fc == 31,
                       [bwdn, bactT], bpt, fc == 31)
                op("dve", lambda e, pt=pt, j=j: e.tensor_tensor(out=xT[:, j, 0:NT], in0=pt[:, 0:NT], in1=xT[:, j, 0:NT],
                                                                op=ALU.add), reads=[bpt, bxT], writes=[bxT])
            rmsnorm(NT, xT, bxT, lambda c: pc2[:, 8 + c:9 + c], bpc2, hT, bhT, sq, bsq, rstd, brstd)
            psrc = psm[l] if sample else pp[l]
            for tt in range(ntile):
                r0 = tt * 128 if sample else t0 + tt * 128
                dma("sp", lambda e, r0=r0, psrc=psrc: e.dma_start(out=ptok[:, :], in_=psrc[r0:r0 + 128, :]), writes=[bptok])
                pt, bpt = nps((0, 1, 2, 3, 4, 5, 6, 7))
                for c in range(2):
                    op("pe", lambda e, c=c, pt=pt: e.transpose(out=pt[:, c * 128:(c + 1) * 128],
                       in_=ptok[:, c * 128:(c + 1) * 128], identity=identf), reads=[bptok] + KC, writes=[bpt], inc=(c == 1))
                op("act", lambda e, pt=pt, tt=tt: e.copy(out=pT[:, :, tt * 128:(tt + 1) * 128],
                                                         in_=pt[:, 0:256].rearrange("p (c t) -> p c t", c=2)),
                   reads=[bpt], writes=[bpT])
            for j in range(8):
                pg, bpg = nps((0, 1, 2, 3, 4, 5, 6, 7))
                for kc in range(8):
                    mm(pg[:, 0:NT], wpg[:, kc, j * 128:(j + 1) * 128], hT[:, kc, 0:NT], kc == 0, kc == 7,
                       [bwpg, bhT], bpg, kc == 7)
                gt, bgt = gt2[j % 2]
                op("act", lambda e, pg=pg, gt=gt: e.activation(out=gt[:, 0:NT], in_=pg[:, 0:NT], func=AF.Sigmoid),
                   reads=[bpg], writes=[bgt])
                pq, bpq = nps((0, 1, 2, 3, 4, 5, 6, 7))
                for c in range(2):
                    mm(pq[:, 0:NT], wpp[:, c, j * 128:(j + 1) * 128], pT[:, c, 0:NT], c == 0, c == 1,
                       [bwpp, bpT], bpq, c == 1)
                op("dve", lambda e, pq=pq, gt=gt: e.tensor_tensor(out=gt[:, 0:NT], in0=gt[:, 0:NT], in1=pq[:, 0:NT],
                                                                  op=ALU.mult), reads=[bgt, bpq], writes=[bgt])
                op("pool", lambda e, gt=gt, j=j: e.tensor_tensor(out=xT[:, j, 0:NT], in0=xT[:, j, 0:NT], in1=gt[:, 0:NT],
                                                                 op=ALU.add), reads=[bgt, bxT], writes=[bxT])
            if l < nlayers - 1:
                for c in range(8):
                    dma("sp", lambda e, c=c: e.dma_start(out=xls[c, :, t0:t0 + NT], in_=xT[:, c, 0:NT]),
                        reads=[bxT], writes=[bxls[gi]])
            else:
                for tt in range(ntile):
                    for c4 in range(2):
                        pt, bpt = nps((0, 1, 2, 3, 4, 5, 6, 7))
                        for c in range(4):
                            cc = c4 * 4 + c
                            op("pe", lambda e, c=c, cc=cc, pt=pt, tt=tt: e.transpose(
                                out=pt[:, c * 128:(c + 1) * 128], in_=xT[:, cc, tt * 128:(tt + 1) * 128], identity=identf),
                               reads=[bxT] + KC, writes=[bpt], inc=(c == 3))
                        op("act", lambda e, pt=pt, c4=c4: e.copy(out=ytok[:, c4 * 512:(c4 + 1) * 512], in_=pt[:, :]),
                           reads=[bpt], writes=[bytok])
                    dst = y_s[tt * 128:(tt + 1) * 128, :] if sample else y_p[t0 + tt * 128:t0 + (tt + 1) * 128, :]
                    dma("sp", lambda e, dst=dst: e.dma_start(out=dst, in_=ytok[:, :]), reads=[bytok])
        kbd.barrier()
        G2.close()
        P2.close()
    kbd.finish()
    glob.close()
    return nc


def sample_attention(l, L):
    kb = L["kbd"]; mixT = L["mixT"]; bmixT = L["bmixT"]
    kb.op("pool", lambda e: e.memset(mixT[:, 4:8, 0:128], 0.0), writes=[bmixT])


def rwkv(l, gi, t0, NT, nseq, T, sample, L):
    kb = L["kbd"]; mixT = L["mixT"]; bmixT = L["bmixT"]
    kb.op("pool", lambda e: e.memset(mixT[:, 0:2, 0:NT], 0.0), writes=[bmixT])


def kernel(**inp):
    f32 = np.float32
    nphys = int(inp["cache_k"].shape[1])
    nc = build_program(nphys=nphys)
    ck = np.ascontiguousarray(inp["cache_k"], dtype=f32).reshape(2 * nphys * 128, 512)
    cv = np.ascontiguousarray(inp["cache_v"], dtype=f32).reshape(2 * nphys * 128, 512)
    wnames = ["g_mix", "w_in", "mu_shift", "w0", "w2", "a0", "a2", "g2", "k_k", "k_a", "r_k", "gn_w", "gn_b", "conv_w",
              "q_gain", "k_gain", "sb_bias", "w_out", "g_mlp", "w_up", "w_down", "g_ple", "w_ple_gate", "w_ple_proj"]
    wd = {n: np.ascontiguousarray(inp[n], dtype=f32) for n in wnames}
    wd["r_k"] = wd["r_k"].reshape(2, 256)
    in_maps = []
    for c in range(NCORES):
        b = c % 4
        s0 = c * NSS
        m = dict(wd)
        m["xp"] = np.ascontiguousarray(inp["x_prompt"][b])
        m["xs"] = np.ascontiguousarray(inp["x_sample"][s0:s0 + NSS]).reshape(128, D)
        if USE_CACHE:
            m["cache_k"] = ck
            m["cache_v"] = cv
        m["s_wkv"] = np.ascontiguousarray(inp["state_wkv"][:, s0:s0 + NSS]).reshape(2, NSS * 4 * 64, 64)
        m["s_shift"] = np.ascontiguousarray(inp["state_shift"][:, s0:s0 + NSS])
        m["s_conv"] = np.ascontiguousarray(inp["state_conv"][:, s0:s0 + NSS]).reshape(2, NSS * 2, 256)
        m["ptab"] = np.ascontiguousarray(inp["page_table"][s0:s0 + NSS]).reshape(1, NSS * NPAGES).astype(np.int32)
        m["pp"] = np.ascontiguousarray(inp["p_prompt"][:, b])
        m["ps"] = np.ascontiguousarray(inp["p_sample"][:, s0:s0 + NSS]).reshape(2, 128, 256)
        m["consts"] = CONSTS
        in_maps.append(m)
    res = run_bass_kernel_spmd(nc, in_maps, core_ids=list(range(NCORES)))
    R = res.results
    y_prompt = np.stack([R[b]["y_p"] for b in range(4)])
    y_sample = np.concatenate([R[c]["y_s"].reshape(NSS, DT, D) for c in range(NCORES)], 0)
    k_prompt = np.stack([R[b]["k_p"] for b in range(4)], 1).reshape(2, 4, SEQ, 8, 64)
    v_prompt = np.stack([R[b]["v_p"] for b in range(4)], 1).reshape(2, 4, SEQ, 8, 64)
    wkv_prompt = np.stack([R[b]["wkv_p"] for b in range(4)], 1).reshape(2, 4, 4, 64, 64)
    shift_prompt = np.stack([R[b]["shift_p"] for b in range(4)], 1)
    conv_prompt = np.stack([R[b]["conv_p"] for b in range(4)], 1)
    k_sample = np.concatenate([R[c]["k_s"].reshape(2, NSS, DT, 8, 64) for c in range(NCORES)], 1)
    v_sample = np.concatenate([R[c]["v_s"].reshape(2, NSS, DT, 8, 64) for c in range(NCORES)], 1)
    wkv_sample = np.concatenate([R[c]["wkv_s"].reshape(2, NSS, 4, 64, 64) for c in range(NCORES)], 1)
    shift_sample = np.concatenate([R[c]["shift_s"] for c in range(NCORES)], 1)
    conv_sample = np.concatenate([R[c]["conv_s"].reshape(2, NSS, 2, 256) for c in range(NCORES)], 1)
    outs = (y_prompt, y_sample, k_prompt, v_prompt, wkv_prompt, shift_prompt, conv_prompt,
            k_sample, v_sample, wkv_sample, shift_sample, conv_sample)
    return tuple(np.ascontiguousarray(o, dtype=f32) for o in outs)
```
